# Optimizing a Trainium2 kernel written in Bass

```python
import math
import jax, jax.numpy as jnp
from jax import lax
import numpy as np

D_MODEL = 2048
BATCH = 4
SEQ = 2048
DEPTH = 1
DEC_BATCH = 128
DEC_SEQ = 4
PAST_LEN = 16384
PAGE_SIZE = 128

A_WIDTH = D_MODEL // 2
N_HEADS_A = 4
DV_A = A_WIDTH // N_HEADS_A
DK_A = DV_A // 2
B_WIDTH = D_MODEL - A_WIDTH
DK_B = 128
N_HEADS_B = B_WIDTH // DK_B
DV_B = B_WIDTH // N_HEADS_B
D_FF = 5504
CONV_W = 3
CHUNK = 64
EPS = 1e-5
ALPHA = (2 * DEPTH) ** 0.25
BETA = (8 * DEPTH) ** -0.25
PROJ_SIZES = (N_HEADS_A * DK_A, N_HEADS_A * DK_A, A_WIDTH, N_HEADS_A, N_HEADS_A, A_WIDTH,
              N_HEADS_B * DK_B, N_HEADS_B * DK_B, B_WIDTH, B_WIDTH)
D_IN = sum(PROJ_SIZES)

kernel_name = 'hymba_mlstm_hgrn2_convffn_deepnorm_adaln_step'


def _split(z, sizes):
    out = []
    start = 0
    for s in sizes:
        out.append(z[..., start:start + s])
        start += s
    return out


def layer_norm(x, g=None, b=None):
    xf = x.astype(jnp.float32)
    mu = jnp.mean(xf, axis=-1, keepdims=True)
    var = jnp.mean(jnp.square(xf - mu), axis=-1, keepdims=True)
    y = (xf - mu) * lax.rsqrt(var + EPS)
    if g is not None:
        y = y * g.astype(jnp.float32) + b.astype(jnp.float32)
    return y.astype(x.dtype)


def _head_rms(h, g):
    hn = h * lax.rsqrt(jnp.mean(jnp.square(h), axis=-1, keepdims=True) + EPS)
    return hn.reshape(h.shape[0], h.shape[1], -1) * g.astype(jnp.float32)


def _to_chunks(t, L):
    B, T, H, d = t.shape
    return t.reshape(B, T // L, L, H, d).transpose(1, 0, 3, 2, 4)


def _from_chunks(t):
    NC, B, H, L, d = t.shape
    return t.transpose(1, 0, 3, 2, 4).reshape(B, NC * L, H, d)


def _mlstm_chunk(carry, xs):
    C, n, m = carry
    q, k, v, logi, logf = xs
    L = q.shape[2]
    b = jnp.cumsum(logf, axis=-1)
    causal = jnp.tril(jnp.ones((L, L), dtype=bool))
    D = jnp.where(causal, b[..., :, None] - b[..., None, :] + logi[..., None, :], -jnp.inf)
    inter = b + m[..., None]
    m_t = jnp.maximum(inter, jnp.max(D, axis=-1))
    w_inter = jnp.exp(inter - m_t)
    S = jnp.einsum('bhtd,bhsd->bhts', q, k) * jnp.exp(D - m_t[..., None])
    num = w_inter[..., None] * jnp.einsum('bhtd,bhde->bhte', q, C) + jnp.einsum('bhts,bhse->bhte', S, v)
    den = w_inter * jnp.einsum('bhtd,bhd->bht', q, n) + jnp.sum(S, axis=-1)
    h = num / jnp.maximum(jnp.abs(den), jnp.exp(-m_t))[..., None]
    dec_end = b[..., -1:] - b + logi
    inter_end = b[..., -1] + m
    m_new = jnp.maximum(inter_end, jnp.max(dec_end, axis=-1))
    wk = jnp.exp(dec_end - m_new[..., None])
    sc = jnp.exp(inter_end - m_new)
    C_new = sc[..., None, None] * C + jnp.einsum('bhs,bhsd,bhse->bhde', wk, k, v)
    n_new = sc[..., None] * n + jnp.einsum('bhs,bhsd->bhd', wk, k)
    return (C_new, n_new, m_new), h


def _hgrn2_chunk(S, xs):
    q, k, logf, v = xs
    L = q.shape[2]
    b = jnp.cumsum(logf, axis=2)
    causal = jnp.tril(jnp.ones((L, L), dtype=bool))
    dec = jnp.where(causal[:, :, None], b[:, :, :, None, :] - b[:, :, None, :, :], -jnp.inf)
    A = jnp.einsum('bhtd,bhtsd,bhsd->bhts', q, jnp.exp(dec), k)
    o = jnp.einsum('bhtd,bhde->bhte', q * jnp.exp(b), S) + jnp.einsum('bhts,bhse->bhte', A, v)
    wk = k * jnp.exp(b[:, :, -1:] - b)
    S_new = jnp.exp(b[:, :, -1])[..., None] * S + jnp.einsum('bhsd,bhse->bhde', wk, v)
    return S_new, o


def _layer(x, c, C0, n0, m0, S0, buf0, lb, w_ada, b_ada, w_in, b_gate_a, norm_a, norm_b,
           w_out, ln1_g, ln1_b, w_up, conv_w, conv_b, w_down, ln2_g, ln2_b):
    B, T, _ = x.shape
    L = math.gcd(T, CHUNK)
    f32 = jnp.float32
    mod = jax.nn.silu(c) @ w_ada + b_ada
    sh1, sc1, g1, sh2, sc2, g2 = jnp.split(mod[:, None, :], 6, axis=-1)
    h = layer_norm(x) * (1 + sc1) + sh1
    z = (h @ w_in).astype(f32)
    qa, ka, va, ia, fa, oa, fb, qb, vb, gb = _split(z, PROJ_SIZES)
    qa = qa.reshape(B, T, N_HEADS_A, DK_A) * DK_A ** -0.5
    ka = ka.reshape(B, T, N_HEADS_A, DK_A)
    va = va.reshape(B, T, N_HEADS_A, DV_A)
    logi = ia + b_gate_a[0].astype(f32)
    logf = jax.nn.log_sigmoid(fa + b_gate_a[1].astype(f32))
    xs_a = (_to_chunks(qa, L), _to_chunks(ka, L), _to_chunks(va, L),
            _to_chunks(logi[..., None], L)[..., 0], _to_chunks(logf[..., None], L)[..., 0])
    (C1, n1, m1), ha = lax.scan(_mlstm_chunk, (C0.astype(f32), n0.astype(f32), m0.astype(f32)), xs_a)
    ya = _head_rms(_from_chunks(ha), norm_a) * jax.nn.sigmoid(oa)
    lbh = lb.astype(f32).reshape(N_HEADS_B, DK_B)
    fb = fb.reshape(B, T, N_HEADS_B, DK_B)
    logfb = jnp.log(lbh + (1 - lbh) * jax.nn.sigmoid(fb))
    kb = (1 - lbh) * jax.nn.sigmoid(-fb)
    qb = qb.reshape(B, T, N_HEADS_B, DK_B)
    vb = vb.reshape(B, T, N_HEADS_B, DV_B)
    xs_b = (_to_chunks(qb, L), _to_chunks(kb, L), _to_chunks(logfb, L), _to_chunks(vb, L))
    S1, hb = lax.scan(_hgrn2_chunk, S0.astype(f32), xs_b)
    yb = _head_rms(_from_chunks(hb), norm_b) * jax.nn.silu(gb)
    mix = jnp.concatenate([ya, yb], axis=-1).astype(x.dtype) @ w_out
    x1 = layer_norm(ALPHA * x + g1 * mix, ln1_g, ln1_b)
    h2 = layer_norm(x1) * (1 + sc2) + sh2
    a, u = jnp.split(h2 @ w_up, 2, axis=-1)
    ext = jnp.concatenate([buf0.astype(a.dtype), a], axis=1)
    conv = conv_b
    for j in range(CONV_W):
        conv = conv + conv_w[j] * ext[:, j:j + T]
    ff = (jax.nn.gelu(conv, approximate=False) * u) @ w_down
    x2 = layer_norm(ALPHA * x1 + g2 * ff, ln2_g, ln2_b)
    dt = x.dtype
    return x2, (C1.astype(dt), n1.astype(dt), m1.astype(dt), S1.astype(dt), ext[:, T:].astype(dt))


def setup_inputs(seed: int = 0) -> dict:
    key = jax.random.key(seed)
    ks = jax.random.split(key, 26)
    nrm = jax.random.normal
    f32 = jnp.float32
    d_inv = D_MODEL ** -0.5
    inp = {}
    inp['x_prompt'] = nrm(ks[0], (BATCH, SEQ, D_MODEL), f32)
    inp['x_sample'] = nrm(ks[1], (DEC_BATCH, DEC_SEQ, D_MODEL), f32)
    inp['state_mlstm_C'] = 0.5 * nrm(ks[2], (DEPTH, DEC_BATCH, N_HEADS_A, DK_A, DV_A), f32)
    inp['state_mlstm_n'] = nrm(ks[3], (DEPTH, DEC_BATCH, N_HEADS_A, DK_A), f32)
    inp['state_mlstm_m'] = 0.5 * nrm(ks[4], (DEPTH, DEC_BATCH, N_HEADS_A), f32)
    inp['state_hgrn_S'] = 0.5 * nrm(ks[5], (DEPTH, DEC_BATCH, N_HEADS_B, DK_B, DV_B), f32)
    inp['cache_ffn_conv'] = 0.7 * nrm(ks[6], (DEPTH, DEC_BATCH, CONV_W - 1, D_FF), f32)
    inp['c_prompt'] = nrm(ks[7], (BATCH, D_MODEL), f32)
    inp['c_sample'] = nrm(ks[8], (DEC_BATCH, D_MODEL), f32)
    inp['hgrn_lb_logits'] = 0.5 * nrm(ks[9], (DEPTH + 1, N_HEADS_B * DK_B), f32)
    inp['w_ada'] = 0.5 * d_inv * nrm(ks[10], (DEPTH, D_MODEL, 6 * D_MODEL), f32)
    inp['b_ada'] = 0.02 * nrm(ks[11], (DEPTH, 6 * D_MODEL), f32)
    inp['w_in'] = d_inv * nrm(ks[12], (DEPTH, D_MODEL, D_IN), f32)
    i_bias = 0.1 * nrm(ks[13], (DEPTH, 1, N_HEADS_A), f32)
    f_bias = 3.0 + 3.0 * jax.random.uniform(ks[14], (DEPTH, 1, N_HEADS_A), f32)
    inp['b_gate_a'] = jnp.concatenate([i_bias, f_bias], axis=1)
    inp['norm_a'] = 1.0 + 0.02 * nrm(ks[15], (DEPTH, A_WIDTH), f32)
    inp['norm_b'] = 1.0 + 0.02 * nrm(ks[16], (DEPTH, B_WIDTH), f32)
    inp['w_out'] = BETA * d_inv * nrm(ks[17], (DEPTH, D_MODEL, D_MODEL), f32)
    inp['ln1_g'] = 1.0 + 0.02 * nrm(ks[18], (DEPTH, D_MODEL), f32)
    inp['ln1_b'] = 0.02 * nrm(ks[19], (DEPTH, D_MODEL), f32)
    inp['w_up'] = BETA * d_inv * nrm(ks[20], (DEPTH, D_MODEL, 2 * D_FF), f32)
    inp['conv_w'] = CONV_W ** -0.5 * nrm(ks[21], (DEPTH, CONV_W, D_FF), f32)
    inp['conv_b'] = 0.02 * nrm(ks[22], (DEPTH, D_FF), f32)
    inp['w_down'] = BETA * D_FF ** -0.5 * nrm(ks[23], (DEPTH, D_FF, D_MODEL), f32)
    inp['ln2_g'] = 1.0 + 0.02 * nrm(ks[24], (DEPTH, D_MODEL), f32)
    inp['ln2_b'] = 0.02 * nrm(ks[25], (DEPTH, D_MODEL), f32)
    return inp


def reference(x_prompt, x_sample, state_mlstm_C, state_mlstm_n, state_mlstm_m, state_hgrn_S,
              cache_ffn_conv, c_prompt, c_sample, hgrn_lb_logits, w_ada, b_ada, w_in, b_gate_a,
              norm_a, norm_b, w_out, ln1_g, ln1_b, w_up, conv_w, conv_b, w_down, ln2_g, ln2_b):
    lb_all = jnp.cumsum(jax.nn.softmax(hgrn_lb_logits.astype(jnp.float32), axis=0), axis=0)
    Bp = x_prompt.shape[0]
    dt = x_prompt.dtype
    yp, ys = x_prompt, x_sample
    sp_list, ss_list = [], []
    for l in range(DEPTH):
        params = (lb_all[l], w_ada[l], b_ada[l], w_in[l], b_gate_a[l], norm_a[l], norm_b[l], w_out[l],
                  ln1_g[l], ln1_b[l], w_up[l], conv_w[l], conv_b[l], w_down[l], ln2_g[l], ln2_b[l])
        yp, sp = _layer(yp, c_prompt,
                        jnp.zeros((Bp, N_HEADS_A, DK_A, DV_A), dt), jnp.zeros((Bp, N_HEADS_A, DK_A), dt),
                        jnp.zeros((Bp, N_HEADS_A), dt), jnp.zeros((Bp, N_HEADS_B, DK_B, DV_B), dt),
                        jnp.zeros((Bp, CONV_W - 1, D_FF), dt), *params)
        ys, ss = _layer(ys, c_sample, state_mlstm_C[l], state_mlstm_n[l], state_mlstm_m[l],
                        state_hgrn_S[l], cache_ffn_conv[l], *params)
        sp_list.append(sp)
        ss_list.append(ss)
    C_p = jnp.stack([s[0] for s in sp_list])
    n_p = jnp.stack([s[1] for s in sp_list])
    m_p = jnp.stack([s[2] for s in sp_list])
    S_p = jnp.stack([s[3] for s in sp_list])
    conv_p = jnp.stack([s[4] for s in sp_list])
    C_s = jnp.stack([s[0] for s in ss_list])
    n_s = jnp.stack([s[1] for s in ss_list])
    m_s = jnp.stack([s[2] for s in ss_list])
    S_s = jnp.stack([s[3] for s in ss_list])
    conv_s = jnp.stack([s[4] for s in ss_list])
    return (yp, ys, C_p, n_p, m_p, S_p, conv_p, C_s, n_s, m_s, S_s, conv_s)
```

```python
import contextlib
import numpy as np
import concourse.bass as bass
import concourse.mybir as mybir
from concourse.bass_utils import run_bass_kernel_spmd

F32 = mybir.dt.float32
BF16 = mybir.dt.bfloat16
AF = mybir.ActivationFunctionType
ALU = mybir.AluOpType

D = 2048
KT = 16
DIN = 7176
DFF = 5504
FT = 43
EPS = 1e-5
ALPHA = 2.0 ** 0.25
O_QA, O_KA, O_VA, O_IA, O_FA, O_OA, O_FB, O_QB, O_VB, O_GB = 0, 512, 1024, 2048, 2052, 2056, 3080, 4104, 5128, 6152
NSEQ = 16
NEG = -1.0e30


class Buf:
    __slots__ = ("name", "t", "wset", "readers", "excl")

    def __init__(self, name, t=None):
        self.name = name
        self.t = t
        self.wset = {}
        self.readers = {}
        self.excl = False

    def __getitem__(self, idx):
        return self.t[idx]


class Part:
    __slots__ = ("b",)

    def __init__(self, b):
        self.b = b


def P(b):
    return Part(b)


class Sched:
    def __init__(self, nc, es):
        self.nc = nc
        self.es = es
        self.eng = {"pe": nc.tensor, "act": nc.scalar, "dve": nc.vector, "pool": nc.gpsimd, "sp": nc.sync}
        self.sem = {}
        self.cnt = {}
        for k in self.eng:
            self.sem[k] = es.enter_context(nc.semaphore("s_" + k))
            self.cnt[k] = 0
        self.waited = {k: {} for k in self.eng}
        self.out_events = []
        self.ninst = 0

    def sb(self, name, shape, dt=F32, es=None):
        self.nalloc = getattr(self, "nalloc", 0) + 1
        name = "%s_u%d" % (name, self.nalloc)
        t = (es or self.es).enter_context(self.nc.sbuf_tensor(name, list(shape), dt))
        try:
            rem = self.nc.sbuf_bytes_remaining
            rem = rem() if callable(rem) else rem
            if rem < getattr(self, "minrem", 1 << 30):
                self.minrem = rem
                self.minrem_at = name
        except Exception:
            pass
        return Buf(name, t)

    def ps(self, name, shape, dt=F32):
        t = self.es.enter_context(self.nc.psum_tensor(name, list(shape), dt))
        b = Buf(name, t)
        b.excl = True
        return b

    def tok(self, name):
        return Buf(name, None)

    def _emit_waits(self, e, deps):
        eng = self.eng[e]
        w = self.waited[e]
        for key, val in deps.items():
            if w.get(key, 0) >= val:
                continue
            eng.wait_ge(self.sem[key], val)
            w[key] = val

    def _deps(self, e, reads, writes, same_engine_ok=False):
        deps = {}

        def need(kv):
            k, v = kv
            if deps.get(k, 0) < v:
                deps[k] = v
        for b in reads:
            for kv in b.wset.items():
                need(kv)
            if b.excl:
                for k, v in b.readers.items():
                    if k != e:
                        need((k, v))
        for w in writes:
            if isinstance(w, Part):
                for kv in w.b.readers.items():
                    need(kv)
            else:
                for kv in w.wset.items():
                    need(kv)
                for kv in w.readers.items():
                    need(kv)
        if same_engine_ok and e in deps:
            del deps[e]
        return deps

    def _record(self, key, v, reads, writes):
        for b in reads:
            if b.readers.get(key, 0) < v:
                b.readers[key] = v
        for w in writes:
            if isinstance(w, Part):
                w.b.wset[key] = v
            else:
                w.wset = {key: v}
                w.readers = {}

    def op(self, e, fn, reads=(), writes=(), same_engine_ok=False):
        deps = self._deps(e, reads, writes, same_engine_ok)
        self._emit_waits(e, deps)
        inst = fn()
        self.cnt[e] += 1
        v = self.cnt[e]
        inst.then_inc(self.sem[e], 1)
        self._record(e, v, reads, writes)
        self.ninst += 1
        return inst

    def dma(self, q, out_ap, in_ap, reads=(), writes=(), dsem=None, is_output=False):
        deps = self._deps(q, reads, writes)
        self._emit_waits(q, deps)
        if dsem not in self.sem:
            self.sem[dsem] = self.es.enter_context(self.nc.semaphore(dsem))
            self.cnt[dsem] = 0
        with self.nc.allow_non_contiguous_dma(reason="small strided layout loads"):
            inst = self.eng[q].dma_start(out=out_ap, in_=in_ap)
        self.cnt[dsem] += 16
        v = self.cnt[dsem]
        inst.then_inc(self.sem[dsem], 16)
        self._record(dsem, v, reads, writes)
        if is_output:
            self.out_events.append((dsem, v))
        self.ninst += 1
        return inst

    def barrier(self):
        for e in self.eng:
            deps = {k: v for k, v in self.cnt.items() if v > 0}
            self._emit_waits(e, deps)

    def finish(self):
        deps = {}
        for k, v in self.out_events:
            if deps.get(k, 0) < v:
                deps[k] = v
        self._emit_waits("sp", deps)


def bc(ap, shape):
    return ap.broadcast_to(list(shape))


def build_program(dbg=False):
    nc = bass.Bass("TRN2", target_bir_lowering=False)

    def din(name, shape):
        return nc.dram_tensor(name, list(shape), F32, kind="ExternalInput").ap()

    def dout(name, shape):
        return nc.dram_tensor(name, list(shape), F32, kind="ExternalOutput").ap()

    xpre = din("xpre", [1024, D])
    xp = din("xp", [1024, D])
    xs = din("xs", [64, D])
    flag_d = din("flag", [128, 1])
    c17_d = din("c17", [17, D])
    sC_d = din("sC", [NSEQ, 4, 128, 256])
    sn_d = din("sn", [NSEQ * 4, 128])
    sm_d = din("sm", [NSEQ, 4])
    sS_d = din("sS", [NSEQ, 8, 128, 128])
    cc_d = din("cconv", [NSEQ * 2, DFF])
    lbl_d = din("lbl", [2, 1024])
    w_ada = din("w_ada", [D, 6 * D])
    b_ada = din("b_ada", [1, 6 * D])
    w_in = din("w_in", [D, DIN])
    bga_d = din("b_gate_a", [2, 4])
    norm_a_d = din("norm_a", [1, 1024])
    norm_b_d = din("norm_b", [1, 1024])
    w_out = din("w_out", [D, D])
    ln1g_d = din("ln1_g", [1, D])
    ln1b_d = din("ln1_b", [1, D])
    w_up = din("w_up", [D, 2 * DFF])
    cw_d = din("conv_w", [3, DFF])
    cb_d = din("conv_b", [1, DFF])
    w_down = din("w_down", [DFF, D])
    ln2g_d = din("ln2_g", [1, D])
    ln2b_d = din("ln2_b", [1, D])
    c_ident = din("c_ident", [128, 128])
    c_maskp = din("c_maskp", [128, 128])
    c_masks = din("c_masks", [64, 64])
    c_bmask = din("c_bmask", [64, 16])
    c_cm16 = din("c_cm16", [1, 16 * 64])
    c_cm2 = din("c_cm2", [1, 2 * 128])
    c_rmA = din("c_rmA", [2, 640])
    c_rmB = din("c_rmB", [2, 576])
    c_diag = din("c_diag", [4, 96])

    yp_o = dout("yp", [1024, D])
    ys_o = dout("ys", [64, D])
    Cp_o = dout("Cp", [4, 128, 256])
    np_o = dout("np", [4, 128])
    mp_o = dout("mp", [1, 4])
    Sp_o = dout("Sp", [8, 128, 128])
    cvp_o = dout("convp", [2, DFF])
    Cs_o = dout("Cs", [NSEQ, 4, 128, 256])
    ns_o = dout("ns", [NSEQ * 4, 128])
    ms_o = dout("ms", [NSEQ, 4])
    Ss_o = dout("Ss", [NSEQ, 8, 128, 128])
    cvs_o = dout("convs", [NSEQ * 2, DFF])

    gscr = nc.dram_tensor("gscr", [2, 17, D], F32, kind="Internal").ap()
    gscr_s = nc.dram_tensor("gscr_s", [2, 64, D], F32, kind="Internal").ap()
    x1scr = nc.dram_tensor("x1scr", [1024 + 128 + 64, D], F32, kind="Internal").ap()

    w_in_v = w_in.rearrange("(kt p) n -> p kt n", p=128)
    w_ada_v = w_ada.rearrange("(kt p) n -> p kt n", p=128)
    w_out_v = w_out.rearrange("(kt p) n -> p kt n", p=128)
    w_up_v = w_up.rearrange("(kt p) n -> p kt n", p=128)
    w_down_v = w_down.rearrange("(kt p) n -> p kt n", p=128)

    def rowbc(ap_row, n, parts=128):
        return bass.AP(ap_row.tensor, ap_row.offset, [[0, parts], [1, n]])

    with contextlib.ExitStack() as es:
        S = Sched(nc, es)
        V = nc.vector
        G = nc.gpsimd

        def ACT(out, in_, func, reads, writes, bias=None, scale=None, accum=None):
            kw = {}
            if bias is not None:
                kw["bias"] = bias
            if scale is not None:
                kw["scale"] = scale
            if accum is not None:
                kw["accum_out"] = accum
            S.op("act", lambda: nc.scalar.activation(out=out, in_=in_, func=func, **kw), reads, writes)

        def TS(eng, out, in0, s1, s2, op0, op1, reads, writes):
            e = V if eng == "dve" else G
            if op1 is None:
                S.op(eng, lambda: e.tensor_scalar(out=out, in0=in0, scalar1=s1, scalar2=None, op0=op0), reads, writes)
            else:
                S.op(eng, lambda: e.tensor_scalar(out=out, in0=in0, scalar1=s1, scalar2=s2, op0=op0, op1=op1), reads, writes)

        def TT(eng, out, in0, in1, op, reads, writes):
            e = V if eng == "dve" else G
            S.op(eng, lambda: e.tensor_tensor(out=out, in0=in0, in1=in1, op=op), reads, writes)

        def STT(out, in0, scalar, in1, op0, op1, reads, writes):
            S.op("dve", lambda: V.scalar_tensor_tensor(out=out, in0=in0, scalar=scalar, in1=in1, op0=op0, op1=op1), reads, writes)

        def CP(eng, out, in_, reads, writes):
            if eng == "act":
                S.op("act", lambda: nc.scalar.copy(out=out, in_=in_), reads, writes)
            else:
                e = V if eng == "dve" else G
                S.op(eng, lambda: e.tensor_copy(out=out, in_=in_), reads, writes)

        def MSET(eng, ap, val, writes):
            e = V if eng == "dve" else G
            S.op(eng, lambda: e.memset(ap, val), (), writes)

        def RECIP(out, in_, reads, writes):
            S.op("dve", lambda: V.reciprocal(out=out, in_=in_), reads, writes)

        def SCAN(out, d0, d1, init, op0, op1, reads, writes):
            S.op("dve", lambda: V.tensor_tensor_scan(out=out, data0=d0, data1=d1, initial=init, op0=op0, op1=op1), reads, writes)

        def MM(bank, out, lhsT, rhs, start, stop, reads):
            S.op("pe", lambda: nc.tensor.matmul(out, lhsT=lhsT, rhs=rhs, start=start, stop=stop), reads, [bank], same_engine_ok=True)

        def TR(bank, out, in_, ident, reads):
            S.op("pe", lambda: nc.tensor.transpose(out=out, in_=in_, identity=ident), reads, [bank], same_engine_ok=True)

        banks = [S.ps("bank%d" % i, [128, 512], F32) for i in range(8)]
        bstate = {"i": 0, "excl": set()}

        def nb():
            while True:
                b = banks[bstate["i"] % 8]
                bstate["i"] += 1
                if (bstate["i"] - 1) % 8 not in bstate["excl"]:
                    return b

        def bfv(bank):
            return bank[:, :].bitcast(BF16)

        identf = S.sb("identf", [128, 128], F32)
        identb = S.sb("identb", [128, 128], BF16)
        maskp = S.sb("maskp", [128, 128], F32)
        masks = S.sb("masks", [64, 64], F32)
        bmask = S.sb("bmask", [64, 16], F32)
        cm16 = S.sb("cm16", [128, 16, 64], BF16)
        cm2 = S.sb("cm2", [128, 2, 128], BF16)
        rmt = S.sb("rmt", [128, 2, 640], F32)
        diag4 = S.sb("diag4", [4, 4, 24], F32)
        ones4 = S.sb("ones4", [4, 128], F32)
        flag = S.sb("flagt", [128, 1], F32)
        cst2 = S.sb("cst2", [128, 2], F32)
        modT = S.sb("modT", [128, 4, KT, 17], F32)
        lbv = S.sb("lbv", [128, 3, 8], F32)
        convw = S.sb("convw", [128, FT, 4], F32)
        bga = S.sb("bga", [4, 4], F32)
        Cst = [S.sb("C%d" % h, [128, 257], F32) for h in range(4)]
        Cbf = [S.sb("Cbf%d" % h, [128, 257], BF16) for h in range(4)]
        Sst = [S.sb("S%d" % h, [128, 128], F32) for h in range(8)]
        Sbf = [S.sb("Sbf%d" % h, [128, 128], BF16) for h in range(8)]
        mcar = S.sb("mcar", [4, 1], F32)
        hist = S.sb("hist", [128, FT, 2], F32)
        nTs = S.sb("nTs", [128, 64], F32)
        nTo = S.sb("nTo", [128, 64], F32)
        mprev_s = S.sb("mprev_s", [4, 16], F32)
        wslots = [S.sb("wslot%d" % i, [128, KT, 512], BF16) for i in range(2)]
        wstate = {"i": 0}
        wg = S.sb("wg", [128, KT, 8], BF16)
        stat = [S.sb("stat%d" % i, [128, 4, 6], F32) for i in range(2)]
        mvv = [S.sb("mv%d" % i, [128, 8], F32) for i in range(2)]
        smt = [S.sb("smt%d" % i, [128, 16], F32) for i in range(4)]
        smstate = {"i": 0}
        lnstate = {"i": 0}
        statA = S.sb("statA", [128, 5, 4, 6], F32)
        mvA = S.sb("mvA", [128, 5, 2], F32)
        rsA = S.sb("rsA", [128, 3, 5], F32)

        def nsm():
            b = smt[smstate["i"] % 4]
            smstate["i"] += 1
            return b

        def wload(pieces):
            idx = wstate["i"] % len(wslots)
            slot = wslots[idx]
            wstate["i"] += 1
            for (view, k0, nkt, c0, ncols, dcol) in pieces:
                S.dma("pool", slot[:, 0:nkt, dcol:dcol + ncols], view[:, k0:k0 + nkt, c0:c0 + ncols], writes=[slot], dsem="d_w%d" % idx)
            return slot

        pending = {"key": None, "slot": None}

        def pkey(pieces):
            return tuple((id(p[0]),) + tuple(p[1:]) for p in pieces)

        def stream(jobs, nxt=None):
            def pcs(job):
                if len(job) == 2:
                    return job[0]
                return [tuple(job[:5]) + (0,)]
            if not jobs:
                return
            depth = len(wslots) - 1
            q = []
            p0 = pcs(jobs[0])
            if pending["key"] is not None and pending["key"] == pkey(p0):
                q.append(pending["slot"])
            else:
                q.append(wload(p0))
            pending["key"] = None
            nl = 1
            n = len(jobs)
            for i, job in enumerate(jobs):
                while nl < n and nl <= i + depth:
                    q.append(wload(pcs(jobs[nl])))
                    nl += 1
                if i == n - 1 and nxt is not None:
                    pending["slot"] = wload(nxt)
                    pending["key"] = pkey(nxt)
                job[-1](q.pop(0))

        def W1(view, k0, nkt, c0, ncols):
            return [(view, k0, nkt, c0, ncols, 0)]

        S.dma("sp", identf[:, :], c_ident[:, :], writes=[identf], dsem="d_c0")
        S.dma("sp", maskp[:, :], c_maskp[:, :], writes=[maskp], dsem="d_c1")
        S.dma("sp", masks[:, :], c_masks[:, :], writes=[masks], dsem="d_c2")
        S.dma("sp", bmask[:, :], c_bmask[:, :], writes=[bmask], dsem="d_c3")
        S.dma("sp", flag[:, :], flag_d[:, :], writes=[flag], dsem="d_c4")
        S.dma("sp", diag4[:, :, :], c_diag.rearrange("k (h c) -> k h c", c=24), writes=[diag4], dsem="d_c5")
        CP("dve", identb[:, :], identf[:, :], [identf], [identb])
        MSET("dve", ones4[:, :], 1.0, [ones4])
        MSET("dve", cst2[:, 0:1], 1.0, [cst2])
        MSET("dve", cst2[:, 1:2], 0.0, [cst2])
        bmaskb = bmask
        MSET("dve", mcar[:, :], 0.0, [mcar])
        MSET("dve", statA[:, :, :, :], 0.0, [statA])
        MSET("dve", mvA[:, :, :], 1.0, [mvA])
        MSET("dve", hist[:, :, :], 0.0, [hist])
        for h in range(4):
            MSET("dve", Cst[h][:, :], 0.0, [Cst[h]])
        for h in range(8):
            MSET("dve", Sst[h][:, :], 0.0, [Sst[h]])
            MSET("dve", Sbf[h][:, :], 0.0, [Sbf[h]])

        pre_es = contextlib.ExitStack()
        for i_ in range(2):
            wslots.append(S.sb("wslotx%d" % i_, [128, KT, 512], BF16, es=pre_es))
        scT = S.sb("scT", [128, KT, 17], BF16, es=pre_es)
        bAt = [S.sb("bAt%d" % i, [17, 512], F32, es=pre_es) for i in range(2)]
        modc = [S.sb("modc%d" % i, [17, 512], F32, es=pre_es) for i in range(2)]
        with contextlib.ExitStack() as p0:
            tmpc = S.sb("tmpc", [128, 16 * 64], F32, es=p0)
            S.dma("sp", tmpc[:, 0:1024], rowbc(c_cm16[0:1, :], 1024), writes=[tmpc], dsem="d_c8")
            CP("dve", cm16[:, :, :], tmpc[:, 0:1024].rearrange("p (a b) -> p a b", b=64), [tmpc], [cm16])
            tmpc2 = S.sb("tmpc2", [128, 256], F32, es=p0)
            S.dma("sp", tmpc2[:, :], rowbc(c_cm2[0:1, :], 256), writes=[tmpc2], dsem="d_c9")
            CP("dve", cm2[:, :, :], tmpc2[:, :].rearrange("p (a b) -> p a b", b=128), [tmpc2], [cm2])
            S.dma("sp", bga[:, 0:2], bass.AP(bga_d.tensor, bga_d.offset, [[1, 4], [4, 2]]), writes=[bga], dsem="d_c10")
            TS("dve", bga[:, 2:3], bga[:, 1:2], -1.0, None, ALU.mult, None, [bga], [bga])
            lbt = S.sb("lbt", [128, 2, 8], F32, es=p0)
            S.dma("sp", lbt[:, :, :], lbl_d.rearrange("l (h d) -> d l h", d=128), writes=[lbt], dsem="d_c11")
            TT("dve", lbv[:, 1, :], lbt[:, 0, :], lbt[:, 1, :], ALU.subtract, [lbt], [lbv])
            ACT(lbv[:, 0, :], lbv[:, 1, :], AF.Sigmoid, [lbv], [lbv])
            TS("dve", lbv[:, 2, :], lbv[:, 0, :], -1.0, None, ALU.add, None, [lbv], [lbv])
            TS("dve", lbv[:, 1, :], lbv[:, 2, :], -1.0, None, ALU.mult, None, [lbv], [lbv])
            cw4 = S.sb("cw4", [4, DFF], F32, es=p0)
            S.dma("sp", cw4[0:3, :], cw_d[:, :], writes=[cw4], dsem="d_c12")
            S.dma("sp", cw4[3:4, :], cb_d[:, :], writes=[cw4], dsem="d_c12")
            for g0 in range(0, FT, 8):
                n = min(8, FT - g0)
                bk = nb()
                for q in range(n):
                    TR(bk, bk[:, q * 4:q * 4 + 4], cw4[0:4, (g0 + q) * 128:(g0 + q + 1) * 128], identf[0:4, 0:4], [cw4, identf])
                CP("dve", convw[:, g0:g0 + n, :], bk[:, 0:4 * n].rearrange("p (a b) -> p a b", b=4), [bk], [convw])
            sn64 = S.sb("sn64", [64, 128], F32, es=p0)
            S.dma("sp", sn64[:, :], sn_d[:, :], writes=[sn64], dsem="d_c14")
            bk = nb()
            TR(bk, bk[:, 0:64], sn64[0:64, :], identf[0:64, 0:64], [sn64, identf])
            CP("dve", nTs[:, :], bk[:, 0:64], [bk], [nTs])
            S.dma("sp", mprev_s[:, :], bass.AP(sm_d.tensor, sm_d.offset, [[1, 4], [4, 16]]), writes=[mprev_s], dsem="d_c15")

            c17 = S.sb("c17", [17, D], F32, es=p0)
            S.dma("sp", c17[:, :], c17_d[:, :], writes=[c17], dsem="d_c16")
            ACT(c17[:, :], c17[:, :], AF.Silu, [c17], [c17])
            for g0 in range(0, KT, 4):
                bk = nb()
                for q in range(4):
                    TR(bk, bk[:, q * 32:q * 32 + 17], c17[0:17, (g0 + q) * 128:(g0 + q + 1) * 128], identf[0:17, 0:17], [c17, identf])
                CP("dve", scT[:, g0:g0 + 4, :], bk[:, 0:128].rearrange("p (a b) -> p a b", b=32)[:, :, 0:17], [bk], [scT])
            gscr_tok = S.tok("gscr")

            def ada_job(ci):
                def fn(slot):
                    bk = nb()
                    for kt in range(KT):
                        MM(bk, bk[0:17, 0:512], scT[:, kt, :], slot[:, kt, :], kt == 0, kt == KT - 1, [scT, slot])
                    ba = bAt[ci % 2]
                    mc = modc[ci % 2]
                    S.dma("sp", ba[:, :], rowbc(b_ada[0:1, 512 * ci:512 * ci + 512], 512, 17), writes=[ba], dsem="d_ba%d" % (ci % 2))
                    TT("dve", mc[:, :], bk[0:17, 0:512], ba[:, :], ALU.add, [bk, ba], [mc])
                    kind, sub = ci // 4, ci % 4
                    if kind in (2, 5):
                        gi_ = 0 if kind == 2 else 1
                        S.dma("sp", gscr[gi_, :, 512 * sub:512 * sub + 512], mc[:, :], reads=[mc], writes=[P(gscr_tok)], dsem="d_gs%d" % (ci % 2))
                        for j4 in range(4):
                            dst = bass.AP(gscr_s.tensor, gscr_s[gi_, j4:j4 + 1, 512 * sub:512 * sub + 1].offset, [[4 * D, 16], [1, 512]])
                            S.dma("sp", dst, mc[1:17, :], reads=[mc], writes=[P(gscr_tok)], dsem="d_gs%d" % (ci % 2))
                    else:
                        kk = {0: 0, 1: 1, 3: 2, 4: 3}[kind]
                        b2 = nb()
                        for q in range(4):
                            TR(b2, b2[:, q * 32:q * 32 + 17], mc[0:17, q * 128:(q + 1) * 128], identf[0:17, 0:17], [mc, identf])
                        src = b2[:, 0:128].rearrange("p (a b) -> p a b", b=32)[:, :, 0:17]
                        if kind in (1, 4):
                            TS("dve", modT[:, kk, 4 * sub:4 * sub + 4, :], src, 1.0, None, ALU.add, None, [b2], [P(modT)])
                        else:
                            CP("dve", modT[:, kk, 4 * sub:4 * sub + 4, :], src, [b2], [P(modT)])
                return fn
            stream([(w_ada_v, 0, KT, 512 * ci, 512, ada_job(ci)) for ci in range(8)], nxt=W1(w_in_v, 0, KT, O_KA, 512))
            ada_late = [(w_ada_v, 0, KT, 512 * ci, 512, ada_job(ci)) for ci in range(8, 24)]
            S.barrier()

        def ln_stats(xt, np_, xbuf):
            k = lnstate["i"] % 2
            lnstate["i"] += 1
            st, mv = stat[k], mvv[k]
            for c in range(4):
                S.op("dve", lambda: V.bn_stats(out=st[0:np_, c, :], in_=xt[0:np_, c * 512:(c + 1) * 512]), [xbuf], [st])
            S.op("dve", lambda: V.bn_aggr(out=mv[0:np_, 0:2], in_=st[0:np_, :, :]), [st], [mv])
            ACT(mv[0:np_, 2:3], mv[0:np_, 1:2], AF.Sqrt, [mv], [mv], bias=EPS, scale=1.0)
            RECIP(mv[0:np_, 3:4], mv[0:np_, 2:3], [mv], [mv])
            TS("dve", mv[0:np_, 4:5], mv[0:np_, 0:1], mv[0:np_, 3:4], -1.0, ALU.mult, ALU.mult, [mv], [mv])
            return mv[0:np_, 3:4], mv[0:np_, 4:5], mv

        def ln_all(xa, xab, tl, lng, lnb, after):
            n = len(tl)
            for ti, t in enumerate(tl):
                np_ = t["np"]
                for c in range(4):
                    S.op("dve", lambda: V.bn_stats(out=statA[0:np_, ti, c, :], in_=xa[0:np_, ti, c * 512:(c + 1) * 512]), [xab[ti]], [P(statA)])
                S.op("dve", lambda: V.bn_aggr(out=mvA[0:np_, ti, :], in_=statA[0:np_, ti, :, :]), [statA], [P(mvA)])
            ACT(rsA[:, 0, 0:n], mvA[:, 0:n, 1], AF.Sqrt, [mvA], [rsA], bias=EPS, scale=1.0)
            RECIP(rsA[:, 1, 0:n], rsA[:, 0, 0:n], [rsA], [rsA])
            STT(rsA[:, 2, 0:n], mvA[:, 0:n, 0], -1.0, rsA[:, 1, 0:n], ALU.mult, ALU.mult, [mvA, rsA], [rsA])
            for ti, t in enumerate(tl):
                np_ = t["np"]
                ACT(xa[0:np_, ti, :], xa[0:np_, ti, :], AF.Identity, [xab[ti], rsA], [xab[ti]], bias=rsA[0:np_, 2, ti:ti + 1], scale=rsA[0:np_, 1, ti:ti + 1])
            for ti, t in enumerate(tl):
                np_ = t["np"]
                TT("dve", xa[0:np_, ti, :], xa[0:np_, ti, :], lng[0:np_, :], ALU.mult, [xab[ti], lng], [xab[ti]])
            for ti, t in enumerate(tl):
                np_ = t["np"]
                TT("dve", xa[0:np_, ti, :], xa[0:np_, ti, :], lnb[0:np_, :], ALU.add, [xab[ti], lnb], [xab[ti]])
                after(ti, t)

        def to_featT(xn, xnbuf, np_, dstT, dstbuf, col, kk_sh, kk_sc, sample):
            for g0 in range(0, KT, 4):
                bk = nb()
                bv = bfv(bk)
                for q in range(4):
                    ft = g0 + q
                    TR(bk, bv[:, q * 128:q * 128 + np_], xn[0:np_, ft * 128:(ft + 1) * 128], identb[0:np_, 0:np_], [xnbuf, identb])
                for q in range(4):
                    ft = g0 + q
                    src = bv[:, q * 128:q * 128 + np_]
                    dst = dstT[:, ft, col:col + np_]
                    if not sample:
                        if (g0 // 4) % 2 == 0:
                            ACT(dst, src, AF.Identity, [bk, modT], [P(dstbuf)], bias=modT[:, kk_sh, ft, 0:1], scale=modT[:, kk_sc, ft, 0:1])
                        else:
                            TS("dve", dst, src, modT[:, kk_sc, ft, 0:1], modT[:, kk_sh, ft, 0:1], ALU.mult, ALU.add, [bk, modT], [P(dstbuf)])
                    else:
                        scv = bc(modT[:, kk_sc, ft, 1:17].unsqueeze(2), [128, 16, 4])
                        shv = bc(modT[:, kk_sh, ft, 1:17].unsqueeze(2), [128, 16, 4])
                        TT("dve", dst.rearrange("p (a b) -> p a b", b=4), src.rearrange("p (a b) -> p a b", b=4), scv, ALU.mult, [bk, modT], [P(dstbuf)])
                        TT("dve", dst.rearrange("p (a b) -> p a b", b=4), dst.rearrange("p (a b) -> p a b", b=4), shv, ALU.add, [dstbuf, modT], [P(dstbuf)])

        def plainT(src, srcbuf, np_, nft, dstT, dstbuf, ft0, col):
            for g0 in range(0, nft, 4):
                n = min(4, nft - g0)
                bk = nb()
                bv = bfv(bk)
                for q in range(n):
                    TR(bk, bv[:, q * 128:q * 128 + np_], src[0:np_, (g0 + q) * 128:(g0 + q + 1) * 128], identb[0:np_, 0:np_], [srcbuf, identb])
                if np_ == 128:
                    CP("act", dstT[:, ft0 + g0:ft0 + g0 + n, col:col + 128], bv[:, 0:128 * n].rearrange("p (a b) -> p a b", b=128), [bk], [P(dstbuf)])
                else:
                    CP("act", dstT[:, ft0 + g0:ft0 + g0 + n, col:col + np_], bv[:, 0:128 * n].rearrange("p (a b) -> p a b", b=128)[:, :, 0:np_], [bk], [P(dstbuf)])

        def mk_tiles(kind):
            if kind == "P0":
                tl = [dict(k="pre", np=128, col=128 * j, src=xpre[128 * j:128 * j + 128, :], j=j) for j in range(4)]
                return tl, tl, 512, [(0, 512)]
            if kind == "P1":
                tl = [dict(k="pre", np=128, col=128 * j, src=xpre[512 + 128 * j:512 + 128 * j + 128, :], j=j) for j in range(3)]
                return tl, tl, 384, [(0, 384)]
            if kind == "M0":
                tl = [dict(k="main", np=128, col=128 * j, src=xp[128 * j:128 * j + 128, :], j=j, orow=128 * j, srow=128 * j) for j in range(4)]
                ex = dict(k="extra", np=128, col=512, src=xpre[896:1024, :], j=4, srow=1024)
                return tl + [ex], [ex] + tl, 640, [(0, 512), (512, 128)]
            if kind == "M1":
                tl = [dict(k="main", np=128, col=128 * j, src=xp[512 + 128 * j:512 + 128 * j + 128, :], j=j, orow=512 + 128 * j, srow=512 + 128 * j) for j in range(4)]
                sa = dict(k="sample", np=64, col=512, src=xs[:, :], j=4, srow=1152)
                return tl + [sa], tl + [sa], 576, [(0, 512), (512, 64)]

        def interleave(jobs, extras):
            out = []
            for j in jobs:
                out.append(j)
                if extras:
                    out.append(extras.pop(0))
            out.extend(extras)
            return out

        def mixer_phase(kind, yT, yTb, bes, extra=()):
            tiles, logical, TB, ranges = mk_tiles(kind)
            full = kind in ("M0", "M1")
            has_sample = kind == "M1"
            NT = len(tiles)
            if not full:
                segs = [(0, TB, TB, 0, 1)]
                NCm = 1
            elif kind == "M0":
                segs = [(0, 640, 128, 0, 5)]
                NCm = 5
            else:
                segs = [(0, 512, 128, 0, 4), (512, 576, 4, 4, 16)]
                NCm = 20
            rm = rmt
            if full:
                crm, wrm = (c_rmB, 576) if kind == "M1" else (c_rmA, 640)
                S.dma("sp", rmt[:, 0, 0:wrm], rowbc(crm[0:1, :], wrm), writes=[rmt], dsem="d_rm")
                S.dma("sp", rmt[:, 1, 0:wrm], rowbc(crm[1:2, :], wrm), writes=[rmt], dsem="d_rm")

            def d0(parts, k):
                if full:
                    return rm[0:parts, k, 0:TB]
                return bc(cst2[0:parts, k:k + 1], [parts, TB])

            def cidx(t):
                return 0 if not full else t["j"]

            with contextlib.ExitStack() as ph:
                hT = S.sb("hT", [128, KT, TB], BF16, es=ph)
                hTb = [S.tok("hT%d" % j) for j in range(NT)]
                with contextlib.ExitStack() as p1:
                    xt2 = [S.sb("xt%d" % i, [128, D], F32, es=p1) for i in range(2)]
                    xn2 = [S.sb("xn%d" % i, [128, D], BF16, es=p1) for i in range(2)]
                    def st_a(ti):
                        t = tiles[ti]
                        xt, xn = xt2[ti % 2], xn2[ti % 2]
                        np_ = t["np"]
                        S.dma("sp", xt[0:np_, :], t["src"], writes=[xt], dsem="d_xt%d" % (ti % 2))
                        rstd, nmr, mvb = ln_stats(xt, np_, xt)
                        ACT(xn[0:np_, :], xt[0:np_, :], AF.Identity, [xt, mvb], [xn], bias=nmr, scale=rstd)

                    def st_b(ti):
                        t = tiles[ti]
                        xn = xn2[ti % 2]
                        to_featT(xn, xn, t["np"], hT, hTb[ti], t["col"], 0, 1, t["k"] == "sample")
                    st_a(0)
                    for ti in range(NT):
                        if ti + 1 < NT:
                            st_a(ti + 1)
                        st_b(ti)
                    S.barrier()

                def hreads(c0, n):
                    return [hTb[ti] for ti, t in enumerate(tiles) if t["col"] < c0 + n and t["col"] + t["np"] > c0]

                def fm_group(slot, cs, M, fn):
                    for (c0, n) in ranges:
                        bk = nb()
                        rd = hreads(c0, n) + [slot]
                        for kt in range(KT):
                            MM(bk, bk[0:M, 0:n], slot[:, kt, cs], hT[:, kt, c0:c0 + n], kt == 0, kt == KT - 1, rd)
                        fn(bk, c0, n)

                def tm_group(slot, ncols, fn):
                    for ti, t in enumerate(tiles):
                        bk = nb()
                        np_ = t["np"]
                        for kt in range(KT):
                            MM(bk, bk[0:np_, 0:ncols], hT[:, kt, t["col"]:t["col"] + np_], slot[:, kt, 0:ncols], kt == 0, kt == KT - 1, [hTb[ti], slot])
                        fn(bk, ti, t)

                def cview(ap2d, lo, hi, L):
                    return ap2d[:, lo:hi].rearrange("p (c l) -> p c l", l=L)

                with contextlib.ExitStack() as ga:
                    qaT = S.sb("qaT", [128, 4, TB], BF16, es=ga) if full else None
                    kaT = S.sb("kaT", [128, 4, TB], BF16, es=ga)
                    ka_tok = S.sb("ka_tok", [128, NT, 512], BF16, es=ga)
                    va = S.sb("va", [128, NT, 4, 257], BF16, es=ga)
                    oga = S.sb("oga", [128, NT, 1024], BF16, es=ga) if full else None
                    na_bc = S.sb("na_bc", [128, 1024], F32, es=ga) if full else None
                    T1 = S.sb("gT1", [4, TB], F32, es=ga)
                    T2 = S.sb("gT2", [4, TB], F32, es=ga)
                    T3 = S.sb("gT3", [4, TB], F32, es=ga)
                    T4 = S.sb("gT4", [4, TB], F32, es=ga)
                    pc = S.sb("pc", [4, 8, 24], F32, es=ga)
                    e1d = S.sb("e1d", [4, 4, 24], F32, es=ga)
                    e1bc = S.sb("e1bc", [128, 4, 24], F32, es=ga)
                    tsc = S.sb("tsc", [128, NT, 12], F32, es=ga)
                    kw = [S.sb("kw%d" % i, [128, 128], BF16, es=ga) for i in range(8)]
                    STsb = [S.sb("STsb%d" % i, [128, 128], BF16, es=ga) for i in range(4)] if full else None
                    sgt = [S.sb("sgt%d" % i, [128, 512], F32, es=ga) for i in range(2)] if full else None
                    ytl = [S.sb("ytl%d" % i, [128, 1024], BF16, es=ga) for i in range(2)] if full else None
                    junk = S.sb("junk", [128, 256], F32, es=ga) if full else None
                    nsb = [[S.sb("nsb%d_%d" % (i, h), [128, 257], F32, es=ga) for h in range(4)] for i in range(2)] if full else None
                    if has_sample:
                        GSA = 4
                        CsS = [S.sb("CsS%d" % i, [128, GSA, 257], F32, es=ga) for i in range(3)]
                        CsB = [S.sb("CsB%d" % i, [128, GSA, 257], BF16, es=ga) for i in range(3)]
                        kwm = [S.sb("kwm%d" % i, [64, 16, 128], BF16, es=ga) for i in range(2)]
                        qps = [S.sb("qps%d" % i, [128, 16, 64], BF16, es=ga) for i in range(2)]
                        kwS = [S.sb("kwS%d" % h, [64, 128], BF16, es=ga) for h in range(4)]
                        STS = [S.sb("STS%d" % h, [64, 64], BF16, es=ga) for h in range(4)]
                    MSET("dve", va[:, :, :, 256:257], 1.0, [va])
                    MSET("dve", e1d[:, :, :], 0.0, [e1d])
                    MSET("dve", pc[:, :, :], 0.0, [pc])
                    if full:
                        S.dma("sp", na_bc[:, :], rowbc(norm_a_d[0:1, :], 1024), writes=[na_bc], dsem="d_nabc")
                    S.dma("pool", wg[:, :, :], w_in_v[:, :, O_IA:O_IA + 8], writes=[wg], dsem="d_wg")

                    jobs = []

                    def job_qa(slot):
                        for h in range(4):
                            fm_group(slot, slice(h * 128, (h + 1) * 128), 128,
                                     lambda bk, c0, n, h=h: ACT(qaT[:, h, c0:c0 + n], bk[:, 0:n], AF.Copy, [bk], [P(qaT)], scale=128.0 ** -0.5))

                    def job_ka(slot):
                        for h in range(4):
                            fm_group(slot, slice(h * 128, (h + 1) * 128), 128,
                                     lambda bk, c0, n, h=h: CP("dve", kaT[:, h, c0:c0 + n], bk[:, 0:n], [bk], [P(kaT)]))
                        for ti, t in enumerate(tiles):
                            np_ = t["np"]
                            bk = nb()
                            bv = bfv(bk)
                            for h in range(4):
                                TR(bk, bv[0:np_, h * 128:(h + 1) * 128], kaT[:, h, t["col"]:t["col"] + np_], identb[:, :], [kaT, identb])
                            CP("act", ka_tok[0:np_, ti, :], bv[0:np_, 0:512], [bk], [P(ka_tok)])

                    def job_va(c):
                        def fn(slot):
                            def ev(bk, ti, t):
                                np_ = t["np"]
                                CP("act" if ti % 2 else "dve", va[0:np_, ti, 2 * c:2 * c + 2, 0:256], bk[0:np_, 0:512].rearrange("p (a b) -> p a b", b=256), [bk], [P(va)])
                            tm_group(slot, 512, ev)
                        return fn

                    def job_oa(c):
                        def fn(slot):
                            def ev(bk, ti, t):
                                np_ = t["np"]
                                sg = sgt[ti % 2]
                                ACT(sg[0:np_, :], bk[0:np_, 0:512], AF.Sigmoid, [bk], [sg])
                                TT("dve", oga[0:np_, ti, 512 * c:512 * c + 512], sg[0:np_, :], na_bc[0:np_, 512 * c:512 * c + 512], ALU.mult, [sg, na_bc], [P(oga)])
                            tm_group(slot, 512, ev)
                        return fn
                    if full:
                        jobs.append((w_in_v, 0, KT, O_QA, 512, job_qa))
                    jobs.append((w_in_v, 0, KT, O_KA, 512, job_ka))
                    jobs.append((w_in_v, 0, KT, O_VA, 512, job_va(0)))
                    jobs.append((w_in_v, 0, KT, O_VA + 512, 512, job_va(1)))
                    if full:
                        jobs.append((w_in_v, 0, KT, O_OA, 512, job_oa(0)))
                        jobs.append((w_in_v, 0, KT, O_OA + 512, 512, job_oa(1)))
                    stream(interleave(jobs, list(extra[0:4])), nxt=W1(w_in_v, 0, KT, O_FB, 512))

                    fm_group(wg, slice(0, 4), 4, lambda bk, c0, n: ACT(T1[:, c0:c0 + n], bk[0:4, 0:n], AF.Identity, [bk, bga], [P(T1)], bias=bga[:, 0:1], scale=1.0))
                    fm_group(wg, slice(4, 8), 4, lambda bk, c0, n: ACT(T2[:, c0:c0 + n], bk[0:4, 0:n], AF.Exp, [bk, bga], [P(T2)], bias=bga[:, 2:3], scale=-1.0))
                    ACT(T2[:, :], T2[:, :], AF.Ln, [T2], [T2], bias=1.0, scale=1.0)
                    SCAN(T3[:, :], d0(4, 0), T2[:, :], 0.0, ALU.mult, ALU.add, [rm, cst2, T2], [T3])
                    TT("dve", T1[:, :], T1[:, :], T3[:, :], ALU.add, [T1, T3], [T1])
                    SCAN(T2[:, :], d0(4, 1), T1[:, :], NEG, ALU.add, ALU.max, [rm, cst2, T1], [T2])
                    for (lo, hi, L, c0_, n_) in segs:
                        CP("dve", pc[:, 0, c0_:c0_ + n_], cview(T2, lo, hi, L)[:, :, L - 1], [T2], [pc])
                        CP("dve", pc[:, 1, c0_:c0_ + n_], cview(T3, lo, hi, L)[:, :, L - 1], [T3], [pc])

                    def mscan(a, b_):
                        SCAN(pc[:, 3, a:b_], pc[:, 0, a:b_], pc[:, 1, a:b_], mcar[:, 0:1], ALU.max, ALU.subtract, [pc, mcar], [pc])
                        CP("dve", pc[:, 2, a:a + 1], mcar[:, 0:1], [mcar], [pc])
                        if b_ - a > 1:
                            CP("dve", pc[:, 2, a + 1:b_], pc[:, 3, a:b_ - 1], [pc], [pc])
                        CP("dve", mcar[:, 0:1], pc[:, 3, b_ - 1:b_], [pc], [mcar])
                    if kind == "M0":
                        mscan(4, 5)
                        TS("dve", mcar[:, 0:1], mcar[:, 0:1], flag[0:4, 0:1], None, ALU.mult, None, [mcar, flag], [mcar])
                        mscan(0, 4)
                    elif kind == "M1":
                        mscan(0, 4)
                    else:
                        mscan(0, 1)
                    if has_sample:
                        CP("dve", pc[:, 2, 4:20], mprev_s[:, :], [mprev_s], [pc])
                        TT("dve", pc[:, 3, 4:20], pc[:, 0, 4:20], pc[:, 2, 4:20], ALU.max, [pc], [pc])
                        TT("dve", pc[:, 3, 4:20], pc[:, 3, 4:20], pc[:, 1, 4:20], ALU.subtract, [pc], [pc])
                        S.dma("sp", bass.AP(ms_o.tensor, ms_o.offset, [[1, 4], [4, 16]]), pc[:, 3, 4:20], reads=[pc], dsem="d_ms", is_output=True)
                    TT("dve", pc[:, 4, 0:NCm], pc[:, 0, 0:NCm], pc[:, 2, 0:NCm], ALU.max, [pc], [pc])
                    TT("dve", pc[:, 5, 0:NCm], pc[:, 2, 0:NCm], pc[:, 4, 0:NCm], ALU.subtract, [pc], [pc])
                    ACT(pc[:, 5, 0:NCm], pc[:, 5, 0:NCm], AF.Exp, [pc], [pc])

                    def pcb(k, a, n_, L):
                        return bc(pc[:, k, a:a + n_].unsqueeze(2), [4, n_, L])
                    for (lo, hi, L, a, n_) in segs:
                        TT("dve", cview(T4, lo, hi, L), cview(T2, lo, hi, L), pcb(2, a, n_, L), ALU.max, [T2, pc], [T4])
                        TT("dve", cview(T1, lo, hi, L), cview(T1, lo, hi, L), pcb(4, a, n_, L), ALU.subtract, [T1, pc], [T1])
                        TT("dve", cview(T3, lo, hi, L), cview(T3, lo, hi, L), cview(T4, lo, hi, L), ALU.subtract, [T3, T4], [T3])
                        TT("dve", cview(T4, lo, hi, L), pcb(4, a, n_, L), cview(T4, lo, hi, L), ALU.subtract, [T4, pc], [T4])
                    ACT(T1[:, :], T1[:, :], AF.Exp, [T1], [T1])
                    if full:
                        ACT(T3[:, :], T3[:, :], AF.Exp, [T3], [T3])
                        ACT(T4[:, :], T4[:, :], AF.Exp, [T4], [T4])
                    for ti, t in enumerate(tiles):
                        np_ = t["np"]
                        bk = nb()
                        lst = (T1, T4, T3) if full else (T1,)
                        for qi, Tt in enumerate(lst):
                            TR(bk, bk[0:np_, 4 * qi:4 * qi + 4], Tt[0:4, t["col"]:t["col"] + np_], identf[0:4, 0:4], [Tt, identf])
                        CP("dve", tsc[0:np_, ti, 0:4 * len(lst)], bk[0:np_, 0:4 * len(lst)], [bk], [P(tsc)])
                    TT("dve", e1d[:, :, 0:NCm], bc(pc[:, 5, 0:NCm].unsqueeze(1), [4, 4, NCm]), diag4[:, :, 0:NCm], ALU.mult, [pc, diag4], [e1d])
                    bk = nb()
                    MM(bk, bk[:, 0:96], ones4[:, :], e1d[:, :, :].rearrange("p a b -> p (a b)"), True, True, [ones4, e1d])
                    CP("dve", e1bc[:, :, :], bk[:, 0:96].rearrange("p (a b) -> p a b", b=24), [bk], [e1bc])

                    if not full:
                        accb = [banks[i] for i in range(4)]
                        bstate["excl"] = {0, 1, 2, 3}
                        for ti, t in enumerate(tiles):
                            for h in range(4):
                                kwb = kw[(ti * 4 + h) % 8]
                                ACT(kwb[:, :], ka_tok[:, ti, h * 128:(h + 1) * 128], AF.Identity, [ka_tok, tsc], [kwb], scale=tsc[:, ti, h:h + 1])
                                MM(accb[h], accb[h][:, 0:257], kwb[:, :], va[:, ti, h, :], ti == 0, ti == NT - 1, [kwb, va])
                        for h in range(4):
                            STT(Cst[h][:, :], Cst[h][:, :], e1bc[:, h, 0:1], accb[h][:, 0:257], ALU.mult, ALU.add, [Cst[h], e1bc, accb[h]], [Cst[h]])
                        bstate["excl"] = set()
                    else:
                        for t in logical:
                            ti = t["j"]
                            np_ = t["np"]
                            col = t["col"]
                            yt = ytl[ti % 2]
                            if t["k"] != "sample":
                                ch = ti
                                bS, bN, bU = [None] * 4, [None] * 4, [None] * 4
                                for h in range(4):
                                    TS("dve", Cst[h][:, :], Cst[h][:, :], e1bc[:, h, ch:ch + 1], None, ALU.mult, None, [Cst[h], e1bc], [Cst[h]])
                                    CP("act", Cbf[h][:, :], Cst[h][:, :], [Cst[h]], [Cbf[h]])
                                for h in range(4):
                                    kwb = kw[(ti * 4 + h) % 8]
                                    ACT(kwb[:, :], ka_tok[:, ti, h * 128:(h + 1) * 128], AF.Identity, [ka_tok, tsc], [kwb], scale=tsc[:, ti, h:h + 1])
                                    bS[h] = nb()
                                    MM(bS[h], bS[h][:, 0:128], kaT[:, h, col:col + 128], qaT[:, h, col:col + 128], True, True, [kaT, qaT])
                                for h in range(4):
                                    STT(STsb[h][:, :], bS[h][:, 0:128], tsc[:, ti, h:h + 1], maskp[:, :], ALU.mult, ALU.mult, [bS[h], tsc, maskp], [STsb[h]])
                                for h in range(4):
                                    kwb = kw[(ti * 4 + h) % 8]
                                    bN[h] = nb()
                                    MM(bN[h], bN[h][:, 0:257], STsb[h][:, :], va[:, ti, h, :], True, False, [STsb[h], va])
                                    MM(bN[h], bN[h][:, 0:257], qaT[:, h, col:col + 128], Cbf[h][:, :], False, True, [qaT, Cbf[h]])
                                    bU[h] = nb()
                                    MM(bU[h], bU[h][:, 0:257], kwb[:, :], va[:, ti, h, :], True, True, [kwb, va])
                                for h in range(4):
                                    TT("dve", Cst[h][:, :], Cst[h][:, :], bU[h][:, 0:257], ALU.add, [Cst[h], bU[h]], [Cst[h]])
                                    if t["k"] == "extra":
                                        TS("dve", Cst[h][:, :], Cst[h][:, :], flag[:, 0:1], None, ALU.mult, None, [Cst[h], flag], [Cst[h]])
                                for h in range(4):
                                    CP("act", nsb[ti % 2][h][:, :], bN[h][:, 0:257], [bN[h]], [nsb[ti % 2][h]])
                                a_out4(nsb[ti % 2], np_, ti, tsc, oga, yt, junk)
                                plainT(yt, yt, np_, 8, yT, yTb[ti], 0, col)
                            else:
                                accb = [banks[i] for i in range(4)]
                                bstate["excl"] = {0, 1, 2, 3}
                                for h in range(4):
                                    ACT(kwS[h][:, :], ka_tok[0:64, ti, h * 128:(h + 1) * 128], AF.Identity, [ka_tok, tsc], [kwS[h]], scale=tsc[0:64, ti, h:h + 1])
                                    bS = nb()
                                    MM(bS, bS[0:64, 0:64], kaT[:, h, col:col + 64], qaT[:, h, col:col + 64], True, True, [kaT, qaT])
                                    STT(STS[h][:, :], bS[0:64, 0:64], tsc[0:64, ti, h:h + 1], masks[:, :], ALU.mult, ALU.mult, [bS, tsc, masks], [STS[h]])
                                    MM(accb[h], accb[h][0:64, 0:257], STS[h][:, :], va[0:64, ti, h, :], True, False, [STS[h], va])
                                nTs3 = nTs[:, :].rearrange("p (s h) -> p s h", h=4)
                                nTo3 = nTo[:, :].rearrange("p (s h) -> p s h", h=4)
                                itemsA = [(h, g) for h in range(4) for g in range(NSEQ // GSA)]

                                def loadA(k):
                                    h, g = itemsA[k]
                                    cs = CsS[k % 3]
                                    s0 = g * GSA
                                    S.dma("sp", cs[:, :, 0:256], sC_d[s0:s0 + GSA, h].rearrange("s d e -> d s e"), writes=[cs], dsem="d_cs%d" % (k % 3))

                                def prepA(k):
                                    h, g = itemsA[k]
                                    cs, cb_ = CsS[k % 3], CsB[k % 3]
                                    s0 = g * GSA
                                    kwm_h, qps_h = kwm[h % 2], qps[h % 2]
                                    if g == 0:
                                        TT("dve", kwm_h[:, :, :], bc(kwS[h][:, :].unsqueeze(1), [64, 16, 128]), bc(bmaskb[:, :].unsqueeze(2), [64, 16, 128]), ALU.mult, [kwS[h], bmaskb], [kwm_h])
                                        TT("dve", qps_h[:, :, :], bc(qaT[:, h, col:col + 64].unsqueeze(1), [128, 16, 64]), cm16[:, :, :], ALU.mult, [qaT, cm16], [qps_h])
                                    CP("act", cs[:, :, 256], nTs3[:, s0:s0 + GSA, h], [nTs], [cs])
                                    TT("dve", cs[:, :, :], cs[:, :, :], bc(e1bc[:, h, 4 + s0:4 + s0 + GSA].unsqueeze(2), [128, GSA, 257]), ALU.mult, [cs, e1bc], [cs])
                                    CP("act", cb_[:, :, :], cs[:, :, :], [cs], [cb_])

                                def compA(k):
                                    h, g = itemsA[k]
                                    cs, cb_ = CsS[k % 3], CsB[k % 3]
                                    s0 = g * GSA
                                    kwm_h, qps_h = kwm[h % 2], qps[h % 2]
                                    for i in range(GSA):
                                        sq = s0 + i
                                        MM(accb[h], accb[h][0:64, 0:257], qps_h[:, sq, :], cb_[:, i, :], False, sq == NSEQ - 1, [qps_h, cb_])
                                        bU = nb()
                                        MM(bU, bU[:, 0:257], kwm_h[:, sq, :], va[0:64, ti, h, :], True, True, [kwm_h, va])
                                        TT("dve", cs[:, i, :], cs[:, i, :], bU[:, 0:257], ALU.add, [cs, bU], [P(cs)])
                                    CP("act", nTo3[:, s0:s0 + GSA, h], cs[:, :, 256], [cs], [P(nTo)])
                                    S.dma("sp", Cs_o[s0:s0 + GSA, h].rearrange("s d e -> d s e"), cs[:, :, 0:256], reads=[cs], dsem="d_cso%d" % (k % 3), is_output=True)
                                nA = len(itemsA)
                                loadA(0)
                                loadA(1)
                                prepA(0)
                                for k in range(nA):
                                    if k + 2 < nA:
                                        loadA(k + 2)
                                    if k + 1 < nA:
                                        prepA(k + 1)
                                    compA(k)
                                a_out4(accb, np_, ti, tsc, oga, yt, junk)
                                bstate["excl"] = set()
                                plainT(yt, yt, np_, 8, yT, yTb[ti], 0, col)
                    S.barrier()

                with contextlib.ExitStack() as gb:
                    gbc = {}

                    def GS(name, shape, dt=F32):
                        if name not in gbc:
                            gbc[name] = S.sb(name, shape, dt, es=gb)
                        return gbc[name]
                    for hg in range(2):
                        SG4 = GS("SG4", [128, 4, TB], F32)
                        F4 = GS("F4", [128, 4, TB], F32)
                        B4 = GS("B4", [128, 4, TB], F32)
                        kinvT = GS("kinvT", [128, 4, TB], BF16)
                        qeT = GS("qeT", [128, 4, TB], BF16) if full else None
                        kinv_tok = GS("kinv_tok", [128, NT, 512], BF16)
                        vb = GS("vb", [128, NT, 512], BF16)
                        ogb = GS("ogb", [128, NT, 512], BF16) if full else None
                        nb_bc = GS("nb_bc", [128, 512], F32) if full else None
                        pcB = GS("pcB", [128, 4, 5, 24], F32)
                        ATsb = [GS("ATsb%d" % i, [128, 128], BF16) for i in range(4)] if full else None
                        sgt = [GS("sgtb%d" % i, [128, 512], F32) for i in range(2)] if full else None
                        ytl = [GS("ytlb%d" % i, [128, 512], BF16) for i in range(2)] if full else None
                        junk = GS("junkb", [128, 128], F32) if full else None
                        osb = [[GS("osb%d_%d" % (i, h), [128, 128], F32) for h in range(4)] for i in range(2)] if full else None
                        if has_sample:
                            GSB = 8
                            SsS = [GS("SsS%d" % i, [128, GSB, 128], F32) for i in range(3)]
                            SsB = [GS("SsB%d" % i, [128, GSB, 128], BF16) for i in range(3)]
                            kim = [GS("kim%d" % i, [64, 16, 128], BF16) for i in range(2)]
                            qps = [GS("qpsb%d" % i, [128, 16, 64], BF16) for i in range(2)]
                            ATS = [GS("ATS%d" % h, [64, 64], BF16) for h in range(4)]
                        MSET("dve", pcB[:, :, :, :], 0.0, [pcB])
                        if full:
                            for i_ in range(4):
                                MSET("dve", ATsb[i_][:, :], 0.0, [ATsb[i_]])
                        if full:
                            S.dma("sp", nb_bc[:, :], rowbc(norm_b_d[0:1, 512 * hg:512 * hg + 512], 512), writes=[nb_bc], dsem="d_nbbc")

                        def job_fb(slot):
                            H0 = 4 * hg
                            for hh in range(4):
                                fm_group(slot, slice(hh * 128, (hh + 1) * 128), 128,
                                         lambda bk, c0, n, hh=hh: ACT(SG4[:, hh, c0:c0 + n], bk[:, 0:n], AF.Sigmoid, [bk], [P(SG4)]))
                            STT(SG4[:, :, :], SG4[:, :, :], -1.0, bc(lbv[:, 2, H0:H0 + 4].unsqueeze(2), [128, 4, TB]), ALU.add, ALU.mult, [SG4, lbv], [SG4])
                            ACT(F4[:, :, :], SG4[:, :, :], AF.Ln, [SG4], [F4], bias=1.0, scale=-1.0)
                            for hh in range(4):
                                SCAN(B4[:, hh, :], d0(128, 0), F4[:, hh, :], 0.0, ALU.mult, ALU.add, [rm, cst2, F4], [B4])
                            for (lo, hi, L, c0_, n_) in segs:
                                rpos = 63 if (full and L == 128) else L - 1
                                v4 = B4[:, :, lo:hi].rearrange("p h (c l) -> p h c l", l=L)
                                CP("dve", pcB[:, :, 0, c0_:c0_ + n_], v4[:, :, :, L - 1], [B4], [pcB])
                                CP("dve", pcB[:, :, 1, c0_:c0_ + n_], v4[:, :, :, rpos], [B4], [pcB])
                            ACT(pcB[:, :, 2, 0:NCm], pcB[:, :, 0, 0:NCm], AF.Exp, [pcB], [pcB])
                            TT("dve", pcB[:, :, 3, 0:NCm], pcB[:, :, 0, 0:NCm], pcB[:, :, 1, 0:NCm], ALU.subtract, [pcB], [pcB])
                            ACT(pcB[:, :, 3, 0:NCm], pcB[:, :, 3, 0:NCm], AF.Exp, [pcB], [pcB])
                            ACT(pcB[:, :, 4, 0:NCm], pcB[:, :, 1, 0:NCm], AF.Exp, [pcB], [pcB])
                            for hh in range(4):
                                for (lo, hi, L, c0_, n_) in segs:
                                    TT("dve", cview(B4[:, hh, :], lo, hi, L), cview(B4[:, hh, :], lo, hi, L), bc(pcB[:, hh, 1, c0_:c0_ + n_].unsqueeze(2), [128, n_, L]), ALU.subtract, [B4, pcB], [B4])
                            ACT(F4[:, :, :], B4[:, :, :], AF.Exp, [B4], [F4], scale=-1.0)
                            TT("dve", kinvT[:, :, :], SG4[:, :, :], F4[:, :, :], ALU.mult, [SG4, F4], [kinvT])
                            if full:
                                ACT(B4[:, :, :], B4[:, :, :], AF.Exp, [B4], [B4])

                        def kinv_transposes():
                            for ti, t in enumerate(tiles):
                                np_ = t["np"]
                                bk = nb()
                                bv = bfv(bk)
                                for hh in range(4):
                                    TR(bk, bv[0:np_, hh * 128:(hh + 1) * 128], kinvT[:, hh, t["col"]:t["col"] + np_], identb[:, :], [kinvT, identb])
                                CP("act", kinv_tok[0:np_, ti, :], bv[0:np_, 0:512], [bk], [P(kinv_tok)])

                        def job_qb(slot):
                            for hh in range(4):
                                fm_group(slot, slice(hh * 128, (hh + 1) * 128), 128,
                                         lambda bk, c0, n, hh=hh: TT("dve", qeT[:, hh, c0:c0 + n], bk[:, 0:n], B4[:, hh, c0:c0 + n], ALU.mult, [bk, B4], [P(qeT)]))

                        def job_vb(slot):
                            def ev(bk, ti, t):
                                np_ = t["np"]
                                CP("act" if ti % 2 else "dve", vb[0:np_, ti, :], bk[0:np_, 0:512], [bk], [P(vb)])
                            tm_group(slot, 512, ev)

                        def job_gb(slot):
                            def ev(bk, ti, t):
                                np_ = t["np"]
                                sg = sgt[ti % 2]
                                ACT(sg[0:np_, :], bk[0:np_, 0:512], AF.Silu, [bk], [sg])
                                TT("dve", ogb[0:np_, ti, :], sg[0:np_, :], nb_bc[0:np_, :], ALU.mult, [sg, nb_bc], [P(ogb)])
                            tm_group(slot, 512, ev)
                        jobs = [(w_in_v, 0, KT, O_FB + 512 * hg, 512, job_fb)]
                        jobs.append((w_in_v, 0, KT, O_VB + 512 * hg, 512, job_vb))
                        if full:
                            jobs.append((w_in_v, 0, KT, O_GB + 512 * hg, 512, job_gb))
                            jobs.append((w_in_v, 0, KT, O_QB + 512 * hg, 512, job_qb))
                        if hg == 0:
                            nx_ = W1(w_in_v, 0, KT, O_FB + 512, 512)
                        elif kind == "P0":
                            nx_ = W1(w_in_v, 0, KT, O_KA, 512)
                        elif kind == "P1":
                            nx_ = None
                        else:
                            nx_ = W1(w_out_v, 0, KT, 0, 512)
                        stream(interleave(jobs, list(extra[4 + 2 * hg:6 + 2 * hg])), nxt=nx_)
                        kinv_transposes()

                        if not full:
                            accb = [banks[i] for i in range(4)]
                            bstate["excl"] = {0, 1, 2, 3}
                            for ti, t in enumerate(tiles):
                                for hh in range(4):
                                    MM(accb[hh], accb[hh][:, 0:128], kinv_tok[:, ti, hh * 128:(hh + 1) * 128], vb[:, ti, hh * 128:(hh + 1) * 128], ti == 0, ti == NT - 1, [kinv_tok, vb])
                            for hh in range(4):
                                H = 4 * hg + hh
                                STT(Sst[H][:, :], Sst[H][:, :], pcB[:, hh, 2, 0:1], accb[hh][:, 0:128], ALU.mult, ALU.add, [Sst[H], pcB, accb[hh]], [Sst[H]])
                            bstate["excl"] = set()
                        else:
                            for t in logical:
                                ti = t["j"]
                                np_ = t["np"]
                                col = t["col"]
                                yt = ytl[ti % 2]
                                if t["k"] != "sample":
                                    ch = ti
                                    bA, bO, bU = [None] * 4, [None] * 4, [None] * 4
                                    for hh in range(4):
                                        H = 4 * hg + hh
                                        ACT(Sbf[H][:, :], Sst[H][:, :], AF.Identity, [Sst[H], pcB], [Sbf[H]], scale=pcB[:, hh, 4, ch:ch + 1])
                                    for hh in range(4):
                                        bA[hh] = nb()
                                        MM(bA[hh], bA[hh][0:64, 0:64], kinvT[:, hh, col:col + 64], qeT[:, hh, col:col + 64], True, True, [kinvT, qeT])
                                        MM(bA[hh], bA[hh][:, 64:128], kinvT[:, hh, col:col + 128], qeT[:, hh, col + 64:col + 128], True, True, [kinvT, qeT])
                                    for hh in range(4):
                                        S.op("dve", lambda: V.copy_predicated(ATsb[hh][0:64, 0:64], maskp[0:64, 0:64].bitcast(mybir.dt.uint32), bA[hh][0:64, 0:64]), [bA[hh], maskp], [P(ATsb[hh])])
                                        S.op("dve", lambda: V.copy_predicated(ATsb[hh][:, 64:128], maskp[:, 64:128].bitcast(mybir.dt.uint32), bA[hh][:, 64:128]), [bA[hh], maskp], [P(ATsb[hh])])
                                    for hh in range(4):
                                        H = 4 * hg + hh
                                        cs_ = slice(hh * 128, (hh + 1) * 128)
                                        bO[hh] = nb()
                                        MM(bO[hh], bO[hh][:, 0:128], ATsb[hh][:, :], vb[:, ti, cs_], True, False, [ATsb[hh], vb])
                                        MM(bO[hh], bO[hh][:, 0:128], qeT[:, hh, col:col + 128], Sbf[H][:, :], False, True, [qeT, Sbf[H]])
                                        bU[hh] = nb()
                                        MM(bU[hh], bU[hh][:, 0:128], kinv_tok[:, ti, cs_], vb[:, ti, cs_], True, True, [kinv_tok, vb])
                                    for hh in range(4):
                                        H = 4 * hg + hh
                                        TS("dve", Sst[H][:, :], Sst[H][:, :], pcB[:, hh, 2, ch:ch + 1], None, ALU.mult, None, [Sst[H], pcB], [Sst[H]])
                                        STT(Sst[H][:, :], bU[hh][:, 0:128], pcB[:, hh, 3, ch:ch + 1], Sst[H][:, :], ALU.mult, ALU.add, [bU[hh], pcB, Sst[H]], [Sst[H]])
                                        if t["k"] == "extra":
                                            TS("dve", Sst[H][:, :], Sst[H][:, :], flag[:, 0:1], None, ALU.mult, None, [Sst[H], flag], [Sst[H]])
                                    for hh in range(4):
                                        CP("act", osb[ti % 2][hh][:, :], bO[hh][:, 0:128], [bO[hh]], [osb[ti % 2][hh]])
                                    b_out4(osb[ti % 2], np_, ti, ogb, yt, junk)
                                    plainT(yt, yt, np_, 4, yT, yTb[ti], 8 + 4 * hg, col)
                                else:
                                    accb = [banks[i] for i in range(4)]
                                    bstate["excl"] = {0, 1, 2, 3}
                                    for hh in range(4):
                                        bA = nb()
                                        MM(bA, bA[0:64, 0:64], kinvT[:, hh, col:col + 64], qeT[:, hh, col:col + 64], True, True, [kinvT, qeT])
                                        TT("dve", ATS[hh][:, :], bA[0:64, 0:64], masks[:, :], ALU.mult, [bA, masks], [ATS[hh]])
                                        MM(accb[hh], accb[hh][0:64, 0:128], ATS[hh][:, :], vb[0:64, ti, hh * 128:(hh + 1) * 128], True, False, [ATS[hh], vb])
                                    itemsB = [(hh, g) for hh in range(4) for g in range(NSEQ // GSB)]

                                    def loadB(k):
                                        hh, g = itemsB[k]
                                        s0 = g * GSB
                                        ss = SsS[k % 3]
                                        S.dma("sp", ss[:, :, :], sS_d[s0:s0 + GSB, 4 * hg + hh].rearrange("s d e -> d s e"), writes=[ss], dsem="d_ss%d" % (k % 3))

                                    def prepB(k):
                                        hh, g = itemsB[k]
                                        s0 = g * GSB
                                        cs_ = slice(hh * 128, (hh + 1) * 128)
                                        kim_h, qps_h = kim[hh % 2], qps[hh % 2]
                                        ss, sb_ = SsS[k % 3], SsB[k % 3]
                                        if g == 0:
                                            TT("dve", kim_h[:, :, :], bc(kinv_tok[0:64, ti, cs_].unsqueeze(1), [64, 16, 128]), bc(bmaskb[:, :].unsqueeze(2), [64, 16, 128]), ALU.mult, [kinv_tok, bmaskb], [kim_h])
                                            TT("dve", qps_h[:, :, :], bc(qeT[:, hh, col:col + 64].unsqueeze(1), [128, 16, 64]), cm16[:, :, :], ALU.mult, [qeT, cm16], [qps_h])
                                        TT("dve", ss[:, :, :], ss[:, :, :], bc(pcB[:, hh, 2, 4 + s0:4 + s0 + GSB].unsqueeze(2), [128, GSB, 128]), ALU.mult, [ss, pcB], [ss])
                                        CP("act", sb_[:, :, :], ss[:, :, :], [ss], [sb_])

                                    def compB(k):
                                        hh, g = itemsB[k]
                                        H = 4 * hg + hh
                                        s0 = g * GSB
                                        cs_ = slice(hh * 128, (hh + 1) * 128)
                                        kim_h, qps_h = kim[hh % 2], qps[hh % 2]
                                        ss, sb_ = SsS[k % 3], SsB[k % 3]
                                        for i in range(GSB):
                                            sq = s0 + i
                                            MM(accb[hh], accb[hh][0:64, 0:128], qps_h[:, sq, :], sb_[:, i, :], False, sq == NSEQ - 1, [qps_h, sb_])
                                            bU = nb()
                                            MM(bU, bU[:, 0:128], kim_h[:, sq, :], vb[0:64, ti, cs_], True, True, [kim_h, vb])
                                            TT("dve", ss[:, i, :], ss[:, i, :], bU[:, 0:128], ALU.add, [ss, bU], [P(ss)])
                                        S.dma("sp", Ss_o[s0:s0 + GSB, H].rearrange("s d e -> d s e"), ss[:, :, :], reads=[ss], dsem="d_sso%d" % (k % 3), is_output=True)
                                    nB = len(itemsB)
                                    loadB(0)
                                    loadB(1)
                                    prepB(0)
                                    for k in range(nB):
                                        if k + 2 < nB:
                                            loadB(k + 2)
                                        if k + 1 < nB:
                                            prepB(k + 1)
                                        compB(k)
                                    b_out4(accb, np_, ti, ogb, yt, junk)
                                    bstate["excl"] = set()
                                    plainT(yt, yt, np_, 4, yT, yTb[ti], 8 + 4 * hg, col)
                        if hg == 1:
                            S.barrier()
            return tiles, logical, TB, ranges

        def a_out4(bN, np_, ti, tsc, oga, yt, junk):
            sm = nsm()
            r4 = tsc[0:np_, ti, 4:8]
            emt4 = tsc[0:np_, ti, 8:12]
            for h in range(4):
                TT("dve", sm[0:np_, h:h + 1], bN[h][0:np_, 256:257], tsc[0:np_, ti, 4 + h:5 + h], ALU.mult, [bN[h], tsc], [sm])
            STT(sm[0:np_, 4:8], sm[0:np_, 0:4], -1.0, sm[0:np_, 0:4], ALU.mult, ALU.max, [sm], [sm])
            TT("dve", sm[0:np_, 4:8], sm[0:np_, 4:8], emt4, ALU.max, [sm, tsc], [sm])
            RECIP(sm[0:np_, 0:4], sm[0:np_, 4:8], [sm], [sm])
            TT("dve", sm[0:np_, 4:8], sm[0:np_, 0:4], r4, ALU.mult, [sm, tsc], [sm])
            for h in range(4):
                ACT(junk[0:np_, 0:256], bN[h][0:np_, 0:256], AF.Square, [bN[h], sm], [junk, sm], scale=sm[0:np_, 4 + h:5 + h], accum=sm[0:np_, 8 + h:9 + h])
            ACT(sm[0:np_, 12:16], sm[0:np_, 8:12], AF.Sqrt, [sm], [sm], bias=EPS, scale=1.0 / 256.0)
            RECIP(sm[0:np_, 0:4], sm[0:np_, 12:16], [sm], [sm])
            TT("dve", sm[0:np_, 8:12], sm[0:np_, 0:4], sm[0:np_, 4:8], ALU.mult, [sm], [sm])
            for h in range(4):
                STT(yt[0:np_, h * 256:(h + 1) * 256], bN[h][0:np_, 0:256], sm[0:np_, 8 + h:9 + h], oga[0:np_, ti, h * 256:(h + 1) * 256], ALU.mult, ALU.mult, [bN[h], sm, oga], [P(yt)])

        def b_out4(bO, np_, ti, ogb, yt, junk):
            sm = nsm()
            for hh in range(4):
                ACT(junk[0:np_, 0:128], bO[hh][0:np_, 0:128], AF.Square, [bO[hh]], [junk, sm], accum=sm[0:np_, hh:hh + 1])
            ACT(sm[0:np_, 4:8], sm[0:np_, 0:4], AF.Sqrt, [sm], [sm], bias=EPS, scale=1.0 / 128.0)
            RECIP(sm[0:np_, 8:12], sm[0:np_, 4:8], [sm], [sm])
            for hh in range(4):
                STT(yt[0:np_, hh * 128:(hh + 1) * 128], bO[hh][0:np_, 0:128], sm[0:np_, 8 + hh:9 + hh], ogb[0:np_, ti, hh * 128:(hh + 1) * 128], ALU.mult, ALU.mult, [bO[hh], sm, ogb], [P(yt)])

        x1tok = S.tok("x1scr")

        def outproj_phase(kind, tiles, yT, yTb):
            with contextlib.ExitStack() as ph:
                NT = len(tiles)
                xa = S.sb("xa", [128, NT, D], F32, es=ph)
                xab = [S.tok("xa%d" % i) for i in range(NT)]
                lng = S.sb("lng", [128, D], F32, es=ph)
                lnb = S.sb("lnb", [128, D], F32, es=ph)
                gP = [S.sb("gP%d" % i, [128, 512], F32, es=ph) for i in range(2)]
                gS = [S.sb("gS%d" % i, [64, 512], F32, es=ph) for i in range(2)]
                tmp = [S.sb("tmpo%d" % i, [128, 512], F32, es=ph) for i in range(2)]
                S.dma("sp", lng[:, :], rowbc(ln1g_d[0:1, :], D), writes=[lng], dsem="d_lng")
                S.dma("sp", lnb[:, :], rowbc(ln1b_d[0:1, :], D), writes=[lnb], dsem="d_lnb")
                for ti, t in enumerate(tiles):
                    S.dma("sp", xa[0:t["np"], ti, :], t["src"], writes=[xab[ti]], dsem="d_xa%d" % ti)
                has_sample = any(t["k"] == "sample" for t in tiles)

                def job(c):
                    def fn(slot):
                        g = gP[c % 2]
                        S.dma("sp", g[:, :], rowbc(gscr[0, 0:1, 512 * c:512 * c + 512], 512), reads=[gscr_tok], writes=[g], dsem="d_gP%d" % (c % 2))
                        if has_sample:
                            g2 = gS[c % 2]
                            S.dma("sp", g2[:, :], gscr_s[0, :, 512 * c:512 * c + 512], reads=[gscr_tok], writes=[g2], dsem="d_gS%d" % (c % 2))
                        for ti, t in enumerate(tiles):
                            np_ = t["np"]
                            bk = nb()
                            for kt in range(KT):
                                MM(bk, bk[0:np_, 0:512], yT[:, kt, t["col"]:t["col"] + np_], slot[:, kt, :], kt == 0, kt == KT - 1, [yTb[ti], slot])
                            gg = g2 if t["k"] == "sample" else g
                            tp = tmp[ti % 2]
                            TT("dve", tp[0:np_, :], bk[0:np_, 0:512], gg[0:np_, :], ALU.mult, [bk, gg], [tp])
                            STT(xa[0:np_, ti, 512 * c:512 * c + 512], xa[0:np_, ti, 512 * c:512 * c + 512], ALPHA, tp[0:np_, :], ALU.mult, ALU.add, [xab[ti], tp], [xab[ti]])
                    return fn
                stream([(w_out_v, 0, KT, 512 * c, 512, job(c)) for c in range(4)], nxt=[(w_up_v, 0, KT, 0, 256, 0), (w_up_v, 0, KT, DFF, 256, 256)])
                def after1(ti, t):
                    np_ = t["np"]
                    S.dma("sp", x1scr[t["srow"]:t["srow"] + np_, :], xa[0:np_, ti, :], reads=[xab[ti]], writes=[P(x1tok)], dsem="d_x1o%d" % ti)
                ln_all(xa, xab, tiles, lng, lnb, after1)
                S.barrier()

        def ffn_phase(kind, tiles):
            main = [t for t in tiles if t["k"] == "main"]
            aux = [t for t in tiles if t["k"] != "main"][0]
            has_sample = aux["k"] == "sample"
            NS = 64 if has_sample else 0
            with contextlib.ExitStack() as ph:
                gT = S.sb("gT", [128, FT, 512 + NS], BF16, es=ph)
                with contextlib.ExitStack() as p5:
                    TBf = 512 + (64 if has_sample else 128)
                    h2T = S.sb("h2T", [128, KT, TBf], BF16, es=p5)
                    h2b = S.tok("h2T")
                    xt2 = [S.sb("xq%d" % i, [128, D], F32, es=p5) for i in range(2)]
                    xn2 = [S.sb("xqn%d" % i, [128, D], BF16, es=p5) for i in range(2)]
                    abuf = [S.sb("abuf%d" % i, [128, 514], F32, es=p5) for i in range(2)]
                    cbuf = [S.sb("cbuf%d" % i, [128, 512], F32, es=p5) for i in range(2)]
                    asb = S.sb("asb", [128, 16, 6], F32, es=p5)
                    csb = S.sb("csb", [128, 16, 4], F32, es=p5)
                    if has_sample:
                        aconv_p = S.sb("aconv_p", [128, FT, 2], F32, es=p5)
                        aconv_s = S.sb("aconv_s", [128, FT, 32], F32, es=p5)
                        cache_s = S.sb("cache_s", [128, FT, 32], F32, es=p5)
                        cc32 = abuf_cc = None
                    def st_a(ti):
                        t = tiles[ti]
                        xt, xn = xt2[ti % 2], xn2[ti % 2]
                        np_ = t["np"]
                        S.dma("sp", xt[0:np_, :], x1scr[t["srow"]:t["srow"] + np_, :], reads=[x1tok], writes=[xt], dsem="d_xq%d" % (ti % 2))
                        rstd, nmr, mvb = ln_stats(xt, np_, xt)
                        ACT(xn[0:np_, :], xt[0:np_, :], AF.Identity, [xt, mvb], [xn], bias=nmr, scale=rstd)

                    def st_b(ti):
                        t = tiles[ti]
                        xn = xn2[ti % 2]
                        to_featT(xn, xn, t["np"], h2T, h2b, t["col"], 2, 3, t["k"] == "sample")
                    st_a(0)
                    for ti in range(len(tiles)):
                        if ti + 1 < len(tiles):
                            st_a(ti + 1)
                        st_b(ti)
                    if has_sample:
                        ccb = [S.sb("ccb%d" % i, [32, 1024], F32, es=p5) for i in range(2)]
                        for g0 in range(0, FT, 8):
                            n = min(8, FT - g0)
                            cb_ = ccb[(g0 // 8) % 2]
                            S.dma("sp", cb_[:, 0:128 * n], cc_d[:, 128 * g0:128 * (g0 + n)], writes=[cb_], dsem="d_ccb%d" % ((g0 // 8) % 2))
                            bk = nb()
                            for q in range(n):
                                TR(bk, bk[:, q * 32:q * 32 + 32], cb_[0:32, q * 128:(q + 1) * 128], identf[0:32, 0:32], [cb_, identf])
                            CP("dve", cache_s[:, g0:g0 + n, :], bk[:, 0:32 * n].rearrange("p (a b) -> p a b", b=32), [bk], [cache_s])
                    ngr = (FT + 1) // 2

                    def job_au(gi):
                        def fn(slot):
                            sa = slot
                            nft = min(2, FT - 2 * gi)
                            for q in range(nft):
                                ft = 2 * gi + q
                                cs = slice(q * 128, (q + 1) * 128)
                                cu = slice(256 + q * 128, 256 + (q + 1) * 128)
                                bA = nb()
                                for kt in range(KT):
                                    MM(bA, bA[:, 0:512], sa[:, kt, cs], h2T[:, kt, 0:512], kt == 0, kt == KT - 1, [sa, h2b])
                                bU = nb()
                                for kt in range(KT):
                                    MM(bU, bU[:, 0:512], slot[:, kt, cu], h2T[:, kt, 0:512], kt == 0, kt == KT - 1, [sa, h2b])
                                bB = nb()
                                if has_sample:
                                    for kt in range(KT):
                                        MM(bB, bB[:, 0:64], sa[:, kt, cs], h2T[:, kt, 512:576], kt == 0, kt == KT - 1, [sa, h2b])
                                    for kt in range(KT):
                                        MM(bB, bB[:, 64:128], slot[:, kt, cu], h2T[:, kt, 512:576], kt == 0, kt == KT - 1, [sa, h2b])
                                else:
                                    for kt in range(KT):
                                        MM(bB, bB[:, 0:2], sa[:, kt, cs], h2T[:, kt, 638:640], kt == 0, kt == KT - 1, [sa, h2b])
                                ab = abuf[ft % 2]
                                cbf = cbuf[ft % 2]
                                if has_sample:
                                    CP("dve", ab[:, 0:2], hist[:, ft, :], [hist], [ab])
                                else:
                                    TS("dve", ab[:, 0:2], bB[:, 0:2], flag[:, 0:1], None, ALU.mult, None, [bB, flag], [ab])
                                CP("act", ab[:, 2:514], bA[:, 0:512], [bA], [ab])
                                if has_sample:
                                    CP("dve", aconv_p[:, ft, :], ab[:, 512:514], [ab], [P(aconv_p)])
                                else:
                                    CP("dve", hist[:, ft, :], ab[:, 512:514], [ab], [P(hist)])
                                ACT(cbf[:, :], ab[:, 2:514], AF.Identity, [ab, convw], [cbf], bias=convw[:, ft, 3:4], scale=convw[:, ft, 2:3])
                                STT(cbf[:, :], ab[:, 1:513], convw[:, ft, 1:2], cbf[:, :], ALU.mult, ALU.add, [ab, convw, cbf], [cbf])
                                STT(cbf[:, :], ab[:, 0:512], convw[:, ft, 0:1], cbf[:, :], ALU.mult, ALU.add, [ab, convw, cbf], [cbf])
                                ACT(cbf[:, :], cbf[:, :], AF.Gelu, [cbf], [cbf])
                                TT("dve", gT[:, ft, 0:512], cbf[:, :], bU[:, 0:512], ALU.mult, [cbf, bU], [P(gT)])
                                if has_sample:
                                    CP("dve", asb[:, :, 0:2], cache_s[:, ft, :].rearrange("p (s j) -> p s j", j=2), [cache_s], [asb])
                                    CP("act", asb[:, :, 2:6], bB[:, 0:64].rearrange("p (s j) -> p s j", j=4), [bB], [asb])
                                    CP("dve", aconv_s[:, ft, :].rearrange("p (s j) -> p s j", j=2), asb[:, :, 4:6], [asb], [P(aconv_s)])
                                    ACT(csb[:, :, :], asb[:, :, 2:6], AF.Identity, [asb, convw], [csb], bias=convw[:, ft, 3:4], scale=convw[:, ft, 2:3])
                                    STT(csb[:, :, :], asb[:, :, 1:5], convw[:, ft, 1:2], csb[:, :, :], ALU.mult, ALU.add, [asb, convw, csb], [csb])
                                    STT(csb[:, :, :], asb[:, :, 0:4], convw[:, ft, 0:1], csb[:, :, :], ALU.mult, ALU.add, [asb, convw, csb], [csb])
                                    ACT(csb[:, :, :], csb[:, :, :], AF.Gelu, [csb], [csb])
                                    TT("dve", gT[:, ft, 512:576].rearrange("p (s j) -> p s j", j=4), csb[:, :, :], bB[:, 64:128].rearrange("p (s j) -> p s j", j=4), ALU.mult, [csb, bB], [P(gT)])
                        return fn
                    jobs = []
                    for gi in range(ngr):
                        nc_ = min(256, DFF - 256 * gi)
                        jobs.append(([(w_up_v, 0, KT, 256 * gi, nc_, 0), (w_up_v, 0, KT, DFF + 256 * gi, nc_, 256)], job_au(gi)))
                    stream(jobs, nxt=W1(w_down_v, 0, 16, 0, 512))
                    if has_sample:
                        cvo = [S.sb("cvo%d" % i, [32, 512], F32, es=p5) for i in range(2)]
                        cvq = [S.sb("cvq%d" % i, [2, 512], F32, es=p5) for i in range(2)]
                        for gi in range((FT + 3) // 4):
                            nft = min(4, FT - 4 * gi)
                            bk = nb()
                            for q in range(nft):
                                TR(bk, bk[0:32, q * 128:(q + 1) * 128], aconv_s[:, 4 * gi + q, :], identf[:, :], [aconv_s, identf])
                            o = cvo[gi % 2]
                            CP("dve", o[:, 0:128 * nft], bk[0:32, 0:128 * nft], [bk], [o])
                            S.dma("sp", cvs_o[:, 512 * gi:512 * gi + 128 * nft], o[:, 0:128 * nft], reads=[o], dsem="d_cvo%d" % (gi % 2), is_output=True)
                            bk = nb()
                            for q in range(nft):
                                TR(bk, bk[0:2, q * 128:(q + 1) * 128], aconv_p[:, 4 * gi + q, :], identf[:, :], [aconv_p, identf])
                            o2 = cvq[gi % 2]
                            CP("dve", o2[:, 0:128 * nft], bk[0:2, 0:128 * nft], [bk], [o2])
                            S.dma("sp", cvp_o[:, 512 * gi:512 * gi + 128 * nft], o2[:, 0:128 * nft], reads=[o2], dsem="d_cvq%d" % (gi % 2), is_output=True)
                    S.barrier()
                with contextlib.ExitStack() as p6:
                    otl = main + ([aux] if has_sample else [])
                    NTo = len(otl)
                    xa = S.sb("xb_", [128, NTo, D], F32, es=p6)
                    xab = [S.tok("xb%d" % i) for i in range(NTo)]
                    lng = S.sb("lng2", [128, D], F32, es=p6)
                    lnb = S.sb("lnb2", [128, D], F32, es=p6)
                    gP = [S.sb("gQ%d" % i, [128, 512], F32, es=p6) for i in range(2)]
                    gS = [S.sb("gR%d" % i, [64, 512], F32, es=p6) for i in range(2)]
                    tmp = [S.sb("tmpd%d" % i, [128, 512], F32, es=p6) for i in range(2)]
                    S.dma("sp", lng[:, :], rowbc(ln2g_d[0:1, :], D), writes=[lng], dsem="d_lng")
                    S.dma("sp", lnb[:, :], rowbc(ln2b_d[0:1, :], D), writes=[lnb], dsem="d_lnb")
                    for ti, t in enumerate(otl):
                        S.dma("sp", xa[0:t["np"], ti, :], x1scr[t["srow"]:t["srow"] + t["np"], :], reads=[x1tok], writes=[xab[ti]], dsem="d_xb%d" % ti)
                    accb = [banks[i] for i in range(NTo)]
                    bstate["excl"] = set(range(NTo))
                    kgroups = [(0, 16), (16, 16), (32, 11)]

                    def gcol(t):
                        return (t["col"], t["np"])

                    def job_d(c, gi):
                        k0, nk = kgroups[gi]

                        def fn(slot):
                            for ti, t in enumerate(otl):
                                np_ = t["np"]
                                for kk in range(nk):
                                    MM(accb[ti], accb[ti][0:np_, 0:512], gT[:, k0 + kk, t["col"]:t["col"] + np_], slot[:, kk, :], gi == 0 and kk == 0, gi == 2 and kk == nk - 1, [gT, slot])
                            if gi == 2:
                                g = gP[c % 2]
                                S.dma("sp", g[:, :], rowbc(gscr[1, 0:1, 512 * c:512 * c + 512], 512), reads=[gscr_tok], writes=[g], dsem="d_gQ%d" % (c % 2))
                                if has_sample:
                                    g2 = gS[c % 2]
                                    S.dma("sp", g2[:, :], gscr_s[1, :, 512 * c:512 * c + 512], reads=[gscr_tok], writes=[g2], dsem="d_gR%d" % (c % 2))
                                for ti, t in enumerate(otl):
                                    np_ = t["np"]
                                    gg = g2 if t["k"] == "sample" else g
                                    tp = tmp[ti % 2]
                                    TT("dve", tp[0:np_, :], accb[ti][0:np_, 0:512], gg[0:np_, :], ALU.mult, [accb[ti], gg], [tp])
                                    STT(xa[0:np_, ti, 512 * c:512 * c + 512], xa[0:np_, ti, 512 * c:512 * c + 512], ALPHA, tp[0:np_, :], ALU.mult, ALU.add, [xab[ti], tp], [xab[ti]])
                        return fn
                    jobs = []
                    for c in range(4):
                        for gi in range(3):
                            k0, nk = kgroups[gi]
                            jobs.append((w_down_v, k0, nk, 512 * c, 512, job_d(c, gi)))
                    stream(jobs, nxt=(W1(w_in_v, 0, KT, O_QA, 512) if kind == "M0" else None))
                    bstate["excl"] = set()
                    def after2(ti, t):
                        np_ = t["np"]
                        if t["k"] == "sample":
                            S.dma("sp", ys_o[:, :], xa[0:np_, ti, :], reads=[xab[ti]], dsem="d_yo%d" % ti, is_output=True)
                        else:
                            S.dma("sp", yp_o[t["orow"]:t["orow"] + 128, :], xa[0:np_, ti, :], reads=[xab[ti]], dsem="d_yo%d" % ti, is_output=True)
                    ln_all(xa, xab, otl, lng, lnb, after2)
                    S.barrier()

        for pi, kind in enumerate(("P0", "P1")):
            mixer_phase(kind, None, None, None, extra=ada_late[8 * pi:8 * pi + 8])
        S.barrier()
        del wslots[2:]
        wstate["i"] = 0
        pending["key"] = None
        pre_es.close()
        for kind in ("M0", "M1"):
            with contextlib.ExitStack() as bes:
                TBk = 640 if kind == "M0" else 576
                yT = S.sb("yT", [128, KT, TBk], BF16, es=bes)
                yTb = [S.tok("yT%d" % j) for j in range(5)]
                tiles, logical, TB, ranges = mixer_phase(kind, yT, yTb, bes)
                outproj_phase(kind, tiles, yT, yTb)
                S.barrier()
            ffn_phase(kind, tiles)

        with contextlib.ExitStack() as pf:
            for h in range(4):
                S.dma("sp", Cp_o[h], Cst[h][:, 0:256], reads=[Cst[h]], dsem="d_cpo", is_output=True)
            npt = S.sb("npt", [128, 4], F32, es=pf)
            for h in range(4):
                CP("dve", npt[:, h:h + 1], Cst[h][:, 256:257], [Cst[h]], [npt])
            bk = nb()
            TR(bk, bk[0:4, 0:128], npt[:, 0:4], identf[:, :], [npt, identf])
            npo = S.sb("npo", [4, 128], F32, es=pf)
            CP("dve", npo[:, :], bk[0:4, 0:128], [bk], [npo])
            S.dma("sp", np_o[:, :], npo[:, :], reads=[npo], dsem="d_npo", is_output=True)
            S.dma("sp", bass.AP(mp_o.tensor, mp_o.offset, [[1, 4], [1, 1]]), mcar[:, 0:1], reads=[mcar], dsem="d_mpo", is_output=True)
            for h in range(8):
                S.dma("sp", Sp_o[h], Sst[h][:, :], reads=[Sst[h]], dsem="d_spo", is_output=True)
            bk = nb()
            TR(bk, bk[0:64, 0:128], nTo[:, 0:64], identf[:, :], [nTo, identf])
            nso = S.sb("nso", [64, 128], F32, es=pf)
            CP("dve", nso[:, :], bk[0:64, 0:128], [bk], [nso])
            S.dma("sp", ns_o[:, :], nso[:, :], reads=[nso], dsem="d_nso", is_output=True)
            S.finish()
        ninst = S.ninst
        print('min sbuf remaining', getattr(S, 'minrem', None), getattr(S, 'minrem_at', None))
    return nc, ninst


_CACHE = {}


def _consts():
    s = np.arange(128)
    maskp = (s[:, None] <= s[None, :]).astype(np.float32)
    s6 = np.arange(64)
    masks = ((s6[:, None] // 4 == s6[None, :] // 4) & (s6[:, None] <= s6[None, :])).astype(np.float32)
    bmask = (s6[:, None] // 4 == np.arange(16)[None, :]).astype(np.float32)
    cm16 = (np.arange(16)[:, None] == s6[None, :] // 4).astype(np.float32).reshape(1, 1024)
    cm2 = (np.arange(2)[:, None] == s[None, :] // 64).astype(np.float32).reshape(1, 256)
    tA = np.arange(640)
    rmA = np.stack([(tA % 128 != 0).astype(np.float32), np.where(tA % 128 == 0, NEG, 0.0).astype(np.float32)])
    tB = np.arange(576)
    startB = np.where(tB < 512, tB % 128 == 0, tB % 4 == 0)
    rmB = np.stack([(~startB).astype(np.float32), np.where(startB, NEG, 0.0).astype(np.float32)])
    diag = np.zeros((4, 4, 24), np.float32)
    for k in range(4):
        diag[k, k, :] = 1.0
    return dict(c_ident=np.eye(128, dtype=np.float32), c_maskp=maskp, c_masks=masks, c_bmask=bmask, c_cm16=cm16, c_cm2=cm2,
                c_rmA=rmA, c_rmB=rmB, c_diag=diag.reshape(4, 96))


def make_in_maps(inp):
    f = lambda a: np.ascontiguousarray(np.asarray(a, dtype=np.float32))
    xpr, xsm = f(inp["x_prompt"]), f(inp["x_sample"])
    cst = _consts()
    shared = dict(
        lbl=f(inp["hgrn_lb_logits"]), w_ada=f(inp["w_ada"][0]), b_ada=f(inp["b_ada"]), w_in=f(inp["w_in"][0]),
        b_gate_a=f(inp["b_gate_a"][0]), norm_a=f(inp["norm_a"]), norm_b=f(inp["norm_b"]), w_out=f(inp["w_out"][0]),
        ln1_g=f(inp["ln1_g"]), ln1_b=f(inp["ln1_b"]), w_up=f(inp["w_up"][0]), conv_w=f(inp["conv_w"][0]), conv_b=f(inp["conv_b"]),
        w_down=f(inp["w_down"][0]), ln2_g=f(inp["ln2_g"]), ln2_b=f(inp["ln2_b"]), **cst)
    maps = []
    for c in range(8):
        b, half = c // 2, c % 2
        sl = slice(16 * c, 16 * c + 16)
        m = dict(shared)
        m["xpre"] = f(xpr[b, 0:1024])
        m["xp"] = f(xpr[b, 1024 * half:1024 * half + 1024])
        m["xs"] = f(xsm[sl].reshape(64, D))
        m["flag"] = np.full((128, 1), float(half), np.float32)
        m["c17"] = f(np.concatenate([inp["c_prompt"][b:b + 1], inp["c_sample"][sl]], axis=0))
        m["sC"] = f(inp["state_mlstm_C"][0, sl])
        m["sn"] = f(inp["state_mlstm_n"][0, sl].reshape(64, 128))
        m["sm"] = f(inp["state_mlstm_m"][0, sl])
        m["sS"] = f(inp["state_hgrn_S"][0, sl])
        m["cconv"] = f(inp["cache_ffn_conv"][0, sl].reshape(32, DFF))
        maps.append(m)
    return maps


def assemble(res):
    R = res.results
    yp = np.zeros((4, 2048, D), np.float32)
    ys = np.zeros((128, 4, D), np.float32)
    Cp = np.zeros((1, 4, 4, 128, 256), np.float32)
    n_p = np.zeros((1, 4, 4, 128), np.float32)
    mp = np.zeros((1, 4, 4), np.float32)
    Sp = np.zeros((1, 4, 8, 128, 128), np.float32)
    cvp = np.zeros((1, 4, 2, DFF), np.float32)
    Cs = np.zeros((1, 128, 4, 128, 256), np.float32)
    ns = np.zeros((1, 128, 4, 128), np.float32)
    ms = np.zeros((1, 128, 4), np.float32)
    Ss = np.zeros((1, 128, 8, 128, 128), np.float32)
    cvs = np.zeros((1, 128, 2, DFF), np.float32)
    for c in range(8):
        b, half = c // 2, c % 2
        r = R[c]
        sl = slice(16 * c, 16 * c + 16)
        yp[b, 1024 * half:1024 * half + 1024] = r["yp"]
        ys[sl] = r["ys"].reshape(16, 4, D)
        if half == 1:
            Cp[0, b] = r["Cp"]
            n_p[0, b] = r["np"]
            mp[0, b] = r["mp"].reshape(4)
            Sp[0, b] = r["Sp"]
            cvp[0, b] = r["convp"]
        Cs[0, sl] = r["Cs"]
        ns[0, sl] = r["ns"].reshape(16, 4, 128)
        ms[0, sl] = r["ms"]
        Ss[0, sl] = r["Ss"]
        cvs[0, sl] = r["convs"].reshape(16, 2, DFF)
    return (yp, ys, Cp, n_p, mp, Sp, cvp, Cs, ns, ms, Ss, cvs)


def kernel(**inputs):
    if "nc" not in _CACHE:
        _CACHE["nc"] = build_program()[0]
    nc = _CACHE["nc"]
    maps = make_in_maps(inputs)
    res = run_bass_kernel_spmd(nc, maps, core_ids=list(range(8)))
    return assemble(res)
```

```python
import contextlib
import numpy as np
import concourse.bass as bass
import concourse.mybir as mybir
from concourse.bass_utils import run_bass_kernel_spmd

F32 = mybir.dt.float32
BF16 = mybir.dt.bfloat16
AF = mybir.ActivationFunctionType
ALU = mybir.AluOpType

D = 2048
KT = 16
DIN = 7176
DFF = 5504
FT = 43
EPS = 1e-5
ALPHA = 2.0 ** 0.25
O_QA, O_KA, O_VA, O_IA, O_FA, O_OA, O_FB, O_QB, O_VB, O_GB = 0, 512, 1024, 2048, 2052, 2056, 3080, 4104, 5128, 6152
NSEQ = 16
NEG = -1.0e30


class Buf:
    __slots__ = ("name", "t", "wset", "readers", "excl")

    def __init__(self, name, t=None):
        self.name = name
        self.t = t
        self.wset = {}
        self.readers = {}
        self.excl = False

    def __getitem__(self, idx):
        return self.t[idx]


class Part:
    __slots__ = ("b",)

    def __init__(self, b):
        self.b = b


def P(b):
    return Part(b)


class Sched:
    def __init__(self, nc, es):
        self.nc = nc
        self.es = es
        self.eng = {"pe": nc.tensor, "act": nc.scalar, "dve": nc.vector, "pool": nc.gpsimd, "sp": nc.sync}
        self.sem = {}
        self.cnt = {}
        for k in self.eng:
            self.sem[k] = es.enter_context(nc.semaphore("s_" + k))
            self.cnt[k] = 0
        self.waited = {k: {} for k in self.eng}
        self.out_events = []
        self.ninst = 0

    def sb(self, name, shape, dt=F32, es=None):
        self.nalloc = getattr(self, "nalloc", 0) + 1
        name = "%s_u%d" % (name, self.nalloc)
        t = (es or self.es).enter_context(self.nc.sbuf_tensor(name, list(shape), dt))
        try:
            rem = self.nc.sbuf_bytes_remaining
            rem = rem() if callable(rem) else rem
            if rem < getattr(self, "minrem", 1 << 30):
                self.minrem = rem
                self.minrem_at = name
        except Exception:
            pass
        return Buf(name, t)

    def ps(self, name, shape, dt=F32):
        t = self.es.enter_context(self.nc.psum_tensor(name, list(shape), dt))
        b = Buf(name, t)
        b.excl = True
        return b

    def tok(self, name):
        return Buf(name, None)

    def _emit_waits(self, e, deps):
        eng = self.eng[e]
        w = self.waited[e]
        for key, val in deps.items():
            if w.get(key, 0) >= val:
                continue
            eng.wait_ge(self.sem[key], val)
            w[key] = val

    def _deps(self, e, reads, writes, same_engine_ok=False):
        deps = {}

        def need(kv):
            k, v = kv
            if deps.get(k, 0) < v:
                deps[k] = v
        for b in reads:
            for kv in b.wset.items():
                need(kv)
            if b.excl:
                for k, v in b.readers.items():
                    if k != e:
                        need((k, v))
        for w in writes:
            if isinstance(w, Part):
                for kv in w.b.readers.items():
                    need(kv)
            else:
                for kv in w.wset.items():
                    need(kv)
                for kv in w.readers.items():
                    need(kv)
        if same_engine_ok and e in deps:
            del deps[e]
        return deps

    def _record(self, key, v, reads, writes):
        for b in reads:
            if b.readers.get(key, 0) < v:
                b.readers[key] = v
        for w in writes:
            if isinstance(w, Part):
                w.b.wset[key] = v
            else:
                w.wset = {key: v}
                w.readers = {}

    def op(self, e, fn, reads=(), writes=(), same_engine_ok=False):
        deps = self._deps(e, reads, writes, same_engine_ok)
        self._emit_waits(e, deps)
        inst = fn()
        self.cnt[e] += 1
        v = self.cnt[e]
        inst.then_inc(self.sem[e], 1)
        self._record(e, v, reads, writes)
        self.ninst += 1
        return inst

    def dma(self, q, out_ap, in_ap, reads=(), writes=(), dsem=None, is_output=False):
        deps = self._deps(q, reads, writes)
        self._emit_waits(q, deps)
        if dsem not in self.sem:
            self.sem[dsem] = self.es.enter_context(self.nc.semaphore(dsem))
            self.cnt[dsem] = 0
        with self.nc.allow_non_contiguous_dma(reason="small strided layout loads"):
            inst = self.eng[q].dma_start(out=out_ap, in_=in_ap)
        self.cnt[dsem] += 16
        v = self.cnt[dsem]
        inst.then_inc(self.sem[dsem], 16)
        self._record(dsem, v, reads, writes)
        if is_output:
            self.out_events.append((dsem, v))
        self.ninst += 1
        return inst

    def barrier(self):
        for e in self.eng:
            deps = {k: v for k, v in self.cnt.items() if v > 0 and not k.startswith("d_w")}
            self._emit_waits(e, deps)

    def finish(self):
        deps = {}
        for k, v in self.out_events:
            if deps.get(k, 0) < v:
                deps[k] = v
        self._emit_waits("sp", deps)


def bc(ap, shape):
    return ap.broadcast_to(list(shape))


def build_program(dbg=False):
    nc = bass.Bass("TRN2", target_bir_lowering=False)

    def din(name, shape):
        return nc.dram_tensor(name, list(shape), F32, kind="ExternalInput").ap()

    def dout(name, shape):
        return nc.dram_tensor(name, list(shape), F32, kind="ExternalOutput").ap()

    xpre = din("xpre", [1024, D])
    xp = din("xp", [1024, D])
    xs = din("xs", [64, D])
    flag_d = din("flag", [128, 1])
    c17_d = din("c17", [17, D])
    sC_d = din("sC", [NSEQ, 4, 128, 256])
    sn_d = din("sn", [NSEQ * 4, 128])
    sm_d = din("sm", [NSEQ, 4])
    sS_d = din("sS", [NSEQ, 8, 128, 128])
    cc_d = din("cconv", [NSEQ * 2, DFF])
    lbl_d = din("lbl", [2, 1024])
    w_ada = din("w_ada", [D, 6 * D])
    b_ada = din("b_ada", [1, 6 * D])
    w_in = din("w_in", [D, DIN])
    bga_d = din("b_gate_a", [2, 4])
    norm_a_d = din("norm_a", [1, 1024])
    norm_b_d = din("norm_b", [1, 1024])
    w_out = din("w_out", [D, D])
    ln1g_d = din("ln1_g", [1, D])
    ln1b_d = din("ln1_b", [1, D])
    w_up = din("w_up", [D, 2 * DFF])
    cw_d = din("conv_w", [3, DFF])
    cb_d = din("conv_b", [1, DFF])
    w_down = din("w_down", [DFF, D])
    ln2g_d = din("ln2_g", [1, D])
    ln2b_d = din("ln2_b", [1, D])
    c_ident = din("c_ident", [128, 128])
    c_maskp = din("c_maskp", [128, 128])
    c_masks = din("c_masks", [64, 64])
    c_bmask = din("c_bmask", [64, 16])
    c_cm16 = din("c_cm16", [1, 16 * 64])
    c_cm2 = din("c_cm2", [1, 2 * 128])
    c_rmA = din("c_rmA", [2, 640])
    c_rmB = din("c_rmB", [2, 576])
    c_diag = din("c_diag", [4, 96])

    yp_o = dout("yp", [1024, D])
    ys_o = dout("ys", [64, D])
    Cp_o = dout("Cp", [4, 128, 256])
    np_o = dout("np", [4, 128])
    mp_o = dout("mp", [1, 4])
    Sp_o = dout("Sp", [8, 128, 128])
    cvp_o = dout("convp", [2, DFF])
    Cs_o = dout("Cs", [NSEQ, 4, 128, 256])
    ns_o = dout("ns", [NSEQ * 4, 128])
    ms_o = dout("ms", [NSEQ, 4])
    Ss_o = dout("Ss", [NSEQ, 8, 128, 128])
    cvs_o = dout("convs", [NSEQ * 2, DFF])

    gscr = nc.dram_tensor("gscr", [2, 17, D], F32, kind="Internal").ap()
    gscr_s = nc.dram_tensor("gscr_s", [2, 64, D], F32, kind="Internal").ap()
    x1scr = nc.dram_tensor("x1scr", [1024 + 128 + 64, D], F32, kind="Internal").ap()

    w_in_v = w_in.rearrange("(kt p) n -> p kt n", p=128)
    w_ada_v = w_ada.rearrange("(kt p) n -> p kt n", p=128)
    w_out_v = w_out.rearrange("(kt p) n -> p kt n", p=128)
    w_up_v = w_up.rearrange("(kt p) n -> p kt n", p=128)
    w_down_v = w_down.rearrange("(kt p) n -> p kt n", p=128)

    def rowbc(ap_row, n, parts=128):
        return bass.AP(ap_row.tensor, ap_row.offset, [[0, parts], [1, n]])

    with contextlib.ExitStack() as es:
        S = Sched(nc, es)
        V = nc.vector
        G = nc.gpsimd

        def ACT(out, in_, func, reads, writes, bias=None, scale=None, accum=None):
            kw = {}
            if bias is not None:
                kw["bias"] = bias
            if scale is not None:
                kw["scale"] = scale
            if accum is not None:
                kw["accum_out"] = accum
            S.op("act", lambda: nc.scalar.activation(out=out, in_=in_, func=func, **kw), reads, writes)

        def TS(eng, out, in0, s1, s2, op0, op1, reads, writes):
            e = V if eng == "dve" else G
            if op1 is None:
                S.op(eng, lambda: e.tensor_scalar(out=out, in0=in0, scalar1=s1, scalar2=None, op0=op0), reads, writes)
            else:
                S.op(eng, lambda: e.tensor_scalar(out=out, in0=in0, scalar1=s1, scalar2=s2, op0=op0, op1=op1), reads, writes)

        def TT(eng, out, in0, in1, op, reads, writes):
            e = V if eng == "dve" else G
            S.op(eng, lambda: e.tensor_tensor(out=out, in0=in0, in1=in1, op=op), reads, writes)

        def STT(out, in0, scalar, in1, op0, op1, reads, writes):
            S.op("dve", lambda: V.scalar_tensor_tensor(out=out, in0=in0, scalar=scalar, in1=in1, op0=op0, op1=op1), reads, writes)

        def CP(eng, out, in_, reads, writes):
            if eng == "act":
                S.op("act", lambda: nc.scalar.copy(out=out, in_=in_), reads, writes)
            else:
                e = V if eng == "dve" else G
                S.op(eng, lambda: e.tensor_copy(out=out, in_=in_), reads, writes)

        def MSET(eng, ap, val, writes):
            e = V if eng == "dve" else G
            S.op(eng, lambda: e.memset(ap, val), (), writes)

        def RECIP(out, in_, reads, writes):
            S.op("dve", lambda: V.reciprocal(out=out, in_=in_), reads, writes)

        def SCAN(out, d0, d1, init, op0, op1, reads, writes):
            S.op("dve", lambda: V.tensor_tensor_scan(out=out, data0=d0, data1=d1, initial=init, op0=op0, op1=op1), reads, writes)

        def MM(bank, out, lhsT, rhs, start, stop, reads):
            S.op("pe", lambda: nc.tensor.matmul(out, lhsT=lhsT, rhs=rhs, start=start, stop=stop), reads, [bank], same_engine_ok=True)

        def TR(bank, out, in_, ident, reads):
            S.op("pe", lambda: nc.tensor.transpose(out=out, in_=in_, identity=ident), reads, [bank], same_engine_ok=True)

        banks = [S.ps("bank%d" % i, [128, 512], F32) for i in range(8)]
        bstate = {"i": 0, "excl": set()}

        def nb():
            while True:
                b = banks[bstate["i"] % 8]
                bstate["i"] += 1
                if (bstate["i"] - 1) % 8 not in bstate["excl"]:
                    return b

        def bfv(bank):
            return bank[:, :].bitcast(BF16)

        identf = S.sb("identf", [128, 128], F32)
        identb = S.sb("identb", [128, 128], BF16)
        maskp = S.sb("maskp", [128, 128], F32)
        masks = S.sb("masks", [64, 64], F32)
        bmask = S.sb("bmask", [64, 16], F32)
        cm16 = S.sb("cm16", [128, 16, 64], BF16)
        cm2 = S.sb("cm2", [128, 2, 128], BF16)
        rmt = S.sb("rmt", [128, 2, 640], F32)
        diag4 = S.sb("diag4", [4, 4, 24], F32)
        ones4 = S.sb("ones4", [4, 128], F32)
        flag = S.sb("flagt", [128, 1], F32)
        cst2 = S.sb("cst2", [128, 2], F32)
        modT = S.sb("modT", [128, 4, KT, 17], F32)
        lbv = S.sb("lbv", [128, 3, 8], F32)
        convw = S.sb("convw", [128, FT, 4], F32)
        bga = S.sb("bga", [4, 4], F32)
        Cst = [S.sb("C%d" % h, [128, 257], F32) for h in range(4)]
        Cbf = [S.sb("Cbf%d" % h, [128, 257], BF16) for h in range(4)]
        Sst = [S.sb("S%d" % h, [128, 128], F32) for h in range(8)]
        Sbf = [S.sb("Sbf%d" % h, [128, 128], BF16) for h in range(8)]
        mcar = S.sb("mcar", [4, 1], F32)
        hist = S.sb("hist", [128, FT, 2], F32)
        nTs = S.sb("nTs", [128, 64], F32)
        nTo = S.sb("nTo", [128, 64], F32)
        mprev_s = S.sb("mprev_s", [4, 16], F32)
        wslots = [S.sb("wslot%d" % i, [128, KT, 512], BF16) for i in range(2)]
        wstate = {"i": 0}
        wg = S.sb("wg", [128, KT, 8], BF16)
        stat = [S.sb("stat%d" % i, [128, 4, 6], F32) for i in range(2)]
        mvv = [S.sb("mv%d" % i, [128, 8], F32) for i in range(2)]
        smt = [S.sb("smt%d" % i, [128, 16], F32) for i in range(4)]
        smstate = {"i": 0}
        lnstate = {"i": 0}
        statA = S.sb("statA", [128, 5, 4, 6], F32)
        mvA = S.sb("mvA", [128, 5, 2], F32)
        rsA = S.sb("rsA", [128, 3, 5], F32)

        def nsm():
            b = smt[smstate["i"] % 4]
            smstate["i"] += 1
            return b

        def wload(pieces):
            idx = wstate["i"] % len(wslots)
            slot = wslots[idx]
            wstate["i"] += 1
            for (view, k0, nkt, c0, ncols, dcol) in pieces:
                S.dma("pool", slot[:, 0:nkt, dcol:dcol + ncols], view[:, k0:k0 + nkt, c0:c0 + ncols], writes=[slot], dsem="d_w%d" % idx)
            return slot

        pending = {"key": None, "slot": None}

        def pkey(pieces):
            return tuple((id(p[0]),) + tuple(p[1:]) for p in pieces)

        def stream(jobs, nxt=None):
            def pcs(job):
                if len(job) == 2:
                    return job[0]
                return [tuple(job[:5]) + (0,)]
            if not jobs:
                return
            depth = len(wslots) - 1
            q = []
            p0 = pcs(jobs[0])
            if pending["key"] is not None and pending["key"] == pkey(p0):
                q.append(pending["slot"])
            else:
                q.append(wload(p0))
            pending["key"] = None
            nl = 1
            n = len(jobs)
            for i, job in enumerate(jobs):
                while nl < n and nl <= i + depth:
                    q.append(wload(pcs(jobs[nl])))
                    nl += 1
                if i == n - 1 and nxt is not None:
                    pending["slot"] = wload(nxt)
                    pending["key"] = pkey(nxt)
                job[-1](q.pop(0))

        def W1(view, k0, nkt, c0, ncols):
            return [(view, k0, nkt, c0, ncols, 0)]

        S.dma("sp", identf[:, :], c_ident[:, :], writes=[identf], dsem="d_c0")
        S.dma("sp", maskp[:, :], c_maskp[:, :], writes=[maskp], dsem="d_c1")
        S.dma("sp", masks[:, :], c_masks[:, :], writes=[masks], dsem="d_c2")
        S.dma("sp", bmask[:, :], c_bmask[:, :], writes=[bmask], dsem="d_c3")
        S.dma("sp", flag[:, :], flag_d[:, :], writes=[flag], dsem="d_c4")
        S.dma("sp", diag4[:, :, :], c_diag.rearrange("k (h c) -> k h c", c=24), writes=[diag4], dsem="d_c5")
        CP("dve", identb[:, :], identf[:, :], [identf], [identb])
        MSET("dve", ones4[:, :], 1.0, [ones4])
        MSET("dve", cst2[:, 0:1], 1.0, [cst2])
        MSET("dve", cst2[:, 1:2], 0.0, [cst2])
        bmaskb = bmask
        MSET("dve", mcar[:, :], 0.0, [mcar])
        MSET("dve", statA[:, :, :, :], 0.0, [statA])
        MSET("dve", mvA[:, :, :], 1.0, [mvA])
        MSET("dve", hist[:, :, :], 0.0, [hist])
        for h in range(4):
            MSET("dve", Cst[h][:, :], 0.0, [Cst[h]])
        for h in range(8):
            MSET("dve", Sst[h][:, :], 0.0, [Sst[h]])
            MSET("dve", Sbf[h][:, :], 0.0, [Sbf[h]])

        pre_es = contextlib.ExitStack()
        for i_ in range(2):
            wslots.append(S.sb("wslotx%d" % i_, [128, KT, 512], BF16, es=pre_es))
        scT = S.sb("scT", [128, KT, 17], BF16, es=pre_es)
        bAt = [S.sb("bAt%d" % i, [17, 512], F32, es=pre_es) for i in range(2)]
        modc = [S.sb("modc%d" % i, [17, 512], F32, es=pre_es) for i in range(2)]
        with contextlib.ExitStack() as p0:
            tmpc = S.sb("tmpc", [128, 16 * 64], F32, es=p0)
            S.dma("sp", tmpc[:, 0:1024], rowbc(c_cm16[0:1, :], 1024), writes=[tmpc], dsem="d_c8")
            CP("dve", cm16[:, :, :], tmpc[:, 0:1024].rearrange("p (a b) -> p a b", b=64), [tmpc], [cm16])
            tmpc2 = S.sb("tmpc2", [128, 256], F32, es=p0)
            S.dma("sp", tmpc2[:, :], rowbc(c_cm2[0:1, :], 256), writes=[tmpc2], dsem="d_c9")
            CP("dve", cm2[:, :, :], tmpc2[:, :].rearrange("p (a b) -> p a b", b=128), [tmpc2], [cm2])
            S.dma("sp", bga[:, 0:2], bass.AP(bga_d.tensor, bga_d.offset, [[1, 4], [4, 2]]), writes=[bga], dsem="d_c10")
            TS("dve", bga[:, 2:3], bga[:, 1:2], -1.0, None, ALU.mult, None, [bga], [bga])
            lbt = S.sb("lbt", [128, 2, 8], F32, es=p0)
            S.dma("sp", lbt[:, :, :], lbl_d.rearrange("l (h d) -> d l h", d=128), writes=[lbt], dsem="d_c11")
            TT("dve", lbv[:, 1, :], lbt[:, 0, :], lbt[:, 1, :], ALU.subtract, [lbt], [lbv])
            ACT(lbv[:, 0, :], lbv[:, 1, :], AF.Sigmoid, [lbv], [lbv])
            TS("dve", lbv[:, 2, :], lbv[:, 0, :], -1.0, None, ALU.add, None, [lbv], [lbv])
            TS("dve", lbv[:, 1, :], lbv[:, 2, :], -1.0, None, ALU.mult, None, [lbv], [lbv])
            cw4 = S.sb("cw4", [4, DFF], F32, es=p0)
            S.dma("sp", cw4[0:3, :], cw_d[:, :], writes=[cw4], dsem="d_c12")
            S.dma("sp", cw4[3:4, :], cb_d[:, :], writes=[cw4], dsem="d_c12")
            for g0 in range(0, FT, 8):
                n = min(8, FT - g0)
                bk = nb()
                for q in range(n):
                    TR(bk, bk[:, q * 4:q * 4 + 4], cw4[0:4, (g0 + q) * 128:(g0 + q + 1) * 128], identf[0:4, 0:4], [cw4, identf])
                CP("dve", convw[:, g0:g0 + n, :], bk[:, 0:4 * n].rearrange("p (a b) -> p a b", b=4), [bk], [convw])
            sn64 = S.sb("sn64", [64, 128], F32, es=p0)
            S.dma("sp", sn64[:, :], sn_d[:, :], writes=[sn64], dsem="d_c14")
            bk = nb()
            TR(bk, bk[:, 0:64], sn64[0:64, :], identf[0:64, 0:64], [sn64, identf])
            CP("dve", nTs[:, :], bk[:, 0:64], [bk], [nTs])
            S.dma("sp", mprev_s[:, :], bass.AP(sm_d.tensor, sm_d.offset, [[1, 4], [4, 16]]), writes=[mprev_s], dsem="d_c15")

            c17 = S.sb("c17", [17, D], F32, es=p0)
            S.dma("sp", c17[:, :], c17_d[:, :], writes=[c17], dsem="d_c16")
            ACT(c17[:, :], c17[:, :], AF.Silu, [c17], [c17])
            for g0 in range(0, KT, 4):
                bk = nb()
                for q in range(4):
                    TR(bk, bk[:, q * 32:q * 32 + 17], c17[0:17, (g0 + q) * 128:(g0 + q + 1) * 128], identf[0:17, 0:17], [c17, identf])
                CP("dve", scT[:, g0:g0 + 4, :], bk[:, 0:128].rearrange("p (a b) -> p a b", b=32)[:, :, 0:17], [bk], [scT])
            gscr_tok = S.tok("gscr")

            def ada_job(ci):
                def fn(slot):
                    bk = nb()
                    for kt in range(KT):
                        MM(bk, bk[0:17, 0:512], scT[:, kt, :], slot[:, kt, :], kt == 0, kt == KT - 1, [scT, slot])
                    ba = bAt[ci % 2]
                    mc = modc[ci % 2]
                    S.dma("sp", ba[:, :], rowbc(b_ada[0:1, 512 * ci:512 * ci + 512], 512, 17), writes=[ba], dsem="d_ba%d" % (ci % 2))
                    TT("dve", mc[:, :], bk[0:17, 0:512], ba[:, :], ALU.add, [bk, ba], [mc])
                    kind, sub = ci // 4, ci % 4
                    if kind in (2, 5):
                        gi_ = 0 if kind == 2 else 1
                        S.dma("sp", gscr[gi_, :, 512 * sub:512 * sub + 512], mc[:, :], reads=[mc], writes=[P(gscr_tok)], dsem="d_gs%d" % (ci % 2))
                        for j4 in range(4):
                            dst = bass.AP(gscr_s.tensor, gscr_s[gi_, j4:j4 + 1, 512 * sub:512 * sub + 1].offset, [[4 * D, 16], [1, 512]])
                            S.dma("sp", dst, mc[1:17, :], reads=[mc], writes=[P(gscr_tok)], dsem="d_gs%d" % (ci % 2))
                    else:
                        kk = {0: 0, 1: 1, 3: 2, 4: 3}[kind]
                        b2 = nb()
                        for q in range(4):
                            TR(b2, b2[:, q * 32:q * 32 + 17], mc[0:17, q * 128:(q + 1) * 128], identf[0:17, 0:17], [mc, identf])
                        src = b2[:, 0:128].rearrange("p (a b) -> p a b", b=32)[:, :, 0:17]
                        if kind in (1, 4):
                            TS("dve", modT[:, kk, 4 * sub:4 * sub + 4, :], src, 1.0, None, ALU.add, None, [b2], [P(modT)])
                        else:
                            CP("dve", modT[:, kk, 4 * sub:4 * sub + 4, :], src, [b2], [P(modT)])
                return fn
            stream([(w_ada_v, 0, KT, 512 * ci, 512, ada_job(ci)) for ci in range(8)], nxt=W1(w_in_v, 0, KT, O_KA, 512))
            ada_late = [(w_ada_v, 0, KT, 512 * ci, 512, ada_job(ci)) for ci in range(8, 24)]
            S.barrier()

        def ln_stats(xt, np_, xbuf):
            k = lnstate["i"] % 2
            lnstate["i"] += 1
            st, mv = stat[k], mvv[k]
            for c in range(4):
                S.op("dve", lambda: V.bn_stats(out=st[0:np_, c, :], in_=xt[0:np_, c * 512:(c + 1) * 512]), [xbuf], [st])
            S.op("dve", lambda: V.bn_aggr(out=mv[0:np_, 0:2], in_=st[0:np_, :, :]), [st], [mv])
            ACT(mv[0:np_, 2:3], mv[0:np_, 1:2], AF.Sqrt, [mv], [mv], bias=EPS, scale=1.0)
            RECIP(mv[0:np_, 3:4], mv[0:np_, 2:3], [mv], [mv])
            TS("dve", mv[0:np_, 4:5], mv[0:np_, 0:1], mv[0:np_, 3:4], -1.0, ALU.mult, ALU.mult, [mv], [mv])
            return mv[0:np_, 3:4], mv[0:np_, 4:5], mv

        def ln_all(xa, xab, tl, lng, lnb, after):
            n = len(tl)
            for ti, t in enumerate(tl):
                np_ = t["np"]
                for c in range(4):
                    S.op("dve", lambda: V.bn_stats(out=statA[0:np_, ti, c, :], in_=xa[0:np_, ti, c * 512:(c + 1) * 512]), [xab[ti]], [P(statA)])
                S.op("dve", lambda: V.bn_aggr(out=mvA[0:np_, ti, :], in_=statA[0:np_, ti, :, :]), [statA], [P(mvA)])
            ACT(rsA[:, 0, 0:n], mvA[:, 0:n, 1], AF.Sqrt, [mvA], [rsA], bias=EPS, scale=1.0)
            RECIP(rsA[:, 1, 0:n], rsA[:, 0, 0:n], [rsA], [rsA])
            STT(rsA[:, 2, 0:n], mvA[:, 0:n, 0], -1.0, rsA[:, 1, 0:n], ALU.mult, ALU.mult, [mvA, rsA], [rsA])
            for ti, t in enumerate(tl):
                np_ = t["np"]
                ACT(xa[0:np_, ti, :], xa[0:np_, ti, :], AF.Identity, [xab[ti], rsA], [xab[ti]], bias=rsA[0:np_, 2, ti:ti + 1], scale=rsA[0:np_, 1, ti:ti + 1])
            for ti, t in enumerate(tl):
                np_ = t["np"]
                TT("dve", xa[0:np_, ti, :], xa[0:np_, ti, :], lng[0:np_, :], ALU.mult, [xab[ti], lng], [xab[ti]])
            for ti, t in enumerate(tl):
                np_ = t["np"]
                TT("dve", xa[0:np_, ti, :], xa[0:np_, ti, :], lnb[0:np_, :], ALU.add, [xab[ti], lnb], [xab[ti]])
                after(ti, t)

        def to_featT(xn, xnbuf, np_, dstT, dstbuf, col, kk_sh, kk_sc, sample):
            for g0 in range(0, KT, 4):
                bk = nb()
                bv = bfv(bk)
                for q in range(4):
                    ft = g0 + q
                    TR(bk, bv[:, q * 128:q * 128 + np_], xn[0:np_, ft * 128:(ft + 1) * 128], identb[0:np_, 0:np_], [xnbuf, identb])
                for q in range(4):
                    ft = g0 + q
                    src = bv[:, q * 128:q * 128 + np_]
                    dst = dstT[:, ft, col:col + np_]
                    if not sample:
                        if (g0 // 4) % 2 == 0:
                            ACT(dst, src, AF.Identity, [bk, modT], [P(dstbuf)], bias=modT[:, kk_sh, ft, 0:1], scale=modT[:, kk_sc, ft, 0:1])
                        else:
                            TS("dve", dst, src, modT[:, kk_sc, ft, 0:1], modT[:, kk_sh, ft, 0:1], ALU.mult, ALU.add, [bk, modT], [P(dstbuf)])
                    else:
                        scv = bc(modT[:, kk_sc, ft, 1:17].unsqueeze(2), [128, 16, 4])
                        shv = bc(modT[:, kk_sh, ft, 1:17].unsqueeze(2), [128, 16, 4])
                        TT("dve", dst.rearrange("p (a b) -> p a b", b=4), src.rearrange("p (a b) -> p a b", b=4), scv, ALU.mult, [bk, modT], [P(dstbuf)])
                        TT("dve", dst.rearrange("p (a b) -> p a b", b=4), dst.rearrange("p (a b) -> p a b", b=4), shv, ALU.add, [dstbuf, modT], [P(dstbuf)])

        def plainT(src, srcbuf, np_, nft, dstT, dstbuf, ft0, col):
            for g0 in range(0, nft, 4):
                n = min(4, nft - g0)
                bk = nb()
                bv = bfv(bk)
                for q in range(n):
                    TR(bk, bv[:, q * 128:q * 128 + np_], src[0:np_, (g0 + q) * 128:(g0 + q + 1) * 128], identb[0:np_, 0:np_], [srcbuf, identb])
                if np_ == 128:
                    CP("act", dstT[:, ft0 + g0:ft0 + g0 + n, col:col + 128], bv[:, 0:128 * n].rearrange("p (a b) -> p a b", b=128), [bk], [P(dstbuf)])
                else:
                    CP("act", dstT[:, ft0 + g0:ft0 + g0 + n, col:col + np_], bv[:, 0:128 * n].rearrange("p (a b) -> p a b", b=128)[:, :, 0:np_], [bk], [P(dstbuf)])

        def mk_tiles(kind):
            if kind == "P0":
                tl = [dict(k="pre", np=128, col=128 * j, src=xpre[128 * j:128 * j + 128, :], j=j) for j in range(4)]
                return tl, tl, 512, [(0, 512)]
            if kind == "P1":
                tl = [dict(k="pre", np=128, col=128 * j, src=xpre[512 + 128 * j:512 + 128 * j + 128, :], j=j) for j in range(3)]
                return tl, tl, 384, [(0, 384)]
            if kind == "M0":
                tl = [dict(k="main", np=128, col=128 * j, src=xp[128 * j:128 * j + 128, :], j=j, orow=128 * j, srow=128 * j) for j in range(4)]
                ex = dict(k="extra", np=128, col=512, src=xpre[896:1024, :], j=4, srow=1024)
                return tl + [ex], [ex] + tl, 640, [(0, 512), (512, 128)]
            if kind == "M1":
                tl = [dict(k="main", np=128, col=128 * j, src=xp[512 + 128 * j:512 + 128 * j + 128, :], j=j, orow=512 + 128 * j, srow=512 + 128 * j) for j in range(4)]
                sa = dict(k="sample", np=64, col=512, src=xs[:, :], j=4, srow=1152)
                return tl + [sa], tl + [sa], 576, [(0, 512), (512, 64)]

        def interleave(jobs, extras):
            out = []
            for j in jobs:
                out.append(j)
                if extras:
                    out.append(extras.pop(0))
            out.extend(extras)
            return out

        def mixer_phase(kind, yT, yTb, bes, extra=()):
            tiles, logical, TB, ranges = mk_tiles(kind)
            full = kind in ("M0", "M1")
            has_sample = kind == "M1"
            NT = len(tiles)
            if not full:
                segs = [(0, TB, TB, 0, 1)]
                NCm = 1
            elif kind == "M0":
                segs = [(0, 640, 128, 0, 5)]
                NCm = 5
            else:
                segs = [(0, 512, 128, 0, 4), (512, 576, 4, 4, 16)]
                NCm = 20
            rm = rmt
            if full:
                crm, wrm = (c_rmB, 576) if kind == "M1" else (c_rmA, 640)
                S.dma("sp", rmt[:, 0, 0:wrm], rowbc(crm[0:1, :], wrm), writes=[rmt], dsem="d_rm")
                S.dma("sp", rmt[:, 1, 0:wrm], rowbc(crm[1:2, :], wrm), writes=[rmt], dsem="d_rm")

            def d0(parts, k):
                if full:
                    return rm[0:parts, k, 0:TB]
                return bc(cst2[0:parts, k:k + 1], [parts, TB])

            def cidx(t):
                return 0 if not full else t["j"]

            with contextlib.ExitStack() as ph:
                hT = S.sb("hT", [128, KT, TB], BF16, es=ph)
                hTb = [S.tok("hT%d" % j) for j in range(NT)]
                with contextlib.ExitStack() as p1:
                    p1es = p1 if full else ph
                    xt2 = [S.sb("xt%d" % i, [128, D], F32, es=p1es) for i in range(2)]
                    xn2 = [S.sb("xn%d" % i, [128, D], BF16, es=p1es) for i in range(2)]
                    def st_a(ti):
                        t = tiles[ti]
                        xt, xn = xt2[ti % 2], xn2[ti % 2]
                        np_ = t["np"]
                        S.dma("sp", xt[0:np_, :], t["src"], writes=[xt], dsem="d_xt%d" % (ti % 2))
                        rstd, nmr, mvb = ln_stats(xt, np_, xt)
                        ACT(xn[0:np_, :], xt[0:np_, :], AF.Identity, [xt, mvb], [xn], bias=nmr, scale=rstd)

                    def st_b(ti):
                        t = tiles[ti]
                        xn = xn2[ti % 2]
                        to_featT(xn, xn, t["np"], hT, hTb[ti], t["col"], 0, 1, t["k"] == "sample")
                    st_a(0)
                    for ti in range(NT):
                        if ti + 1 < NT:
                            st_a(ti + 1)
                        st_b(ti)
                    if full:
                        S.barrier()

                def hreads(c0, n):
                    return [hTb[ti] for ti, t in enumerate(tiles) if t["col"] < c0 + n and t["col"] + t["np"] > c0]

                def fm_group(slot, cs, M, fn):
                    for (c0, n) in ranges:
                        bk = nb()
                        rd = hreads(c0, n) + [slot]
                        for kt in range(KT):
                            MM(bk, bk[0:M, 0:n], slot[:, kt, cs], hT[:, kt, c0:c0 + n], kt == 0, kt == KT - 1, rd)
                        fn(bk, c0, n)

                def tm_group(slot, ncols, fn):
                    for ti, t in enumerate(tiles):
                        bk = nb()
                        np_ = t["np"]
                        for kt in range(KT):
                            MM(bk, bk[0:np_, 0:ncols], hT[:, kt, t["col"]:t["col"] + np_], slot[:, kt, 0:ncols], kt == 0, kt == KT - 1, [hTb[ti], slot])
                        fn(bk, ti, t)

                def cview(ap2d, lo, hi, L):
                    return ap2d[:, lo:hi].rearrange("p (c l) -> p c l", l=L)

                with contextlib.ExitStack() as ga:
                    qaT = S.sb("qaT", [128, 4, TB], BF16, es=ga) if full else None
                    kaT = S.sb("kaT", [128, 4, TB], BF16, es=ga)
                    ka_tok = S.sb("ka_tok", [128, NT, 512], BF16, es=ga)
                    va = S.sb("va", [128, NT, 4, 257], BF16, es=ga)
                    oga = S.sb("oga", [128, NT, 1024], BF16, es=ga) if full else None
                    na_bc = S.sb("na_bc", [128, 1024], F32, es=ga) if full else None
                    T1 = S.sb("gT1", [4, TB], F32, es=ga)
                    T2 = S.sb("gT2", [4, TB], F32, es=ga)
                    T3 = S.sb("gT3", [4, TB], F32, es=ga)
                    T4 = S.sb("gT4", [4, TB], F32, es=ga)
                    pc = S.sb("pc", [4, 8, 24], F32, es=ga)
                    e1d = S.sb("e1d", [4, 4, 24], F32, es=ga)
                    e1bc = S.sb("e1bc", [128, 4, 24], F32, es=ga)
                    tsc = S.sb("tsc", [128, NT, 12], F32, es=ga)
                    kw = [S.sb("kw%d" % i, [128, 128], BF16, es=ga) for i in range(8)]
                    STsb = [S.sb("STsb%d" % i, [128, 128], BF16, es=ga) for i in range(4)] if full else None
                    sgt = [S.sb("sgt%d" % i, [128, 512], F32, es=ga) for i in range(2)] if full else None
                    ytl = [S.sb("ytl%d" % i, [128, 1024], BF16, es=ga) for i in range(2)] if full else None
                    junk = S.sb("junk", [128, 256], F32, es=ga) if full else None
                    nsb = [[S.sb("nsb%d_%d" % (i, h), [128, 257], F32, es=ga) for h in range(4)] for i in range(2)] if full else None
                    if has_sample:
                        GSA = 4
                        CsS = [S.sb("CsS%d" % i, [128, GSA, 257], F32, es=ga) for i in range(3)]
                        CsB = [S.sb("CsB%d" % i, [128, GSA, 257], BF16, es=ga) for i in range(3)]
                        kwm = [S.sb("kwm%d" % i, [64, 16, 128], BF16, es=ga) for i in range(2)]
                        qps = [S.sb("qps%d" % i, [128, 16, 64], BF16, es=ga) for i in range(2)]
                        kwS = [S.sb("kwS%d" % h, [64, 128], BF16, es=ga) for h in range(4)]
                        STS = [S.sb("STS%d" % h, [64, 64], BF16, es=ga) for h in range(4)]
                    MSET("dve", va[:, :, :, 256:257], 1.0, [va])
                    MSET("dve", e1d[:, :, :], 0.0, [e1d])
                    MSET("dve", pc[:, :, :], 0.0, [pc])
                    if full:
                        S.dma("sp", na_bc[:, :], rowbc(norm_a_d[0:1, :], 1024), writes=[na_bc], dsem="d_nabc")
                    S.dma("pool", wg[:, :, :], w_in_v[:, :, O_IA:O_IA + 8], writes=[wg], dsem="d_wg")

                    jobs = []

                    def job_qa(slot):
                        for h in range(4):
                            fm_group(slot, slice(h * 128, (h + 1) * 128), 128,
                                     lambda bk, c0, n, h=h: ACT(qaT[:, h, c0:c0 + n], bk[:, 0:n], AF.Copy, [bk], [P(qaT)], scale=128.0 ** -0.5))

                    def job_ka(slot):
                        for h in range(4):
                            fm_group(slot, slice(h * 128, (h + 1) * 128), 128,
                                     lambda bk, c0, n, h=h: CP("dve", kaT[:, h, c0:c0 + n], bk[:, 0:n], [bk], [P(kaT)]))
                        for ti, t in enumerate(tiles):
                            np_ = t["np"]
                            bk = nb()
                            bv = bfv(bk)
                            for h in range(4):
                                TR(bk, bv[0:np_, h * 128:(h + 1) * 128], kaT[:, h, t["col"]:t["col"] + np_], identb[:, :], [kaT, identb])
                            CP("act", ka_tok[0:np_, ti, :], bv[0:np_, 0:512], [bk], [P(ka_tok)])

                    def job_va(c):
                        def fn(slot):
                            def ev(bk, ti, t):
                                np_ = t["np"]
                                CP("act" if ti % 2 else "dve", va[0:np_, ti, 2 * c:2 * c + 2, 0:256], bk[0:np_, 0:512].rearrange("p (a b) -> p a b", b=256), [bk], [P(va)])
                            tm_group(slot, 512, ev)
                        return fn

                    def job_oa(c):
                        def fn(slot):
                            def ev(bk, ti, t):
                                np_ = t["np"]
                                sg = sgt[ti % 2]
                                ACT(sg[0:np_, :], bk[0:np_, 0:512], AF.Sigmoid, [bk], [sg])
                                TT("dve", oga[0:np_, ti, 512 * c:512 * c + 512], sg[0:np_, :], na_bc[0:np_, 512 * c:512 * c + 512], ALU.mult, [sg, na_bc], [P(oga)])
                            tm_group(slot, 512, ev)
                        return fn
                    if full:
                        jobs.append((w_in_v, 0, KT, O_QA, 512, job_qa))
                    jobs.append((w_in_v, 0, KT, O_KA, 512, job_ka))
                    jobs.append((w_in_v, 0, KT, O_VA, 512, job_va(0)))
                    jobs.append((w_in_v, 0, KT, O_VA + 512, 512, job_va(1)))
                    if full:
                        jobs.append((w_in_v, 0, KT, O_OA, 512, job_oa(0)))
                        jobs.append((w_in_v, 0, KT, O_OA + 512, 512, job_oa(1)))
                    stream(interleave(jobs, list(extra[0:4])), nxt=W1(w_in_v, 0, KT, O_FB, 512))

                    fm_group(wg, slice(0, 4), 4, lambda bk, c0, n: ACT(T1[:, c0:c0 + n], bk[0:4, 0:n], AF.Identity, [bk, bga], [P(T1)], bias=bga[:, 0:1], scale=1.0))
                    fm_group(wg, slice(4, 8), 4, lambda bk, c0, n: ACT(T2[:, c0:c0 + n], bk[0:4, 0:n], AF.Exp, [bk, bga], [P(T2)], bias=bga[:, 2:3], scale=-1.0))
                    ACT(T2[:, :], T2[:, :], AF.Ln, [T2], [T2], bias=1.0, scale=1.0)
                    SCAN(T3[:, :], d0(4, 0), T2[:, :], 0.0, ALU.mult, ALU.add, [rm, cst2, T2], [T3])
                    TT("dve", T1[:, :], T1[:, :], T3[:, :], ALU.add, [T1, T3], [T1])
                    SCAN(T2[:, :], d0(4, 1), T1[:, :], NEG, ALU.add, ALU.max, [rm, cst2, T1], [T2])
                    for (lo, hi, L, c0_, n_) in segs:
                        CP("dve", pc[:, 0, c0_:c0_ + n_], cview(T2, lo, hi, L)[:, :, L - 1], [T2], [pc])
                        CP("dve", pc[:, 1, c0_:c0_ + n_], cview(T3, lo, hi, L)[:, :, L - 1], [T3], [pc])

                    def mscan(a, b_):
                        SCAN(pc[:, 3, a:b_], pc[:, 0, a:b_], pc[:, 1, a:b_], mcar[:, 0:1], ALU.max, ALU.subtract, [pc, mcar], [pc])
                        CP("dve", pc[:, 2, a:a + 1], mcar[:, 0:1], [mcar], [pc])
                        if b_ - a > 1:
                            CP("dve", pc[:, 2, a + 1:b_], pc[:, 3, a:b_ - 1], [pc], [pc])
                        CP("dve", mcar[:, 0:1], pc[:, 3, b_ - 1:b_], [pc], [mcar])
                    if kind == "M0":
                        mscan(4, 5)
                        TS("dve", mcar[:, 0:1], mcar[:, 0:1], flag[0:4, 0:1], None, ALU.mult, None, [mcar, flag], [mcar])
                        mscan(0, 4)
                    elif kind == "M1":
                        mscan(0, 4)
                    else:
                        mscan(0, 1)
                    if has_sample:
                        CP("dve", pc[:, 2, 4:20], mprev_s[:, :], [mprev_s], [pc])
                        TT("dve", pc[:, 3, 4:20], pc[:, 0, 4:20], pc[:, 2, 4:20], ALU.max, [pc], [pc])
                        TT("dve", pc[:, 3, 4:20], pc[:, 3, 4:20], pc[:, 1, 4:20], ALU.subtract, [pc], [pc])
                        S.dma("sp", bass.AP(ms_o.tensor, ms_o.offset, [[1, 4], [4, 16]]), pc[:, 3, 4:20], reads=[pc], dsem="d_ms", is_output=True)
                    TT("dve", pc[:, 4, 0:NCm], pc[:, 0, 0:NCm], pc[:, 2, 0:NCm], ALU.max, [pc], [pc])
                    TT("dve", pc[:, 5, 0:NCm], pc[:, 2, 0:NCm], pc[:, 4, 0:NCm], ALU.subtract, [pc], [pc])
                    ACT(pc[:, 5, 0:NCm], pc[:, 5, 0:NCm], AF.Exp, [pc], [pc])

                    def pcb(k, a, n_, L):
                        return bc(pc[:, k, a:a + n_].unsqueeze(2), [4, n_, L])
                    for (lo, hi, L, a, n_) in segs:
                        TT("dve", cview(T4, lo, hi, L), cview(T2, lo, hi, L), pcb(2, a, n_, L), ALU.max, [T2, pc], [T4])
                        TT("dve", cview(T1, lo, hi, L), cview(T1, lo, hi, L), pcb(4, a, n_, L), ALU.subtract, [T1, pc], [T1])
                        TT("dve", cview(T3, lo, hi, L), cview(T3, lo, hi, L), cview(T4, lo, hi, L), ALU.subtract, [T3, T4], [T3])
                        TT("dve", cview(T4, lo, hi, L), pcb(4, a, n_, L), cview(T4, lo, hi, L), ALU.subtract, [T4, pc], [T4])
                    ACT(T1[:, :], T1[:, :], AF.Exp, [T1], [T1])
                    if full:
                        ACT(T3[:, :], T3[:, :], AF.Exp, [T3], [T3])
                        ACT(T4[:, :], T4[:, :], AF.Exp, [T4], [T4])
                    for ti, t in enumerate(tiles):
                        np_ = t["np"]
                        bk = nb()
                        lst = (T1, T4, T3) if full else (T1,)
                        for qi, Tt in enumerate(lst):
                            TR(bk, bk[0:np_, 4 * qi:4 * qi + 4], Tt[0:4, t["col"]:t["col"] + np_], identf[0:4, 0:4], [Tt, identf])
                        CP("dve", tsc[0:np_, ti, 0:4 * len(lst)], bk[0:np_, 0:4 * len(lst)], [bk], [P(tsc)])
                    TT("dve", e1d[:, :, 0:NCm], bc(pc[:, 5, 0:NCm].unsqueeze(1), [4, 4, NCm]), diag4[:, :, 0:NCm], ALU.mult, [pc, diag4], [e1d])
                    bk = nb()
                    MM(bk, bk[:, 0:96], ones4[:, :], e1d[:, :, :].rearrange("p a b -> p (a b)"), True, True, [ones4, e1d])
                    CP("dve", e1bc[:, :, :], bk[:, 0:96].rearrange("p (a b) -> p a b", b=24), [bk], [e1bc])

                    if not full:
                        accb = [banks[i] for i in range(4)]
                        bstate["excl"] = {0, 1, 2, 3}
                        for ti, t in enumerate(tiles):
                            for h in range(4):
                                kwb = kw[(ti * 4 + h) % 8]
                                ACT(kwb[:, :], ka_tok[:, ti, h * 128:(h + 1) * 128], AF.Identity, [ka_tok, tsc], [kwb], scale=tsc[:, ti, h:h + 1])
                                MM(accb[h], accb[h][:, 0:257], kwb[:, :], va[:, ti, h, :], ti == 0, ti == NT - 1, [kwb, va])
                        for h in range(4):
                            STT(Cst[h][:, :], Cst[h][:, :], e1bc[:, h, 0:1], accb[h][:, 0:257], ALU.mult, ALU.add, [Cst[h], e1bc, accb[h]], [Cst[h]])
                        bstate["excl"] = set()
                    else:
                        for t in logical:
                            ti = t["j"]
                            np_ = t["np"]
                            col = t["col"]
                            yt = ytl[ti % 2]
                            if t["k"] != "sample":
                                ch = ti
                                bS, bN, bU = [None] * 4, [None] * 4, [None] * 4
                                for h in range(4):
                                    TS("dve", Cst[h][:, :], Cst[h][:, :], e1bc[:, h, ch:ch + 1], None, ALU.mult, None, [Cst[h], e1bc], [Cst[h]])
                                    CP("act", Cbf[h][:, :], Cst[h][:, :], [Cst[h]], [Cbf[h]])
                                for h in range(4):
                                    kwb = kw[(ti * 4 + h) % 8]
                                    ACT(kwb[:, :], ka_tok[:, ti, h * 128:(h + 1) * 128], AF.Identity, [ka_tok, tsc], [kwb], scale=tsc[:, ti, h:h + 1])
                                    bS[h] = nb()
                                    MM(bS[h], bS[h][:, 0:128], kaT[:, h, col:col + 128], qaT[:, h, col:col + 128], True, True, [kaT, qaT])
                                for h in range(4):
                                    STT(STsb[h][:, :], bS[h][:, 0:128], tsc[:, ti, h:h + 1], maskp[:, :], ALU.mult, ALU.mult, [bS[h], tsc, maskp], [STsb[h]])
                                for h in range(4):
                                    kwb = kw[(ti * 4 + h) % 8]
                                    bN[h] = nb()
                                    MM(bN[h], bN[h][:, 0:257], STsb[h][:, :], va[:, ti, h, :], True, False, [STsb[h], va])
                                    MM(bN[h], bN[h][:, 0:257], qaT[:, h, col:col + 128], Cbf[h][:, :], False, True, [qaT, Cbf[h]])
                                    bU[h] = nb()
                                    MM(bU[h], bU[h][:, 0:257], kwb[:, :], va[:, ti, h, :], True, True, [kwb, va])
                                for h in range(4):
                                    TT("dve", Cst[h][:, :], Cst[h][:, :], bU[h][:, 0:257], ALU.add, [Cst[h], bU[h]], [Cst[h]])
                                    if t["k"] == "extra":
                                        TS("dve", Cst[h][:, :], Cst[h][:, :], flag[:, 0:1], None, ALU.mult, None, [Cst[h], flag], [Cst[h]])
                                for h in range(4):
                                    CP("act", nsb[ti % 2][h][:, :], bN[h][:, 0:257], [bN[h]], [nsb[ti % 2][h]])
                                a_out4(nsb[ti % 2], np_, ti, tsc, oga, yt, junk)
                                plainT(yt, yt, np_, 8, yT, yTb[ti], 0, col)
                            else:
                                accb = [banks[i] for i in range(4)]
                                bstate["excl"] = {0, 1, 2, 3}
                                for h in range(4):
                                    ACT(kwS[h][:, :], ka_tok[0:64, ti, h * 128:(h + 1) * 128], AF.Identity, [ka_tok, tsc], [kwS[h]], scale=tsc[0:64, ti, h:h + 1])
                                    bS = nb()
                                    MM(bS, bS[0:64, 0:64], kaT[:, h, col:col + 64], qaT[:, h, col:col + 64], True, True, [kaT, qaT])
                                    STT(STS[h][:, :], bS[0:64, 0:64], tsc[0:64, ti, h:h + 1], masks[:, :], ALU.mult, ALU.mult, [bS, tsc, masks], [STS[h]])
                                    MM(accb[h], accb[h][0:64, 0:257], STS[h][:, :], va[0:64, ti, h, :], True, False, [STS[h], va])
                                nTs3 = nTs[:, :].rearrange("p (s h) -> p s h", h=4)
                                nTo3 = nTo[:, :].rearrange("p (s h) -> p s h", h=4)
                                itemsA = [(h, g) for h in range(4) for g in range(NSEQ // GSA)]

                                def loadA(k):
                                    h, g = itemsA[k]
                                    cs = CsS[k % 3]
                                    s0 = g * GSA
                                    S.dma("sp", cs[:, :, 0:256], sC_d[s0:s0 + GSA, h].rearrange("s d e -> d s e"), writes=[cs], dsem="d_cs%d" % (k % 3))

                                def prepA(k):
                                    h, g = itemsA[k]
                                    cs, cb_ = CsS[k % 3], CsB[k % 3]
                                    s0 = g * GSA
                                    kwm_h, qps_h = kwm[h % 2], qps[h % 2]
                                    if g == 0:
                                        TT("dve", kwm_h[:, :, :], bc(kwS[h][:, :].unsqueeze(1), [64, 16, 128]), bc(bmaskb[:, :].unsqueeze(2), [64, 16, 128]), ALU.mult, [kwS[h], bmaskb], [kwm_h])
                                        TT("dve", qps_h[:, :, :], bc(qaT[:, h, col:col + 64].unsqueeze(1), [128, 16, 64]), cm16[:, :, :], ALU.mult, [qaT, cm16], [qps_h])
                                    CP("act", cs[:, :, 256], nTs3[:, s0:s0 + GSA, h], [nTs], [cs])
                                    TT("dve", cs[:, :, :], cs[:, :, :], bc(e1bc[:, h, 4 + s0:4 + s0 + GSA].unsqueeze(2), [128, GSA, 257]), ALU.mult, [cs, e1bc], [cs])
                                    CP("act", cb_[:, :, :], cs[:, :, :], [cs], [cb_])

                                def compA(k):
                                    h, g = itemsA[k]
                                    cs, cb_ = CsS[k % 3], CsB[k % 3]
                                    s0 = g * GSA
                                    kwm_h, qps_h = kwm[h % 2], qps[h % 2]
                                    for i in range(GSA):
                                        sq = s0 + i
                                        MM(accb[h], accb[h][0:64, 0:257], qps_h[:, sq, :], cb_[:, i, :], False, sq == NSEQ - 1, [qps_h, cb_])
                                        bU = nb()
                                        MM(bU, bU[:, 0:257], kwm_h[:, sq, :], va[0:64, ti, h, :], True, True, [kwm_h, va])
                                        TT("dve", cs[:, i, :], cs[:, i, :], bU[:, 0:257], ALU.add, [cs, bU], [P(cs)])
                                    CP("act", nTo3[:, s0:s0 + GSA, h], cs[:, :, 256], [cs], [P(nTo)])
                                    S.dma("sp", Cs_o[s0:s0 + GSA, h].rearrange("s d e -> d s e"), cs[:, :, 0:256], reads=[cs], dsem="d_cso%d" % (k % 3), is_output=True)
                                nA = len(itemsA)
                                loadA(0)
                                loadA(1)
                                prepA(0)
                                for k in range(nA):
                                    if k + 2 < nA:
                                        loadA(k + 2)
                                    if k + 1 < nA:
                                        prepA(k + 1)
                                    compA(k)
                                a_out4(accb, np_, ti, tsc, oga, yt, junk)
                                bstate["excl"] = set()
                                plainT(yt, yt, np_, 8, yT, yTb[ti], 0, col)
                    S.barrier()

                with contextlib.ExitStack() as gb:
                    gbc = {}

                    def GS(name, shape, dt=F32):
                        if name not in gbc:
                            gbc[name] = S.sb(name, shape, dt, es=gb)
                        return gbc[name]
                    for hg in range(2):
                        SG4 = GS("SG4", [128, 4, TB], F32)
                        F4 = GS("F4", [128, 4, TB], F32)
                        B4 = GS("B4", [128, 4, TB], F32)
                        kinvT = GS("kinvT", [128, 4, TB], BF16)
                        qeT = GS("qeT", [128, 4, TB], BF16) if full else None
                        kinv_tok = GS("kinv_tok", [128, NT, 512], BF16)
                        vb = GS("vb", [128, NT, 512], BF16)
                        ogb = GS("ogb", [128, NT, 512], BF16) if full else None
                        nb_bc = GS("nb_bc", [128, 512], F32) if full else None
                        pcB = GS("pcB", [128, 4, 5, 24], F32)
                        ATsb = [GS("ATsb%d" % i, [128, 128], BF16) for i in range(4)] if full else None
                        sgt = [GS("sgtb%d" % i, [128, 512], F32) for i in range(2)] if full else None
                        ytl = [GS("ytlb%d" % i, [128, 512], BF16) for i in range(2)] if full else None
                        junk = GS("junkb", [128, 128], F32) if full else None
                        osb = [[GS("osb%d_%d" % (i, h), [128, 128], F32) for h in range(4)] for i in range(2)] if full else None
                        if has_sample:
                            GSB = 8
                            SsS = [GS("SsS%d" % i, [128, GSB, 128], F32) for i in range(3)]
                            SsB = [GS("SsB%d" % i, [128, GSB, 128], BF16) for i in range(3)]
                            kim = [GS("kim%d" % i, [64, 16, 128], BF16) for i in range(2)]
                            qps = [GS("qpsb%d" % i, [128, 16, 64], BF16) for i in range(2)]
                            ATS = [GS("ATS%d" % h, [64, 64], BF16) for h in range(4)]
                        MSET("dve", pcB[:, :, :, :], 0.0, [pcB])
                        if full:
                            for i_ in range(4):
                                MSET("dve", ATsb[i_][:, :], 0.0, [ATsb[i_]])
                        if full:
                            S.dma("sp", nb_bc[:, :], rowbc(norm_b_d[0:1, 512 * hg:512 * hg + 512], 512), writes=[nb_bc], dsem="d_nbbc")

                        def job_fb(slot):
                            H0 = 4 * hg
                            for hh in range(4):
                                fm_group(slot, slice(hh * 128, (hh + 1) * 128), 128,
                                         lambda bk, c0, n, hh=hh: ACT(SG4[:, hh, c0:c0 + n], bk[:, 0:n], AF.Sigmoid, [bk], [P(SG4)]))
                            STT(SG4[:, :, :], SG4[:, :, :], -1.0, bc(lbv[:, 2, H0:H0 + 4].unsqueeze(2), [128, 4, TB]), ALU.add, ALU.mult, [SG4, lbv], [SG4])
                            ACT(F4[:, :, :], SG4[:, :, :], AF.Ln, [SG4], [F4], bias=1.0, scale=-1.0)
                            for hh in range(4):
                                SCAN(B4[:, hh, :], d0(128, 0), F4[:, hh, :], 0.0, ALU.mult, ALU.add, [rm, cst2, F4], [B4])
                            for (lo, hi, L, c0_, n_) in segs:
                                rpos = 63 if (full and L == 128) else L - 1
                                v4 = B4[:, :, lo:hi].rearrange("p h (c l) -> p h c l", l=L)
                                CP("dve", pcB[:, :, 0, c0_:c0_ + n_], v4[:, :, :, L - 1], [B4], [pcB])
                                CP("dve", pcB[:, :, 1, c0_:c0_ + n_], v4[:, :, :, rpos], [B4], [pcB])
                            ACT(pcB[:, :, 2, 0:NCm], pcB[:, :, 0, 0:NCm], AF.Exp, [pcB], [pcB])
                            TT("dve", pcB[:, :, 3, 0:NCm], pcB[:, :, 0, 0:NCm], pcB[:, :, 1, 0:NCm], ALU.subtract, [pcB], [pcB])
                            ACT(pcB[:, :, 3, 0:NCm], pcB[:, :, 3, 0:NCm], AF.Exp, [pcB], [pcB])
                            ACT(pcB[:, :, 4, 0:NCm], pcB[:, :, 1, 0:NCm], AF.Exp, [pcB], [pcB])
                            for hh in range(4):
                                for (lo, hi, L, c0_, n_) in segs:
                                    TT("dve", cview(B4[:, hh, :], lo, hi, L), cview(B4[:, hh, :], lo, hi, L), bc(pcB[:, hh, 1, c0_:c0_ + n_].unsqueeze(2), [128, n_, L]), ALU.subtract, [B4, pcB], [B4])
                            ACT(F4[:, :, :], B4[:, :, :], AF.Exp, [B4], [F4], scale=-1.0)
                            TT("dve", kinvT[:, :, :], SG4[:, :, :], F4[:, :, :], ALU.mult, [SG4, F4], [kinvT])
                            if full:
                                ACT(B4[:, :, :], B4[:, :, :], AF.Exp, [B4], [B4])

                        def kinv_transposes():
                            for ti, t in enumerate(tiles):
                                np_ = t["np"]
                                bk = nb()
                                bv = bfv(bk)
                                for hh in range(4):
                                    TR(bk, bv[0:np_, hh * 128:(hh + 1) * 128], kinvT[:, hh, t["col"]:t["col"] + np_], identb[:, :], [kinvT, identb])
                                CP("act", kinv_tok[0:np_, ti, :], bv[0:np_, 0:512], [bk], [P(kinv_tok)])

                        def job_qb(slot):
                            for hh in range(4):
                                fm_group(slot, slice(hh * 128, (hh + 1) * 128), 128,
                                         lambda bk, c0, n, hh=hh: TT("dve", qeT[:, hh, c0:c0 + n], bk[:, 0:n], B4[:, hh, c0:c0 + n], ALU.mult, [bk, B4], [P(qeT)]))

                        def job_vb(slot):
                            def ev(bk, ti, t):
                                np_ = t["np"]
                                CP("act" if ti % 2 else "dve", vb[0:np_, ti, :], bk[0:np_, 0:512], [bk], [P(vb)])
                            tm_group(slot, 512, ev)

                        def job_gb(slot):
                            def ev(bk, ti, t):
                                np_ = t["np"]
                                sg = sgt[ti % 2]
                                ACT(sg[0:np_, :], bk[0:np_, 0:512], AF.Silu, [bk], [sg])
                                TT("dve", ogb[0:np_, ti, :], sg[0:np_, :], nb_bc[0:np_, :], ALU.mult, [sg, nb_bc], [P(ogb)])
                            tm_group(slot, 512, ev)
                        jobs = [(w_in_v, 0, KT, O_FB + 512 * hg, 512, job_fb)]
                        jobs.append((w_in_v, 0, KT, O_VB + 512 * hg, 512, job_vb))
                        if full:
                            jobs.append((w_in_v, 0, KT, O_GB + 512 * hg, 512, job_gb))
                            jobs.append((w_in_v, 0, KT, O_QB + 512 * hg, 512, job_qb))
                        if hg == 0:
                            nx_ = W1(w_in_v, 0, KT, O_FB + 512, 512)
                        elif kind == "P0":
                            nx_ = W1(w_in_v, 0, KT, O_KA, 512)
                        elif kind == "P1":
                            nx_ = None
                        else:
                            nx_ = W1(w_out_v, 0, KT, 0, 512)
                        stream(interleave(jobs, list(extra[4 + 2 * hg:6 + 2 * hg])), nxt=nx_)
                        kinv_transposes()

                        if not full:
                            accb = [banks[i] for i in range(4)]
                            bstate["excl"] = {0, 1, 2, 3}
                            for ti, t in enumerate(tiles):
                                for hh in range(4):
                                    MM(accb[hh], accb[hh][:, 0:128], kinv_tok[:, ti, hh * 128:(hh + 1) * 128], vb[:, ti, hh * 128:(hh + 1) * 128], ti == 0, ti == NT - 1, [kinv_tok, vb])
                            for hh in range(4):
                                H = 4 * hg + hh
                                STT(Sst[H][:, :], Sst[H][:, :], pcB[:, hh, 2, 0:1], accb[hh][:, 0:128], ALU.mult, ALU.add, [Sst[H], pcB, accb[hh]], [Sst[H]])
                            bstate["excl"] = set()
                        else:
                            for t in logical:
                                ti = t["j"]
                                np_ = t["np"]
                                col = t["col"]
                                yt = ytl[ti % 2]
                                if t["k"] != "sample":
                                    ch = ti
                                    bA, bO, bU = [None] * 4, [None] * 4, [None] * 4
                                    for hh in range(4):
                                        H = 4 * hg + hh
                                        ACT(Sbf[H][:, :], Sst[H][:, :], AF.Identity, [Sst[H], pcB], [Sbf[H]], scale=pcB[:, hh, 4, ch:ch + 1])
                                    for hh in range(4):
                                        bA[hh] = nb()
                                        MM(bA[hh], bA[hh][0:64, 0:64], kinvT[:, hh, col:col + 64], qeT[:, hh, col:col + 64], True, True, [kinvT, qeT])
                                        MM(bA[hh], bA[hh][:, 64:128], kinvT[:, hh, col:col + 128], qeT[:, hh, col + 64:col + 128], True, True, [kinvT, qeT])
                                    for hh in range(4):
                                        S.op("dve", lambda: V.copy_predicated(ATsb[hh][0:64, 0:64], maskp[0:64, 0:64].bitcast(mybir.dt.uint32), bA[hh][0:64, 0:64]), [bA[hh], maskp], [P(ATsb[hh])])
                                        S.op("dve", lambda: V.copy_predicated(ATsb[hh][:, 64:128], maskp[:, 64:128].bitcast(mybir.dt.uint32), bA[hh][:, 64:128]), [bA[hh], maskp], [P(ATsb[hh])])
                                    for hh in range(4):
                                        H = 4 * hg + hh
                                        cs_ = slice(hh * 128, (hh + 1) * 128)
                                        bO[hh] = nb()
                                        MM(bO[hh], bO[hh][:, 0:128], ATsb[hh][:, :], vb[:, ti, cs_], True, False, [ATsb[hh], vb])
                                        MM(bO[hh], bO[hh][:, 0:128], qeT[:, hh, col:col + 128], Sbf[H][:, :], False, True, [qeT, Sbf[H]])
                                        bU[hh] = nb()
                                        MM(bU[hh], bU[hh][:, 0:128], kinv_tok[:, ti, cs_], vb[:, ti, cs_], True, True, [kinv_tok, vb])
                                    for hh in range(4):
                                        H = 4 * hg + hh
                                        TS("dve", Sst[H][:, :], Sst[H][:, :], pcB[:, hh, 2, ch:ch + 1], None, ALU.mult, None, [Sst[H], pcB], [Sst[H]])
                                        STT(Sst[H][:, :], bU[hh][:, 0:128], pcB[:, hh, 3, ch:ch + 1], Sst[H][:, :], ALU.mult, ALU.add, [bU[hh], pcB, Sst[H]], [Sst[H]])
                                        if t["k"] == "extra":
                                            TS("dve", Sst[H][:, :], Sst[H][:, :], flag[:, 0:1], None, ALU.mult, None, [Sst[H], flag], [Sst[H]])
                                    for hh in range(4):
                                        CP("act", osb[ti % 2][hh][:, :], bO[hh][:, 0:128], [bO[hh]], [osb[ti % 2][hh]])
                                    b_out4(osb[ti % 2], np_, ti, ogb, yt, junk)
                                    plainT(yt, yt, np_, 4, yT, yTb[ti], 8 + 4 * hg, col)
                                else:
                                    accb = [banks[i] for i in range(4)]
                                    bstate["excl"] = {0, 1, 2, 3}
                                    for hh in range(4):
                                        bA = nb()
                                        MM(bA, bA[0:64, 0:64], kinvT[:, hh, col:col + 64], qeT[:, hh, col:col + 64], True, True, [kinvT, qeT])
                                        TT("dve", ATS[hh][:, :], bA[0:64, 0:64], masks[:, :], ALU.mult, [bA, masks], [ATS[hh]])
                                        MM(accb[hh], accb[hh][0:64, 0:128], ATS[hh][:, :], vb[0:64, ti, hh * 128:(hh + 1) * 128], True, False, [ATS[hh], vb])
                                    itemsB = [(hh, g) for hh in range(4) for g in range(NSEQ // GSB)]

                                    def loadB(k):
                                        hh, g = itemsB[k]
                                        s0 = g * GSB
                                        ss = SsS[k % 3]
                                        S.dma("sp", ss[:, :, :], sS_d[s0:s0 + GSB, 4 * hg + hh].rearrange("s d e -> d s e"), writes=[ss], dsem="d_ss%d" % (k % 3))

                                    def prepB(k):
                                        hh, g = itemsB[k]
                                        s0 = g * GSB
                                        cs_ = slice(hh * 128, (hh + 1) * 128)
                                        kim_h, qps_h = kim[hh % 2], qps[hh % 2]
                                        ss, sb_ = SsS[k % 3], SsB[k % 3]
                                        if g == 0:
                                            TT("dve", kim_h[:, :, :], bc(kinv_tok[0:64, ti, cs_].unsqueeze(1), [64, 16, 128]), bc(bmaskb[:, :].unsqueeze(2), [64, 16, 128]), ALU.mult, [kinv_tok, bmaskb], [kim_h])
                                            TT("dve", qps_h[:, :, :], bc(qeT[:, hh, col:col + 64].unsqueeze(1), [128, 16, 64]), cm16[:, :, :], ALU.mult, [qeT, cm16], [qps_h])
                                        TT("dve", ss[:, :, :], ss[:, :, :], bc(pcB[:, hh, 2, 4 + s0:4 + s0 + GSB].unsqueeze(2), [128, GSB, 128]), ALU.mult, [ss, pcB], [ss])
                                        CP("act", sb_[:, :, :], ss[:, :, :], [ss], [sb_])

                                    def compB(k):
                                        hh, g = itemsB[k]
                                        H = 4 * hg + hh
                                        s0 = g * GSB
                                        cs_ = slice(hh * 128, (hh + 1) * 128)
                                        kim_h, qps_h = kim[hh % 2], qps[hh % 2]
                                        ss, sb_ = SsS[k % 3], SsB[k % 3]
                                        for i in range(GSB):
                                            sq = s0 + i
                                            MM(accb[hh], accb[hh][0:64, 0:128], qps_h[:, sq, :], sb_[:, i, :], False, sq == NSEQ - 1, [qps_h, sb_])
                                            bU = nb()
                                            MM(bU, bU[:, 0:128], kim_h[:, sq, :], vb[0:64, ti, cs_], True, True, [kim_h, vb])
                                            TT("dve", ss[:, i, :], ss[:, i, :], bU[:, 0:128], ALU.add, [ss, bU], [P(ss)])
                                        S.dma("sp", Ss_o[s0:s0 + GSB, H].rearrange("s d e -> d s e"), ss[:, :, :], reads=[ss], dsem="d_sso%d" % (k % 3), is_output=True)
                                    nB = len(itemsB)
                                    loadB(0)
                                    loadB(1)
                                    prepB(0)
                                    for k in range(nB):
                                        if k + 2 < nB:
                                            loadB(k + 2)
                                        if k + 1 < nB:
                                            prepB(k + 1)
                                        compB(k)
                                    b_out4(accb, np_, ti, ogb, yt, junk)
                                    bstate["excl"] = set()
                                    plainT(yt, yt, np_, 4, yT, yTb[ti], 8 + 4 * hg, col)
                        if hg == 1:
                            S.barrier()
            return tiles, logical, TB, ranges

        def a_out4(bN, np_, ti, tsc, oga, yt, junk):
            sm = nsm()
            r4 = tsc[0:np_, ti, 4:8]
            emt4 = tsc[0:np_, ti, 8:12]
            for h in range(4):
                TT("dve", sm[0:np_, h:h + 1], bN[h][0:np_, 256:257], tsc[0:np_, ti, 4 + h:5 + h], ALU.mult, [bN[h], tsc], [sm])
            STT(sm[0:np_, 4:8], sm[0:np_, 0:4], -1.0, sm[0:np_, 0:4], ALU.mult, ALU.max, [sm], [sm])
            TT("dve", sm[0:np_, 4:8], sm[0:np_, 4:8], emt4, ALU.max, [sm, tsc], [sm])
            RECIP(sm[0:np_, 0:4], sm[0:np_, 4:8], [sm], [sm])
            TT("dve", sm[0:np_, 4:8], sm[0:np_, 0:4], r4, ALU.mult, [sm, tsc], [sm])
            for h in range(4):
                ACT(junk[0:np_, 0:256], bN[h][0:np_, 0:256], AF.Square, [bN[h], sm], [junk, sm], scale=sm[0:np_, 4 + h:5 + h], accum=sm[0:np_, 8 + h:9 + h])
            ACT(sm[0:np_, 12:16], sm[0:np_, 8:12], AF.Sqrt, [sm], [sm], bias=EPS, scale=1.0 / 256.0)
            RECIP(sm[0:np_, 0:4], sm[0:np_, 12:16], [sm], [sm])
            TT("dve", sm[0:np_, 8:12], sm[0:np_, 0:4], sm[0:np_, 4:8], ALU.mult, [sm], [sm])
            for h in range(4):
                STT(yt[0:np_, h * 256:(h + 1) * 256], bN[h][0:np_, 0:256], sm[0:np_, 8 + h:9 + h], oga[0:np_, ti, h * 256:(h + 1) * 256], ALU.mult, ALU.mult, [bN[h], sm, oga], [P(yt)])

        def b_out4(bO, np_, ti, ogb, yt, junk):
            sm = nsm()
            for hh in range(4):
                ACT(junk[0:np_, 0:128], bO[hh][0:np_, 0:128], AF.Square, [bO[hh]], [junk, sm], accum=sm[0:np_, hh:hh + 1])
            ACT(sm[0:np_, 4:8], sm[0:np_, 0:4], AF.Sqrt, [sm], [sm], bias=EPS, scale=1.0 / 128.0)
            RECIP(sm[0:np_, 8:12], sm[0:np_, 4:8], [sm], [sm])
            for hh in range(4):
                STT(yt[0:np_, hh * 128:(hh + 1) * 128], bO[hh][0:np_, 0:128], sm[0:np_, 8 + hh:9 + hh], ogb[0:np_, ti, hh * 128:(hh + 1) * 128], ALU.mult, ALU.mult, [bO[hh], sm, ogb], [P(yt)])

        x1tok = S.tok("x1scr")

        def outproj_phase(kind, tiles, yT, yTb):
            with contextlib.ExitStack() as ph:
                NT = len(tiles)
                xa = S.sb("xa", [128, NT, D], F32, es=ph)
                xab = [S.tok("xa%d" % i) for i in range(NT)]
                lng = S.sb("lng", [128, D], F32, es=ph)
                lnb = S.sb("lnb", [128, D], F32, es=ph)
                gP = [S.sb("gP%d" % i, [128, 512], F32, es=ph) for i in range(2)]
                gS = [S.sb("gS%d" % i, [64, 512], F32, es=ph) for i in range(2)]
                tmp = [S.sb("tmpo%d" % i, [128, 512], F32, es=ph) for i in range(2)]
                S.dma("sp", lng[:, :], rowbc(ln1g_d[0:1, :], D), writes=[lng], dsem="d_lng")
                S.dma("sp", lnb[:, :], rowbc(ln1b_d[0:1, :], D), writes=[lnb], dsem="d_lnb")
                for ti, t in enumerate(tiles):
                    S.dma("sp", xa[0:t["np"], ti, :], t["src"], writes=[xab[ti]], dsem="d_xa%d" % ti)
                has_sample = any(t["k"] == "sample" for t in tiles)

                def job(c):
                    def fn(slot):
                        g = gP[c % 2]
                        S.dma("sp", g[:, :], rowbc(gscr[0, 0:1, 512 * c:512 * c + 512], 512), reads=[gscr_tok], writes=[g], dsem="d_gP%d" % (c % 2))
                        if has_sample:
                            g2 = gS[c % 2]
                            S.dma("sp", g2[:, :], gscr_s[0, :, 512 * c:512 * c + 512], reads=[gscr_tok], writes=[g2], dsem="d_gS%d" % (c % 2))
                        for ti, t in enumerate(tiles):
                            np_ = t["np"]
                            bk = nb()
                            for kt in range(KT):
                                MM(bk, bk[0:np_, 0:512], yT[:, kt, t["col"]:t["col"] + np_], slot[:, kt, :], kt == 0, kt == KT - 1, [yTb[ti], slot])
                            gg = g2 if t["k"] == "sample" else g
                            tp = tmp[ti % 2]
                            TT("dve", tp[0:np_, :], bk[0:np_, 0:512], gg[0:np_, :], ALU.mult, [bk, gg], [tp])
                            STT(xa[0:np_, ti, 512 * c:512 * c + 512], xa[0:np_, ti, 512 * c:512 * c + 512], ALPHA, tp[0:np_, :], ALU.mult, ALU.add, [xab[ti], tp], [xab[ti]])
                    return fn
                stream([(w_out_v, 0, KT, 512 * c, 512, job(c)) for c in range(4)], nxt=[(w_up_v, 0, KT, 0, 256, 0), (w_up_v, 0, KT, DFF, 256, 256)])
                def after1(ti, t):
                    np_ = t["np"]
                    S.dma("sp", x1scr[t["srow"]:t["srow"] + np_, :], xa[0:np_, ti, :], reads=[xab[ti]], writes=[P(x1tok)], dsem="d_x1o%d" % ti)
                ln_all(xa, xab, tiles, lng, lnb, after1)
                S.barrier()

        def ffn_phase(kind, tiles):
            main = [t for t in tiles if t["k"] == "main"]
            aux = [t for t in tiles if t["k"] != "main"][0]
            has_sample = aux["k"] == "sample"
            NS = 64 if has_sample else 0
            with contextlib.ExitStack() as ph:
                gT = S.sb("gT", [128, FT, 512 + NS], BF16, es=ph)
                with contextlib.ExitStack() as p5:
                    TBf = 512 + (64 if has_sample else 128)
                    h2T = S.sb("h2T", [128, KT, TBf], BF16, es=p5)
                    h2b = S.tok("h2T")
                    xt2 = [S.sb("xq%d" % i, [128, D], F32, es=p5) for i in range(2)]
                    xn2 = [S.sb("xqn%d" % i, [128, D], BF16, es=p5) for i in range(2)]
                    abuf = [S.sb("abuf%d" % i, [128, 514], F32, es=p5) for i in range(2)]
                    cbuf = [S.sb("cbuf%d" % i, [128, 512], F32, es=p5) for i in range(2)]
                    asb = S.sb("asb", [128, 16, 6], F32, es=p5)
                    csb = S.sb("csb", [128, 16, 4], F32, es=p5)
                    if has_sample:
                        aconv_p = S.sb("aconv_p", [128, FT, 2], F32, es=p5)
                        aconv_s = S.sb("aconv_s", [128, FT, 32], F32, es=p5)
                        cache_s = S.sb("cache_s", [128, FT, 32], F32, es=p5)
                        cc32 = abuf_cc = None
                    def st_a(ti):
                        t = tiles[ti]
                        xt, xn = xt2[ti % 2], xn2[ti % 2]
                        np_ = t["np"]
                        S.dma("sp", xt[0:np_, :], x1scr[t["srow"]:t["srow"] + np_, :], reads=[x1tok], writes=[xt], dsem="d_xq%d" % (ti % 2))
                        rstd, nmr, mvb = ln_stats(xt, np_, xt)
                        ACT(xn[0:np_, :], xt[0:np_, :], AF.Identity, [xt, mvb], [xn], bias=nmr, scale=rstd)

                    def st_b(ti):
                        t = tiles[ti]
                        xn = xn2[ti % 2]
                        to_featT(xn, xn, t["np"], h2T, h2b, t["col"], 2, 3, t["k"] == "sample")
                    st_a(0)
                    for ti in range(len(tiles)):
                        if ti + 1 < len(tiles):
                            st_a(ti + 1)
                        st_b(ti)
                    if has_sample:
                        ccb = [S.sb("ccb%d" % i, [32, 1024], F32, es=p5) for i in range(2)]
                        for g0 in range(0, FT, 8):
                            n = min(8, FT - g0)
                            cb_ = ccb[(g0 // 8) % 2]
                            S.dma("sp", cb_[:, 0:128 * n], cc_d[:, 128 * g0:128 * (g0 + n)], writes=[cb_], dsem="d_ccb%d" % ((g0 // 8) % 2))
                            bk = nb()
                            for q in range(n):
                                TR(bk, bk[:, q * 32:q * 32 + 32], cb_[0:32, q * 128:(q + 1) * 128], identf[0:32, 0:32], [cb_, identf])
                            CP("dve", cache_s[:, g0:g0 + n, :], bk[:, 0:32 * n].rearrange("p (a b) -> p a b", b=32), [bk], [cache_s])
                    ngr = (FT + 1) // 2

                    def job_au(gi):
                        def fn(slot):
                            sa = slot
                            nft = min(2, FT - 2 * gi)
                            for q in range(nft):
                                ft = 2 * gi + q
                                cs = slice(q * 128, (q + 1) * 128)
                                cu = slice(256 + q * 128, 256 + (q + 1) * 128)
                                bA = nb()
                                for kt in range(KT):
                                    MM(bA, bA[:, 0:512], sa[:, kt, cs], h2T[:, kt, 0:512], kt == 0, kt == KT - 1, [sa, h2b])
                                bU = nb()
                                for kt in range(KT):
                                    MM(bU, bU[:, 0:512], slot[:, kt, cu], h2T[:, kt, 0:512], kt == 0, kt == KT - 1, [sa, h2b])
                                bB = nb()
                                if has_sample:
                                    for kt in range(KT):
                                        MM(bB, bB[:, 0:64], sa[:, kt, cs], h2T[:, kt, 512:576], kt == 0, kt == KT - 1, [sa, h2b])
                                    for kt in range(KT):
                                        MM(bB, bB[:, 64:128], slot[:, kt, cu], h2T[:, kt, 512:576], kt == 0, kt == KT - 1, [sa, h2b])
                                else:
                                    for kt in range(KT):
                                        MM(bB, bB[:, 0:2], sa[:, kt, cs], h2T[:, kt, 638:640], kt == 0, kt == KT - 1, [sa, h2b])
                                ab = abuf[ft % 2]
                                cbf = cbuf[ft % 2]
                                if has_sample:
                                    CP("dve", ab[:, 0:2], hist[:, ft, :], [hist], [ab])
                                else:
                                    TS("dve", ab[:, 0:2], bB[:, 0:2], flag[:, 0:1], None, ALU.mult, None, [bB, flag], [ab])
                                CP("act", ab[:, 2:514], bA[:, 0:512], [bA], [ab])
                                if has_sample:
                                    CP("dve", aconv_p[:, ft, :], ab[:, 512:514], [ab], [P(aconv_p)])
                                else:
                                    CP("dve", hist[:, ft, :], ab[:, 512:514], [ab], [P(hist)])
                                ACT(cbf[:, :], ab[:, 2:514], AF.Identity, [ab, convw], [cbf], bias=convw[:, ft, 3:4], scale=convw[:, ft, 2:3])
                                STT(cbf[:, :], ab[:, 1:513], convw[:, ft, 1:2], cbf[:, :], ALU.mult, ALU.add, [ab, convw, cbf], [cbf])
                                STT(cbf[:, :], ab[:, 0:512], convw[:, ft, 0:1], cbf[:, :], ALU.mult, ALU.add, [ab, convw, cbf], [cbf])
                                ACT(cbf[:, :], cbf[:, :], AF.Gelu, [cbf], [cbf])
                                TT("dve", gT[:, ft, 0:512], cbf[:, :], bU[:, 0:512], ALU.mult, [cbf, bU], [P(gT)])
                                if has_sample:
                                    CP("dve", asb[:, :, 0:2], cache_s[:, ft, :].rearrange("p (s j) -> p s j", j=2), [cache_s], [asb])
                                    CP("act", asb[:, :, 2:6], bB[:, 0:64].rearrange("p (s j) -> p s j", j=4), [bB], [asb])
                                    CP("dve", aconv_s[:, ft, :].rearrange("p (s j) -> p s j", j=2), asb[:, :, 4:6], [asb], [P(aconv_s)])
                                    ACT(csb[:, :, :], asb[:, :, 2:6], AF.Identity, [asb, convw], [csb], bias=convw[:, ft, 3:4], scale=convw[:, ft, 2:3])
                                    STT(csb[:, :, :], asb[:, :, 1:5], convw[:, ft, 1:2], csb[:, :, :], ALU.mult, ALU.add, [asb, convw, csb], [csb])
                                    STT(csb[:, :, :], asb[:, :, 0:4], convw[:, ft, 0:1], csb[:, :, :], ALU.mult, ALU.add, [asb, convw, csb], [csb])
                                    ACT(csb[:, :, :], csb[:, :, :], AF.Gelu, [csb], [csb])
                                    TT("dve", gT[:, ft, 512:576].rearrange("p (s j) -> p s j", j=4), csb[:, :, :], bB[:, 64:128].rearrange("p (s j) -> p s j", j=4), ALU.mult, [csb, bB], [P(gT)])
                        return fn
                    jobs = []
                    for gi in range(ngr):
                        nc_ = min(256, DFF - 256 * gi)
                        jobs.append(([(w_up_v, 0, KT, 256 * gi, nc_, 0), (w_up_v, 0, KT, DFF + 256 * gi, nc_, 256)], job_au(gi)))
                    stream(jobs, nxt=W1(w_down_v, 0, 16, 0, 512))
                    if has_sample:
                        cvo = [S.sb("cvo%d" % i, [32, 512], F32, es=p5) for i in range(2)]
                        cvq = [S.sb("cvq%d" % i, [2, 512], F32, es=p5) for i in range(2)]
                        for gi in range((FT + 3) // 4):
                            nft = min(4, FT - 4 * gi)
                            bk = nb()
                            for q in range(nft):
                                TR(bk, bk[0:32, q * 128:(q + 1) * 128], aconv_s[:, 4 * gi + q, :], identf[:, :], [aconv_s, identf])
                            o = cvo[gi % 2]
                            CP("dve", o[:, 0:128 * nft], bk[0:32, 0:128 * nft], [bk], [o])
                            S.dma("sp", cvs_o[:, 512 * gi:512 * gi + 128 * nft], o[:, 0:128 * nft], reads=[o], dsem="d_cvo%d" % (gi % 2), is_output=True)
                            bk = nb()
                            for q in range(nft):
                                TR(bk, bk[0:2, q * 128:(q + 1) * 128], aconv_p[:, 4 * gi + q, :], identf[:, :], [aconv_p, identf])
                            o2 = cvq[gi % 2]
                            CP("dve", o2[:, 0:128 * nft], bk[0:2, 0:128 * nft], [bk], [o2])
                            S.dma("sp", cvp_o[:, 512 * gi:512 * gi + 128 * nft], o2[:, 0:128 * nft], reads=[o2], dsem="d_cvq%d" % (gi % 2), is_output=True)
                    S.barrier()
                with contextlib.ExitStack() as p6:
                    otl = main + ([aux] if has_sample else [])
                    NTo = len(otl)
                    xa = S.sb("xb_", [128, NTo, D], F32, es=p6)
                    xab = [S.tok("xb%d" % i) for i in range(NTo)]
                    lng = S.sb("lng2", [128, D], F32, es=p6)
                    lnb = S.sb("lnb2", [128, D], F32, es=p6)
                    gP = [S.sb("gQ%d" % i, [128, 512], F32, es=p6) for i in range(2)]
                    gS = [S.sb("gR%d" % i, [64, 512], F32, es=p6) for i in range(2)]
                    tmp = [S.sb("tmpd%d" % i, [128, 512], F32, es=p6) for i in range(2)]
                    S.dma("sp", lng[:, :], rowbc(ln2g_d[0:1, :], D), writes=[lng], dsem="d_lng")
                    S.dma("sp", lnb[:, :], rowbc(ln2b_d[0:1, :], D), writes=[lnb], dsem="d_lnb")
                    for ti, t in enumerate(otl):
                        S.dma("sp", xa[0:t["np"], ti, :], x1scr[t["srow"]:t["srow"] + t["np"], :], reads=[x1tok], writes=[xab[ti]], dsem="d_xb%d" % ti)
                    accb = [banks[i] for i in range(NTo)]
                    bstate["excl"] = set(range(NTo))
                    kgroups = [(0, 16), (16, 16), (32, 11)]

                    def gcol(t):
                        return (t["col"], t["np"])

                    def job_d(c, gi):
                        k0, nk = kgroups[gi]

                        def fn(slot):
                            for ti, t in enumerate(otl):
                                np_ = t["np"]
                                for kk in range(nk):
                                    MM(accb[ti], accb[ti][0:np_, 0:512], gT[:, k0 + kk, t["col"]:t["col"] + np_], slot[:, kk, :], gi == 0 and kk == 0, gi == 2 and kk == nk - 1, [gT, slot])
                            if gi == 2:
                                g = gP[c % 2]
                                S.dma("sp", g[:, :], rowbc(gscr[1, 0:1, 512 * c:512 * c + 512], 512), reads=[gscr_tok], writes=[g], dsem="d_gQ%d" % (c % 2))
                                if has_sample:
                                    g2 = gS[c % 2]
                                    S.dma("sp", g2[:, :], gscr_s[1, :, 512 * c:512 * c + 512], reads=[gscr_tok], writes=[g2], dsem="d_gR%d" % (c % 2))
                                for ti, t in enumerate(otl):
                                    np_ = t["np"]
                                    gg = g2 if t["k"] == "sample" else g
                                    tp = tmp[ti % 2]
                                    TT("dve", tp[0:np_, :], accb[ti][0:np_, 0:512], gg[0:np_, :], ALU.mult, [accb[ti], gg], [tp])
                                    STT(xa[0:np_, ti, 512 * c:512 * c + 512], xa[0:np_, ti, 512 * c:512 * c + 512], ALPHA, tp[0:np_, :], ALU.mult, ALU.add, [xab[ti], tp], [xab[ti]])
                        return fn
                    jobs = []
                    for c in range(4):
                        for gi in range(3):
                            k0, nk = kgroups[gi]
                            jobs.append((w_down_v, k0, nk, 512 * c, 512, job_d(c, gi)))
                    stream(jobs, nxt=(W1(w_in_v, 0, KT, O_QA, 512) if kind == "M0" else None))
                    bstate["excl"] = set()
                    def after2(ti, t):
                        np_ = t["np"]
                        if t["k"] == "sample":
                            S.dma("sp", ys_o[:, :], xa[0:np_, ti, :], reads=[xab[ti]], dsem="d_yo%d" % ti, is_output=True)
                        else:
                            S.dma("sp", yp_o[t["orow"]:t["orow"] + 128, :], xa[0:np_, ti, :], reads=[xab[ti]], dsem="d_yo%d" % ti, is_output=True)
                    ln_all(xa, xab, otl, lng, lnb, after2)
                    S.barrier()

        for pi, kind in enumerate(("P0", "P1")):
            mixer_phase(kind, None, None, None, extra=ada_late[8 * pi:8 * pi + 8])
        S.barrier()
        del wslots[2:]
        wstate["i"] = 0
        pending["key"] = None
        pre_es.close()
        for kind in ("M0", "M1"):
            with contextlib.ExitStack() as bes:
                TBk = 640 if kind == "M0" else 576
                yT = S.sb("yT", [128, KT, TBk], BF16, es=bes)
                yTb = [S.tok("yT%d" % j) for j in range(5)]
                tiles, logical, TB, ranges = mixer_phase(kind, yT, yTb, bes)
                outproj_phase(kind, tiles, yT, yTb)
                S.barrier()
            ffn_phase(kind, tiles)

        with contextlib.ExitStack() as pf:
            for h in range(4):
                S.dma("sp", Cp_o[h], Cst[h][:, 0:256], reads=[Cst[h]], dsem="d_cpo", is_output=True)
            npt = S.sb("npt", [128, 4], F32, es=pf)
            for h in range(4):
                CP("dve", npt[:, h:h + 1], Cst[h][:, 256:257], [Cst[h]], [npt])
            bk = nb()
            TR(bk, bk[0:4, 0:128], npt[:, 0:4], identf[:, :], [npt, identf])
            npo = S.sb("npo", [4, 128], F32, es=pf)
            CP("dve", npo[:, :], bk[0:4, 0:128], [bk], [npo])
            S.dma("sp", np_o[:, :], npo[:, :], reads=[npo], dsem="d_npo", is_output=True)
            S.dma("sp", bass.AP(mp_o.tensor, mp_o.offset, [[1, 4], [1, 1]]), mcar[:, 0:1], reads=[mcar], dsem="d_mpo", is_output=True)
            for h in range(8):
                S.dma("sp", Sp_o[h], Sst[h][:, :], reads=[Sst[h]], dsem="d_spo", is_output=True)
            bk = nb()
            TR(bk, bk[0:64, 0:128], nTo[:, 0:64], identf[:, :], [nTo, identf])
            nso = S.sb("nso", [64, 128], F32, es=pf)
            CP("dve", nso[:, :], bk[0:64, 0:128], [bk], [nso])
            S.dma("sp", ns_o[:, :], nso[:, :], reads=[nso], dsem="d_nso", is_output=True)
            S.finish()
        ninst = S.ninst
        print('min sbuf remaining', getattr(S, 'minrem', None), getattr(S, 'minrem_at', None))
    return nc, ninst


_CACHE = {}


def _consts():
    s = np.arange(128)
    maskp = (s[:, None] <= s[None, :]).astype(np.float32)
    s6 = np.arange(64)
    masks = ((s6[:, None] // 4 == s6[None, :] // 4) & (s6[:, None] <= s6[None, :])).astype(np.float32)
    bmask = (s6[:, None] // 4 == np.arange(16)[None, :]).astype(np.float32)
    cm16 = (np.arange(16)[:, None] == s6[None, :] // 4).astype(np.float32).reshape(1, 1024)
    cm2 = (np.arange(2)[:, None] == s[None, :] // 64).astype(np.float32).reshape(1, 256)
    tA = np.arange(640)
    rmA = np.stack([(tA % 128 != 0).astype(np.float32), np.where(tA % 128 == 0, NEG, 0.0).astype(np.float32)])
    tB = np.arange(576)
    startB = np.where(tB < 512, tB % 128 == 0, tB % 4 == 0)
    rmB = np.stack([(~startB).astype(np.float32), np.where(startB, NEG, 0.0).astype(np.float32)])
    diag = np.zeros((4, 4, 24), np.float32)
    for k in range(4):
        diag[k, k, :] = 1.0
    return dict(c_ident=np.eye(128, dtype=np.float32), c_maskp=maskp, c_masks=masks, c_bmask=bmask, c_cm16=cm16, c_cm2=cm2,
                c_rmA=rmA, c_rmB=rmB, c_diag=diag.reshape(4, 96))


def make_in_maps(inp):
    f = lambda a: np.ascontiguousarray(np.asarray(a, dtype=np.float32))
    xpr, xsm = f(inp["x_prompt"]), f(inp["x_sample"])
    cst = _consts()
    shared = dict(
        lbl=f(inp["hgrn_lb_logits"]), w_ada=f(inp["w_ada"][0]), b_ada=f(inp["b_ada"]), w_in=f(inp["w_in"][0]),
        b_gate_a=f(inp["b_gate_a"][0]), norm_a=f(inp["norm_a"]), norm_b=f(inp["norm_b"]), w_out=f(inp["w_out"][0]),
        ln1_g=f(inp["ln1_g"]), ln1_b=f(inp["ln1_b"]), w_up=f(inp["w_up"][0]), conv_w=f(inp["conv_w"][0]), conv_b=f(inp["conv_b"]),
        w_down=f(inp["w_down"][0]), ln2_g=f(inp["ln2_g"]), ln2_b=f(inp["ln2_b"]), **cst)
    maps = []
    for c in range(8):
        b, half = c // 2, c % 2
        sl = slice(16 * c, 16 * c + 16)
        m = dict(shared)
        m["xpre"] = f(xpr[b, 0:1024])
        m["xp"] = f(xpr[b, 1024 * half:1024 * half + 1024])
        m["xs"] = f(xsm[sl].reshape(64, D))
        m["flag"] = np.full((128, 1), float(half), np.float32)
        m["c17"] = f(np.concatenate([inp["c_prompt"][b:b + 1], inp["c_sample"][sl]], axis=0))
        m["sC"] = f(inp["state_mlstm_C"][0, sl])
        m["sn"] = f(inp["state_mlstm_n"][0, sl].reshape(64, 128))
        m["sm"] = f(inp["state_mlstm_m"][0, sl])
        m["sS"] = f(inp["state_hgrn_S"][0, sl])
        m["cconv"] = f(inp["cache_ffn_conv"][0, sl].reshape(32, DFF))
        maps.append(m)
    return maps


def assemble(res):
    R = res.results
    yp = np.zeros((4, 2048, D), np.float32)
    ys = np.zeros((128, 4, D), np.float32)
    Cp = np.zeros((1, 4, 4, 128, 256), np.float32)
    n_p = np.zeros((1, 4, 4, 128), np.float32)
    mp = np.zeros((1, 4, 4), np.float32)
    Sp = np.zeros((1, 4, 8, 128, 128), np.float32)
    cvp = np.zeros((1, 4, 2, DFF), np.float32)
    Cs = np.zeros((1, 128, 4, 128, 256), np.float32)
    ns = np.zeros((1, 128, 4, 128), np.float32)
    ms = np.zeros((1, 128, 4), np.float32)
    Ss = np.zeros((1, 128, 8, 128, 128), np.float32)
    cvs = np.zeros((1, 128, 2, DFF), np.float32)
    for c in range(8):
        b, half = c // 2, c % 2
        r = R[c]
        sl = slice(16 * c, 16 * c + 16)
        yp[b, 1024 * half:1024 * half + 1024] = r["yp"]
        ys[sl] = r["ys"].reshape(16, 4, D)
        if half == 1:
            Cp[0, b] = r["Cp"]
            n_p[0, b] = r["np"]
            mp[0, b] = r["mp"].reshape(4)
            Sp[0, b] = r["Sp"]
            cvp[0, b] = r["convp"]
        Cs[0, sl] = r["Cs"]
        ns[0, sl] = r["ns"].reshape(16, 4, 128)
        ms[0, sl] = r["ms"]
        Ss[0, sl] = r["Ss"]
        cvs[0, sl] = r["convs"].reshape(16, 2, DFF)
    return (yp, ys, Cp, n_p, mp, Sp, cvp, Cs, ns, ms, Ss, cvs)


def kernel(**inputs):
    if "nc" not in _CACHE:
        _CACHE["nc"] = build_program()[0]
    nc = _CACHE["nc"]
    maps = make_in_maps(inputs)
    res = run_bass_kernel_spmd(nc, maps, core_ids=list(range(8)))
    return assemble(res)
```
